# Optimizing a Trainium2 kernel written in Bass

```python
import jax, jax.numpy as jnp
from jax import lax
import numpy as np

D_MODEL = 1024
BATCH = 8
SEQ = 2048
DEPTH = 4
DEC_BATCH = 128
DEC_SEQ = 8
PAST_LEN = 16384
PAGE_SIZE = 128

D_MIX = D_MODEL
D_POOL = D_MIX // 2
POOL_WINDOWS = (2, 4, 8, 16)
POOL_GROUPS = len(POOL_WINDOWS)
POOL_GC = D_POOL // POOL_GROUPS
POOL_BUF = max(POOL_WINDOWS) - 1
DN_DK = 128
DN_DV = 128
DN_HEADS = (D_MIX - D_POOL) // DN_DV
D_DN = DN_HEADS * DN_DV
D_QKV = DN_HEADS * (2 * DN_DK + DN_DV)
DN_CONV = 4
DN_CHUNK = 64
IN_AB = D_POOL + D_QKV + D_DN + 2 * DN_HEADS
SC_CONV = 3
D_FF = 2816
FFN_CONV = 3
N_AB = (DEPTH + 1) // 2
N_C = DEPTH // 2
EPS = 1e-6

kernel_name = 'hybrid_pool_gdn_shortconv_convffn_step'


def rmsnorm(x, w):
    xf = x.astype(jnp.float32)
    y = xf * lax.rsqrt(jnp.mean(xf * xf, axis=-1, keepdims=True) + EPS)
    return (y * w.astype(jnp.float32)).astype(x.dtype)


def l2norm(x):
    return x * lax.rsqrt(jnp.sum(x * x, axis=-1, keepdims=True) + EPS)


def causal_dwconv(u, buf, w):
    width = w.shape[0]
    t_len = u.shape[1]
    ext = jnp.concatenate([buf.astype(u.dtype), u], axis=1)
    y = ext[:, 0:t_len] * w[0]
    for j in range(1, width):
        y = y + ext[:, j:j + t_len] * w[j]
    return y, ext[:, ext.shape[1] - (width - 1):]


def pool_mixer(u, buf, pool_w, pool_scale, pos0):
    b, t_len, _ = u.shape
    ext = jnp.concatenate([buf.astype(u.dtype), u], axis=1)
    cs = jnp.cumsum(ext.astype(jnp.float32), axis=1)
    cs = jnp.concatenate([jnp.zeros_like(cs[:, :1]), cs], axis=1)
    end = cs[:, POOL_BUF + 1:]
    pos = jnp.arange(t_len) + pos0 + 1
    means = []
    for gi, win in enumerate(POOL_WINDOWS):
        sl = slice(gi * POOL_GC, (gi + 1) * POOL_GC)
        start = cs[:, POOL_BUF + 1 - win:POOL_BUF + 1 - win + t_len, sl]
        cnt = jnp.minimum(pos, win).astype(jnp.float32)[None, :, None]
        means.append((end[..., sl] - start) / cnt)
    y = jnp.concatenate(means, axis=-1) - u.astype(jnp.float32)
    y = y.astype(u.dtype).reshape(b, t_len, POOL_GROUPS, POOL_GC)
    y = jnp.einsum('btgc,gcd->btgd', y, pool_w).reshape(b, t_len, D_POOL)
    return y * pool_scale, ext[:, t_len:]


def gated_delta_chunked(q, k, v, g, beta, h0):
    b, t_len, nh, dk = q.shape
    dv = v.shape[-1]
    csz = min(DN_CHUNK, t_len)
    n = -(-t_len // csz)
    pad = n * csz - t_len

    def blocks(a):
        a = jnp.pad(a, [(0, 0), (0, pad)] + [(0, 0)] * (a.ndim - 2))
        a = a.reshape((b, n, csz) + a.shape[2:])
        return jnp.moveaxis(a, 3, 1)

    q, k, v, g, beta = (blocks(a) for a in (q, k, v, g, beta))
    gc = jnp.cumsum(g, axis=-1)
    idx = jnp.arange(csz)
    incl = idx[:, None] >= idx[None, :]
    strict = idx[:, None] > idx[None, :]
    diff = gc[..., :, None] - gc[..., None, :]
    decay = jnp.where(incl, jnp.exp(jnp.where(incl, diff, 0.0)), 0.0)
    kk = jnp.einsum('bhnik,bhnjk->bhnij', k, k)
    lmat = jnp.where(strict, beta[..., :, None] * kk * decay, 0.0)
    amat = lmat + jnp.eye(csz, dtype=lmat.dtype)
    gam = jnp.exp(gc)
    rhs = jnp.concatenate([(beta * gam)[..., None] * k, beta[..., None] * v], axis=-1)
    sol = lax.linalg.triangular_solve(amat, rhs, left_side=True, lower=True, unit_diagonal=True)
    w_blk, u_blk = sol[..., :dk], sol[..., dk:]
    aqk = jnp.einsum('bhnik,bhnjk->bhnij', q, k) * decay
    qg = q * gam[..., None]
    kd = k * jnp.exp(gc[..., -1:] - gc)[..., None]
    g_end = gam[..., -1]

    def step(hs, xs):
        w_c, u_c, a_c, q_c, k_c, ge = xs
        u_true = u_c - jnp.einsum('bhik,bhkv->bhiv', w_c, hs)
        o_c = jnp.einsum('bhik,bhkv->bhiv', q_c, hs) + jnp.einsum('bhij,bhjv->bhiv', a_c, u_true)
        hs = ge[..., None, None] * hs + jnp.einsum('bhik,bhiv->bhkv', k_c, u_true)
        return hs, o_c

    xs = tuple(jnp.moveaxis(a, 2, 0) for a in (w_blk, u_blk, aqk, qg, kd, g_end))
    h_end, o = lax.scan(step, h0, xs)
    o = jnp.transpose(o, (1, 0, 3, 2, 4)).reshape(b, n * csz, nh, dv)[:, :t_len]
    return o, h_end


def ab_mixer(h, pool_buf, conv_buf, dn_state, w_in, pool_w, pool_scale, conv_w, a_log, dt_bias, norm_w, w_out, pos0):
    b, t_len, _ = h.shape
    proj = h @ w_in
    o0 = D_POOL
    u_pool = proj[..., :o0]
    qkv = proj[..., o0:o0 + D_QKV]
    o0 += D_QKV
    z = proj[..., o0:o0 + D_DN]
    o0 += D_DN
    b_lin = proj[..., o0:o0 + DN_HEADS]
    a_lin = proj[..., o0 + DN_HEADS:]
    y_pool, new_pool = pool_mixer(u_pool, pool_buf, pool_w, pool_scale, pos0)
    qkv_c, new_conv = causal_dwconv(qkv, conv_buf, conv_w)
    qkv_c = jax.nn.silu(qkv_c.astype(jnp.float32))
    dq = DN_HEADS * DN_DK
    q = l2norm(qkv_c[..., :dq].reshape(b, t_len, DN_HEADS, DN_DK)) * (DN_DK ** -0.5)
    k = l2norm(qkv_c[..., dq:2 * dq].reshape(b, t_len, DN_HEADS, DN_DK))
    v = qkv_c[..., 2 * dq:].reshape(b, t_len, DN_HEADS, DN_DV)
    beta = jax.nn.sigmoid(b_lin.astype(jnp.float32))
    g = -jnp.exp(a_log.astype(jnp.float32)) * jax.nn.softplus(a_lin.astype(jnp.float32) + dt_bias.astype(jnp.float32))
    o, new_state = gated_delta_chunked(q, k, v, g, beta, dn_state.astype(jnp.float32))
    on = o * lax.rsqrt(jnp.mean(o * o, axis=-1, keepdims=True) + EPS) * norm_w.astype(jnp.float32)
    on = on * jax.nn.silu(z.astype(jnp.float32).reshape(b, t_len, DN_HEADS, DN_DV))
    y_dn = on.reshape(b, t_len, D_DN).astype(h.dtype)
    y = jnp.concatenate([y_pool.astype(h.dtype), y_dn], axis=-1) @ w_out
    return y, new_pool, new_conv, new_state.astype(dn_state.dtype)


def sconv_mixer(h, buf, w_in, conv_w, w_out):
    proj = h @ w_in
    b_gate, c_gate, hv = proj[..., :D_MODEL], proj[..., D_MODEL:2 * D_MODEL], proj[..., 2 * D_MODEL:]
    yc, new_buf = causal_dwconv(c_gate * hv, buf, conv_w)
    return (b_gate * yc) @ w_out, new_buf


def conv_ffn(h, buf, w_up, conv_w, w_down):
    up = h @ w_up
    gate, val = up[..., :D_FF], up[..., D_FF:]
    gc, new_buf = causal_dwconv(gate, buf, conv_w)
    return (jax.nn.silu(gc) * val) @ w_down, new_buf


def trunk(x, pool_buf, dnconv_buf, dn_state, sconv_buf, ffn_buf, pos0,
          norm_mix, norm_ffn, norm_final, w_in_ab, pool_w, pool_scale, dn_conv_w, dn_a_log,
          dn_dt_bias, dn_norm_w, w_out_ab, w_in_c, sc_conv_w, w_out_c, w_up, ffn_conv_w, w_down):
    new_pool, new_dnconv, new_dn, new_sconv, new_ffn = [], [], [], [], []
    for l in range(DEPTH):
        i = l // 2
        h = rmsnorm(x, norm_mix[l])
        if l % 2 == 0:
            y, p_b, c_b, s_b = ab_mixer(h, pool_buf[i], dnconv_buf[i], dn_state[i], w_in_ab[i], pool_w[i],
                                        pool_scale[i], dn_conv_w[i], dn_a_log[i], dn_dt_bias[i],
                                        dn_norm_w[i], w_out_ab[i], pos0)
            new_pool.append(p_b)
            new_dnconv.append(c_b)
            new_dn.append(s_b)
        else:
            y, sc_b = sconv_mixer(h, sconv_buf[i], w_in_c[i], sc_conv_w[i], w_out_c[i])
            new_sconv.append(sc_b)
        x = x + y
        h = rmsnorm(x, norm_ffn[l])
        y, f_b = conv_ffn(h, ffn_buf[l], w_up[l], ffn_conv_w[l], w_down[l])
        new_ffn.append(f_b)
        x = x + y
    x = rmsnorm(x, norm_final)
    return (x, jnp.stack(new_pool), jnp.stack(new_dnconv), jnp.stack(new_dn),
            jnp.stack(new_sconv), jnp.stack(new_ffn))


def setup_inputs(seed: int = 0) -> dict:
    key = jax.random.key(seed)
    ks = jax.random.split(key, 24)
    f32 = jnp.float32
    nrm = lambda k, shape, s: jax.random.normal(k, shape, f32) * s
    dt = jnp.exp(jax.random.uniform(ks[15], (N_AB, DN_HEADS), f32, np.log(1e-3), np.log(1e-1)))
    return {
        'x_prompt': nrm(ks[0], (BATCH, SEQ, D_MODEL), 1.0),
        'x_sample': nrm(ks[1], (DEC_BATCH, DEC_SEQ, D_MODEL), 1.0),
        'state_pool': nrm(ks[2], (N_AB, DEC_BATCH, POOL_BUF, D_POOL), 1.0),
        'state_dn_conv': nrm(ks[3], (N_AB, DEC_BATCH, DN_CONV - 1, D_QKV), 1.0),
        'state_dn': nrm(ks[4], (N_AB, DEC_BATCH, DN_HEADS, DN_DK, DN_DV), 0.3),
        'state_sconv': nrm(ks[5], (N_C, DEC_BATCH, SC_CONV - 1, D_MODEL), 1.0),
        'state_ffn_conv': nrm(ks[6], (DEPTH, DEC_BATCH, FFN_CONV - 1, D_FF), 1.0),
        'norm_mix': 1.0 + nrm(ks[7], (DEPTH, D_MODEL), 0.02),
        'norm_ffn': 1.0 + nrm(ks[8], (DEPTH, D_MODEL), 0.02),
        'norm_final': 1.0 + nrm(ks[9], (D_MODEL,), 0.02),
        'w_in_ab': nrm(ks[10], (N_AB, D_MODEL, IN_AB), D_MODEL ** -0.5),
        'pool_w': nrm(ks[11], (N_AB, POOL_GROUPS, POOL_GC, POOL_GC), POOL_GC ** -0.5),
        'pool_scale': 1.0 + nrm(ks[12], (N_AB, D_POOL), 0.02),
        'dn_conv_w': nrm(ks[13], (N_AB, DN_CONV, D_QKV), DN_CONV ** -0.5),
        'dn_a_log': jnp.log(jax.random.uniform(ks[14], (N_AB, DN_HEADS), f32, 1.0, 16.0)),
        'dn_dt_bias': dt + jnp.log(-jnp.expm1(-dt)),
        'dn_norm_w': 1.0 + nrm(ks[16], (N_AB, DN_DV), 0.02),
        'w_out_ab': nrm(ks[17], (N_AB, D_MIX, D_MODEL), D_MIX ** -0.5),
        'w_in_c': nrm(ks[18], (N_C, D_MODEL, 3 * D_MODEL), D_MODEL ** -0.5),
        'sc_conv_w': nrm(ks[19], (N_C, SC_CONV, D_MODEL), SC_CONV ** -0.5),
        'w_out_c': nrm(ks[20], (N_C, D_MODEL, D_MODEL), D_MODEL ** -0.5),
        'w_up': nrm(ks[21], (DEPTH, D_MODEL, 2 * D_FF), D_MODEL ** -0.5),
        'ffn_conv_w': nrm(ks[22], (DEPTH, FFN_CONV, D_FF), FFN_CONV ** -0.5),
        'w_down': nrm(ks[23], (DEPTH, D_FF, D_MODEL), D_FF ** -0.5),
    }


def reference(x_prompt, x_sample, state_pool, state_dn_conv, state_dn, state_sconv, state_ffn_conv,
              norm_mix, norm_ffn, norm_final, w_in_ab, pool_w, pool_scale, dn_conv_w, dn_a_log,
              dn_dt_bias, dn_norm_w, w_out_ab, w_in_c, sc_conv_w, w_out_c, w_up, ffn_conv_w, w_down):
    weights = (norm_mix, norm_ffn, norm_final, w_in_ab, pool_w, pool_scale, dn_conv_w, dn_a_log,
               dn_dt_bias, dn_norm_w, w_out_ab, w_in_c, sc_conv_w, w_out_c, w_up, ffn_conv_w, w_down)
    bp = x_prompt.shape[0]
    dtp = x_prompt.dtype
    zp_pool = jnp.zeros((N_AB, bp, POOL_BUF, D_POOL), dtp)
    zp_dnconv = jnp.zeros((N_AB, bp, DN_CONV - 1, D_QKV), dtp)
    zp_dn = jnp.zeros((N_AB, bp, DN_HEADS, DN_DK, DN_DV), state_dn.dtype)
    zp_sconv = jnp.zeros((N_C, bp, SC_CONV - 1, D_MODEL), dtp)
    zp_ffn = jnp.zeros((DEPTH, bp, FFN_CONV - 1, D_FF), dtp)
    y_prompt, new_pool_p, new_dnconv_p, new_dn_p, new_sconv_p, new_ffnconv_p = trunk(
        x_prompt, zp_pool, zp_dnconv, zp_dn, zp_sconv, zp_ffn, 0, *weights)
    y_sample, new_pool_s, new_dnconv_s, new_dn_s, new_sconv_s, new_ffnconv_s = trunk(
        x_sample, state_pool, state_dn_conv, state_dn, state_sconv, state_ffn_conv, PAST_LEN, *weights)
    return (y_prompt, y_sample, new_pool_p, new_pool_s, new_dnconv_p, new_dnconv_s, new_dn_p, new_dn_s,
            new_sconv_p, new_sconv_s, new_ffnconv_p, new_ffnconv_s)
```

```python
import contextlib
import numpy as np
import concourse.bass as bass
import concourse.mybir as mybir
from concourse.bass_utils import run_bass_kernel_spmd

F32 = mybir.dt.float32
BF16 = mybir.dt.bfloat16
AF = mybir.ActivationFunctionType
ALU = mybir.AluOpType

ENGS = ("pe", "act", "dve", "pool", "sp")
BLOCKATTR = {"pe": "tensor", "act": "scalar", "dve": "vector", "pool": "gpsimd", "sp": "sync"}


class Op:
    __slots__ = ("eng", "fn", "deps", "odeps", "cost", "gi", "fin", "signal", "semval", "idx", "is_dma", "dsem",
                 "dval", "qprev")

    def __init__(self, eng, fn, is_dma=False):
        self.eng = eng
        self.fn = fn
        self.deps = set()
        self.odeps = set()
        self.cost = 300.0
        self.gi = 0
        self.fin = 0.0
        self.signal = False
        self.semval = None
        self.idx = None
        self.is_dma = is_dma
        self.dsem = None
        self.dval = None
        self.qprev = None


def _isz(dt):
    return 2 if dt == BF16 else 4


def _box(ap):
    t = ap.tensor
    item = _isz(ap.dtype)
    pstep = 1
    for s in list(t.shape)[1:]:
        pstep *= int(s)
    off = int(ap.offset)
    dims = [(int(a), int(b)) for a, b in ap.ap]
    p0 = off // pstep
    f0 = off % pstep
    if dims and dims[0][0] == pstep:
        pn = dims[0][1]
        rest = dims[1:]
    elif dims and dims[0][0] == 0 and len(dims) > 1:
        pn = 1
        rest = dims[1:]
    else:
        pn = 1
        rest = dims
    ext = 0
    lo = 0
    for st, cnt in rest:
        if st >= 0:
            ext += st * (cnt - 1)
        else:
            lo += st * (cnt - 1)
    if type(t).__name__.startswith("PSum"):
        return (t.name, 0, 128, 0, 1 << 30)
    return (t.name, p0, p0 + pn, (f0 + lo) * item, (f0 + ext + 1) * item)


def _overlap(a, b):
    return a[1] < b[2] and b[1] < a[2] and a[3] < b[4] and b[3] < a[4]


def _covers(a, b):
    return a[1] <= b[1] and a[2] >= b[2] and a[3] <= b[3] and a[4] >= b[4]


def _is_dram(ap):
    return type(ap.tensor).__name__.startswith("DRam")


class Prog:
    NDSEM = 8

    def __init__(self, nc):
        self.nc = nc
        self.q = {e: [] for e in ENGS}
        self.acc = {}
        self.dma_ops = {e: [] for e in ENGS}
        self.nwaits = 0
        self.ngi = 0

    def rec(self, eng, fn, reads=(), writes=(), is_dma=False, cost=None):
        op = Op(eng, fn, is_dma=is_dma)
        op.idx = len(self.q[eng])
        op.gi = self.ngi
        self.ngi += 1
        if cost is None:
            n = 0
            for ap in list(writes)[:1]:
                n = 1
                for d_ in list(ap.shape)[1:]:
                    n *= int(d_)
            if is_dma:
                nb = n * 128 * 4
                cost = 2000.0 + nb / 80.0
            elif eng == "act":
                cost = 224.0 + 0.85 * n
            elif eng == "dve":
                cost = 200.0 + 0.95 * n
            elif eng == "pool":
                cost = 150.0 + 2.0 * n
            else:
                cost = 100.0
        op.cost = cost
        for ap in reads:
            if ap is None or _is_dram(ap):
                continue
            bx = _box(ap)
            lst = self.acc.setdefault(bx[0], [])
            for (b2, o2, w2) in lst:
                if w2 and o2 is not op and _overlap(bx, b2):
                    op.deps.add(o2)
            if not is_dma:
                for e in lst:
                    if (not e[2]) and e[0] == bx and e[1].eng == eng and not e[1].is_dma and e[1] is not op:
                        op.odeps.add(e[1])
                lst[:] = [e for e in lst if not ((not e[2]) and e[0] == bx and e[1].eng == eng and not e[1].is_dma)]
            lst.append((bx, op, False))
        psum_reads = [ap for ap in reads if ap is not None and type(ap.tensor).__name__.startswith("PSum")]
        for ap in list(writes) + psum_reads:
            if ap is None or _is_dram(ap):
                continue
            bx = _box(ap)
            lst = self.acc.setdefault(bx[0], [])
            keep = []
            for ent in lst:
                b2, o2, w2 = ent
                if o2 is op:
                    keep.append(ent)
                    continue
                if _overlap(bx, b2):
                    same_compute = (o2.eng == eng == "pe") and (not o2.is_dma) and (not is_dma)
                    if not same_compute:
                        op.deps.add(o2)
                    else:
                        op.odeps.add(o2)
                    if _covers(bx, b2):
                        continue
                keep.append(ent)
            keep.append((bx, op, True))
            self.acc[bx[0]] = keep
        self.q[eng].append(op)
        return op

    def schedule(self):
        import heapq
        allops = []
        for e in ENGS:
            allops.extend(self.q[e])
        allops.sort(key=lambda o: o.gi)
        succ = {}
        indeg = {}
        for o in allops:
            ds = o.deps | o.odeps
            indeg[o] = len(ds)
            for d in ds:
                succ.setdefault(d, []).append(o)
        import os as _os2
        HOP = float(_os2.environ.get("KHOP", "700"))
        CSC = float(_os2.environ.get("KCSC", "1.3"))
        MODE = _os2.environ.get("KMODE", "cp")
        tail = {}
        for o in reversed(allops):
            m_ = 0.0
            for c in succ.get(o, ()):
                lat = HOP if (c.eng != o.eng or o.is_dma) else 60.0
                v_ = lat + tail[c]
                if v_ > m_:
                    m_ = v_
            tail[o] = o.cost + m_
        est = {}
        free = {e: 0.0 for e in ENGS}
        newq = {e: [] for e in ENGS}
        pend = {e: [] for e in ENGS}
        avail = {e: [] for e in ENGS}
        for o in allops:
            if indeg[o] == 0:
                est[o] = 0.0
                heapq.heappush(pend[o.eng], (0.0, o.gi, o))
        INF = float("inf")
        nleft = len(allops)
        while nleft:
            best_e, best_t = None, INF
            for e in ENGS:
                pe_, av_ = pend[e], avail[e]
                while pe_ and pe_[0][0] <= free[e]:
                    t_, g_, o_ = heapq.heappop(pe_)
                    heapq.heappush(av_, ((-tail[o_]) if MODE == "cp" else t_, g_, o_))
                if av_:
                    t = free[e]
                elif pe_:
                    t = pe_[0][0]
                else:
                    continue
                if t < best_t:
                    best_e, best_t = e, t
            e = best_e
            if not avail[e]:
                t_, g_, o_ = heapq.heappop(pend[e])
                heapq.heappush(avail[e], ((-tail[o_]) if MODE == "cp" else t_, g_, o_))
            _, _, o = heapq.heappop(avail[e])
            start = max(est[o], free[e])
            if o.is_dma:
                free[e] = start + 60.0
                o.fin = start + o.cost
            else:
                o.fin = start + o.cost * (CSC if o.eng != "pe" else 1.0)
                free[e] = o.fin
            newq[e].append(o)
            nleft -= 1
            for c in succ.get(o, ()):
                indeg[c] -= 1
                lat = HOP if (c.eng != o.eng or o.is_dma) else 60.0
                if o in c.odeps and o not in c.deps:
                    tt_ = max(est.get(c, 0.0), start)
                else:
                    tt_ = max(est.get(c, 0.0), o.fin + lat)
                est[c] = tt_
                if indeg[c] == 0:
                    heapq.heappush(pend[c.eng], (tt_, c.gi, c))
        assert sum(len(v) for v in newq.values()) == len(allops)
        import os as _os
        keep = _os.environ.get("KEEP_ORDER", "").split(",")
        for e in ENGS:
            if e in keep or "all" in keep:
                newq[e] = sorted(newq[e], key=lambda o: o.gi)
        self.q = newq
        for e in ENGS:
            self.dma_ops[e] = []
            for i, o in enumerate(self.q[e]):
                o.idx = i
            for o in self.q[e]:
                if o.is_dma:
                    i = len(self.dma_ops[e])
                    o.dsem = i % self.NDSEM
                    o.dval = 16 * (i // self.NDSEM + 1)
                    o.qprev = self.dma_ops[e][i - self.NDSEM] if i >= self.NDSEM else None
                    self.dma_ops[e].append(o)

    def mm(self, out, lhsT, rhs, start=True, stop=True, skip=False):
        n = 1
        for d_ in list(rhs.shape)[1:]:
            n *= int(d_)
        cost = 64.0 + 0.52 * n
        if rhs.dtype == F32:
            cost *= 4
        if skip:
            return self.rec("pe", lambda e: e.matmul(out, lhsT, rhs, start=start, stop=stop, skip_group_check=True),
                            reads=[lhsT, rhs], writes=[out], cost=cost)
        return self.rec("pe", lambda e: e.matmul(out, lhsT, rhs, start=start, stop=stop),
                        reads=[lhsT, rhs], writes=[out], cost=cost)

    def transpose(self, out, in_, ident):
        return self.rec("pe", lambda e: e.transpose(out, in_, ident), reads=[in_, ident], writes=[out], cost=110.0)

    def act(self, out, in_, func, bias=None, scale=None, accum_out=None):
        kw = {}
        rd = [in_]
        wr = [out]
        if bias is not None:
            kw["bias"] = bias
            if not isinstance(bias, (int, float)):
                rd.append(bias)
        if scale is not None:
            kw["scale"] = scale
            if not isinstance(scale, (int, float)):
                rd.append(scale)
        if accum_out is not None:
            kw["accum_out"] = accum_out
            wr.append(accum_out)
        return self.rec("act", lambda e: e.activation(out, in_, func, **kw), reads=rd, writes=wr)

    def tt(self, eng, out, in0, in1, op):
        return self.rec(eng, lambda e: e.tensor_tensor(out, in0, in1, op), reads=[in0, in1], writes=[out])

    def ts(self, eng, out, in0, s1, op0, s2=None, op1=None):
        rd = [in0]
        for s in (s1, s2):
            if s is not None and not isinstance(s, (int, float)):
                rd.append(s)
        if op1 is None:
            return self.rec(eng, lambda e: e.tensor_scalar(out, in0, s1, None, op0), reads=rd, writes=[out])
        return self.rec(eng, lambda e: e.tensor_scalar(out, in0, s1, s2, op0, op1), reads=rd, writes=[out])

    def stt(self, out, in0, scalar, in1, op0, op1):
        rd = [in0, in1]
        if not isinstance(scalar, (int, float)):
            rd.append(scalar)
        return self.rec("dve", lambda e: e.scalar_tensor_tensor(out, in0, scalar, in1, op0, op1),
                        reads=rd, writes=[out])

    def copy(self, eng, out, in_):
        if eng == "act":
            return self.rec("act", lambda e: e.copy(out, in_), reads=[in_], writes=[out])
        return self.rec(eng, lambda e: e.tensor_copy(out, in_), reads=[in_], writes=[out])

    def memset(self, eng, ap, val):
        return self.rec(eng, lambda e: e.memset(ap, val), reads=[], writes=[ap])

    def dma(self, eng, out, in_, **kw):
        if out.dtype != in_.dtype:
            kw.setdefault("max_dma_last_dim", 4096)
        return self.rec(eng, lambda e: e.dma_start(out=out, in_=in_, **kw), reads=[in_], writes=[out], is_dma=True)

    def emit(self):
        nc = self.nc
        self.schedule()
        for e in ENGS:
            for op in self.q[e]:
                for d in op.deps:
                    if not d.is_dma:
                        d.signal = True
        for e in ENGS:
            c = 0
            for op in self.q[e]:
                if op.signal:
                    c += 1
                    op.semval = c
        with contextlib.ExitStack() as st:
            csem = {e: st.enter_context(nc.semaphore("s_" + e)) for e in ENGS}
            dsem = {e: [st.enter_context(nc.semaphore("d_%s%d" % (e, i))) for i in range(self.NDSEM)]
                    for e in ENGS if self.dma_ops[e]}
            block = st.enter_context(nc.Block())

            def make(ename):
                def body(eh):
                    waited = {}
                    for op in self.q[ename]:
                        need = {}
                        for d in op.deps:
                            if d.is_dma:
                                key = ("d", d.eng, d.dsem)
                                val = d.dval
                            else:
                                key = ("c", d.eng)
                                val = d.semval
                            if val > need.get(key, 0):
                                need[key] = val
                        if op.is_dma and op.qprev is not None:
                            key = ("d", op.eng, op.dsem)
                            val = op.qprev.dval
                            if val > need.get(key, 0):
                                need[key] = val
                        for key, val in need.items():
                            if waited.get(key, 0) >= val:
                                continue
                            waited[key] = val
                            sem = csem[key[1]] if key[0] == "c" else dsem[key[1]][key[2]]
                            eh.wait_ge(sem, val)
                            self.nwaits += 1
                        inst = op.fn(eh)
                        if op.is_dma:
                            inst.then_inc(dsem[op.eng][op.dsem], 16)
                        elif op.signal:
                            inst.then_inc(csem[op.eng], 1)
                    if ename == "sp":
                        for e2 in ENGS:
                            lst = self.dma_ops[e2]
                            if not lst:
                                continue
                            last = {}
                            for o in lst:
                                last[o.dsem] = o.dval
                            for s, v in last.items():
                                if waited.get(("d", e2, s), 0) < v:
                                    eh.wait_ge(dsem[e2][s], v)
                return body

            for ename in ENGS:
                getattr(block, BLOCKATTR[ename])(make(ename))


D = 1024
DFF = 2816
NFC = DFF // 128
IN_AB = 2568
DPOOL = 512
DQKV = 1536
NH = 4
NSS = 16
LS = 8
EPS = 1e-6
POOL_WINDOWS = (2, 4, 8, 16)
NCORES = 8
BIG = 30000.0


class Tile_:
    def __init__(self, b0, nblk, nseq, L, sample, first, last):
        self.b0, self.nblk, self.nseq, self.L = b0, nblk, nseq, L
        self.T = nblk * 128
        self.sample, self.first, self.last = sample, first, last


class Arena:
    def __init__(self, ar, nbytes):
        self.ar = ar
        self.nbytes = nbytes
        self.top = 0
        self.marks = []

    def alloc(self, shape, dt, at=None):
        n = 1
        for s in shape:
            n *= s
        nb = n * _isz(dt)
        nb_al = (nb + 63) // 64 * 64
        if at is None:
            off = self.top
            self.top += nb_al
        else:
            off = at
            assert off + nb_al <= self.top
        assert self.top <= self.nbytes, ("arena overflow", self.top, self.nbytes)
        v = self.ar[:, off // 2: off // 2 + nb // 2]
        if dt == F32:
            v = v.bitcast(F32)
        if len(shape) > 1:
            names = " ".join("d%d" % i for i in range(len(shape)))
            kw = {"d%d" % i: shape[i] for i in range(len(shape) - 1)}
            v = v.rearrange("p (%s) -> p %s" % (names, names), **kw)
        return v

    def mark(self):
        self.marks.append(self.top)

    def release(self):
        self.top = self.marks.pop()


def build_program(SEQ, layers):
    DEPTH = len(layers)
    N_AB = sum(1 for t in layers if t == "ab")
    N_C = sum(1 for t in layers if t == "c")
    NPB = SEQ // 128
    NB = NPB + 1
    NTOK = NB * 128
    nc = bass.Bass("TRN2", target_bir_lowering=False)
    P = Prog(nc)

    def din(name, shape):
        return nc.dram_tensor(name, list(shape), F32, kind="ExternalInput").ap()

    def dout(name, shape):
        return nc.dram_tensor(name, list(shape), F32, kind="ExternalOutput").ap()

    x_p = din("x_p", [SEQ, D])
    x_s = din("x_s", [128, D])
    norm_mix = din("norm_mix", [DEPTH, D])
    norm_ffn = din("norm_ffn", [DEPTH, D])
    norm_final = din("norm_final", [1, D])
    w_up = din("w_up", [DEPTH, 128, 8, 2 * DFF])
    w_down = din("w_down", [DEPTH, 128, NFC, D])
    ffn_cw = din("ffn_cw", [DEPTH, 128, NFC, 3])
    st_ffn = din("st_ffn", [DEPTH, 128, NFC, NSS, 2])
    ident_in = din("ident", [128, 128])
    y_p = dout("y_p", [SEQ, D])
    y_s = dout("y_s", [128, D])
    o_ffn_p = dout("o_ffn_p", [DEPTH, 128, NFC, 2])
    o_ffn_s = dout("o_ffn_s", [DEPTH, 128, NFC, NSS, 2])
    if N_C:
        w_in_c = din("w_in_c", [N_C, 128, 8, 3 * D])
        w_out_c = din("w_out_c", [N_C, 128, 8, D])
        sc_cw = din("sc_cw", [N_C, 128, 8, 3])
        st_sc = din("st_sc", [N_C, 128, 8, NSS, 2])
        o_sc_p = dout("o_sc_p", [N_C, 128, 8, 2])
        o_sc_s = dout("o_sc_s", [N_C, 128, 8, NSS, 2])

    if N_AB:
        w_in_ab = din("w_in_ab", [N_AB, 128, 8, IN_AB])
        w_out_ab = din("w_out_ab", [N_AB, 128, 8, D])
        pool_w_in = din("pool_w", [N_AB, 128, 4, 128])
        pool_scale_in = din("pool_scale", [N_AB, 128, 4])
        dn_cw_in = din("dn_cw", [N_AB, 128, 12, 4])
        dn_nw_in = din("dn_nw", [N_AB, 128, 1])
        a_log_in = din("a_log", [N_AB, 4])
        dt_bias_in = din("dt_bias", [N_AB, 4])
        st_pool = din("st_pool", [N_AB, 128, 4, NSS, 15])
        st_dnc = din("st_dnc", [N_AB, 128, 12, NSS, 3])
        st_dn = din("st_dn", [N_AB, NSS, 128, 4, 128])
        masks_in = din("masks", [2, 4, 128, 128])
        onehot_in = din("onehot", [128, NSS])
        invcnt_in = din("invcnt", [128, 4, 16])
        lvlmask_in = din("lvlmask", [5, 128, 128])
        o_pool_p = dout("o_pool_p", [N_AB, 128, 4, 15])
        o_pool_s = dout("o_pool_s", [N_AB, 128, 4, NSS, 15])
        o_dnc_p = dout("o_dnc_p", [N_AB, 128, 12, 3])
        o_dnc_s = dout("o_dnc_s", [N_AB, 128, 12, NSS, 3])
        o_dn_p = dout("o_dn_p", [N_AB, 128, 4, 128])
        o_dn_s = dout("o_dn_s", [N_AB, NSS, 128, 4, 128])

    with contextlib.ExitStack() as st:
        def sb(name, shape, dt):
            return st.enter_context(nc.sbuf_tensor(name, list(shape), dt))

        xs = sb("xs", [128, NB, D], F32)
        ident_f = sb("ident_f", [128, 128], F32)
        ident_b = sb("ident_b", [128, 128], BF16)
        eps_col = sb("eps_col", [128, 1], F32)
        hn = [sb("hn%d" % i, [128, D], BF16) for i in range(2)]
        ss = sb("ss", [128, NB], F32)
        rstd = sb("rstd", [128, NB], F32)
        ARB = (int(nc.sbuf_bytes_remaining) - 256) // 64 * 64
        ar_t = sb("arena", [128, ARB // 2], BF16)
        AR = Arena(ar_t, ARB)
        ps = [st.enter_context(nc.psum_tensor("ps%d" % i, [128, 512], F32)) for i in range(8)]

        P.dma("sp", ident_f[:], ident_in)
        P.copy("dve", ident_b[:], ident_f[:])
        P.memset("dve", eps_col[:], EPS)
        for b in range(NPB):
            P.dma("sp" if b % 2 == 0 else "act", xs[:, b, :], x_p[b * 128:(b + 1) * 128, :])
        P.dma("sp", xs[:, NPB, :], x_s)

        prompt_tiles = lambda tb: [Tile_(i * tb, tb, 1, tb * 128, False, i == 0, i == NPB // tb - 1)
                                   for i in range(NPB // tb)]
        sample_tile = Tile_(NPB, 1, NSS, LS, True, True, True)

        def load_norm_w(src_row):
            wnb_ = AR.alloc([D], F32)
            P.dma("sp", wnb_, src_row.broadcast_to([128, D]))
            return wnb_

        hn_ctr = [0]
        tp_ctr = [0]

        def norm_blocks(blocks, wn, hT, col0, tp_banks):
            for i, b in enumerate(blocks):
                h = hn[hn_ctr[0] % 2]
                hn_ctr[0] += 1
                P.act(h[:], xs[:, b, :], AF.Square, accum_out=ss[:, b:b + 1])
                P.act(rstd[:, b:b + 1], ss[:, b:b + 1], AF.Ln, bias=eps_col[:], scale=1.0 / D)
                P.act(rstd[:, b:b + 1], rstd[:, b:b + 1], AF.Exp, scale=-0.5)
                P.stt(h[:], xs[:, b, :], rstd[:, b:b + 1], wn[:], ALU.mult, ALU.mult)
                bank = tp_banks[tp_ctr[0] % len(tp_banks)]
                tp_ctr[0] += 1
                pv = bank[:].bitcast(BF16).rearrange("p (k t) -> p k t", k=8)
                for k in range(8):
                    P.transpose(pv[:, k, :], h[:, k * 128:(k + 1) * 128], ident_b[:])
                c0 = col0 + i * 128
                P.copy("act" if i % 2 == 0 else "dve", hT[:, :, c0:c0 + 128], pv)

        def conv_tap0(ext, cw, L, acc):
            P.act(acc, ext[:, :, 0:L], AF.Identity, scale=cw[:, 0:1])

        def conv_rest(out, ext, cw, ntaps, L, acc):
            for j in range(1, ntaps):
                dst = out if j == ntaps - 1 else acc
                P.stt(dst, ext[:, :, j:j + L], cw[:, j:j + 1], acc, ALU.mult, ALU.add)

        def conv_taps(out, ext, cw, ntaps, L, acc):
            conv_tap0(ext, cw, L, acc)
            conv_rest(out, ext, cw, ntaps, L, acc)

        def skewed(n, A_, B_, skew=1):
            for i in range(min(skew, n)):
                A_(i)
            for i in range(n):
                if i + skew < n:
                    A_(i + skew)
                B_(i)

        def out_proj_accum(tile, lhs_fn, nk, rhs_fn, banks):
            i = 0
            for blk in range(tile.nblk):
                for half in range(2):
                    bank = banks[i % len(banks)]
                    i += 1
                    for k in range(nk):
                        P.mm(bank[:, :], lhs_fn(k, blk), rhs_fn(k, half), start=(k == 0), stop=(k == nk - 1))
                    xv = xs[:, tile.b0 + blk, half * 512:(half + 1) * 512]
                    P.tt("dve", xv, xv, bank[:, :], ALU.add)

        def mixer_c(l, ic):
            AR.mark()
            w_in = AR.alloc([8, 3 * D], BF16)
            w_out = AR.alloc([8, D], BF16)
            hT = AR.alloc([8, 512], BF16)
            m = AR.alloc([8, 512], BF16)
            cw = AR.alloc([8, 3], F32)
            ext_p = AR.alloc([8, 1, 2 + 512], F32)
            ext_s = AR.alloc([8, NSS, 2 + LS], F32)
            stage = AR.alloc([8, NSS, 2], F32)
            tmpc = [AR.alloc([512], F32) for _ in range(2)]
            acc = [AR.alloc([512], F32) for _ in range(2)]
            cv = [AR.alloc([512], F32) for _ in range(2)]
            for j in range(3):
                for k0 in range(0, 8, 4):
                    P.dma("pool", w_in[:, k0:k0 + 4, j * D:(j + 1) * D], w_in_c[ic, :, k0:k0 + 4, j * D:(j + 1) * D])
            for k0 in range(0, 8, 4):
                P.dma("pool", w_out[:, k0:k0 + 4, :], w_out_c[ic, :, k0:k0 + 4, :])
            P.dma("sp", cw, sc_cw[ic])
            P.dma("sp", stage, st_sc[ic])
            P.memset("pool", ext_p[:, :, 0, 0:2], 0.0)
            for c in range(8):
                P.copy("pool", ext_s[:, c, :, 0:2], stage[:, c])
            wn = load_norm_w(norm_mix[l:l + 1, :])
            tasks = [(tile, c) for tile in prompt_tiles(4) + [sample_tile] for c in range(8)]

            def CA(i):
                tile, c = tasks[i]
                T, L, nseq = tile.T, tile.L, tile.nseq
                ext = ext_s if tile.sample else ext_p
                if c == 0:
                    norm_blocks(range(tile.b0, tile.b0 + tile.nblk), wn, hT, 0, [ps[6], ps[7]])
                pb, pc, ph = ps[(i % 2) * 3], ps[(i % 2) * 3 + 1], ps[(i % 2) * 3 + 2]
                for j, bank in enumerate((pb, pc, ph)):
                    for k in range(8):
                        P.mm(bank[:, :T], w_in[:, k, j * D + c * 128: j * D + (c + 1) * 128], hT[:, k, :T],
                             start=(k == 0), stop=(k == 7))
                v3 = lambda a: a.rearrange("p (s l) -> p s l", s=nseq)
                t = tmpc[i % 2]
                P.copy("act", t[:, :T], pc[:, :T])
                P.tt("dve", ext[:, c, :, 2:2 + L], v3(t[:, :T]), v3(ph[:, :T]), ALU.mult)
                conv_tap0(ext[:, c], cw[:, c, :], L, v3(acc[i % 2][:, :T]))

            def CB(i):
                tile, c = tasks[i]
                T, L, nseq = tile.T, tile.L, tile.nseq
                ext = ext_s if tile.sample else ext_p
                pb = ps[(i % 2) * 3]
                v3 = lambda a: a.rearrange("p (s l) -> p s l", s=nseq)
                conv_rest(v3(cv[i % 2][:, :T]), ext[:, c], cw[:, c, :], 3, L, v3(acc[i % 2][:, :T]))
                P.copy("pool", ext[:, c, :, 0:2], ext[:, c, :, L:L + 2])
                P.tt("dve", m[:, c, :T], cv[i % 2][:, :T], pb[:, :T], ALU.mult)
                if c == 7:
                    out_proj_accum(tile, lambda k, blk: m[:, k, blk * 128:(blk + 1) * 128],
                                   8, lambda k, half: w_out[:, k, half * 512:(half + 1) * 512], [ps[6], ps[7]])
                    if tile.last:
                        if tile.sample:
                            for c2 in range(8):
                                P.copy("pool", stage[:, c2], ext_s[:, c2, :, 0:2])
                            P.dma("sp", o_sc_s[ic], stage)
                        else:
                            stp = AR.alloc([8, 2], F32)
                            P.copy("pool", stp, ext_p[:, :, 0, 0:2])
                            P.dma("sp", o_sc_p[ic], stp)
            skewed(len(tasks), CA, CB)
            AR.release()

        def mixer_ab(l, ia):
            AR.mark()
            A = AR.alloc
            w_in = A([8, IN_AB], BF16)
            wo_ring = [A([D], BF16) for _ in range(4)]
            poolw = A([4, 128], BF16)
            hT = A([8, 128], BF16)
            wnb = A([D], F32)
            pscale = A([4], F32)
            dcw = A([12, 4], F32)
            nrmw = A([1], F32)
            alog = A([4], F32)
            dtb = A([4], F32)
            negA = A([4], F32)
            lnq = A([1], F32)
            WPS = NSS * (15 + LS)
            off_pooltmp = AR.top
            ebp = [A([WPS], F32) for _ in range(2)]
            T1 = A([WPS], F32)
            T2 = A([WPS], F32)
            pfx_pool = A([4, NSS, 15], F32)
            yp = [A([128], BF16) for _ in range(2)]
            t16 = A([16], F32)
            WQS = NSS * (3 + LS)
            ebq = [A([WQS], F32) for _ in range(2)]
            pfx_q = A([12, NSS, 3], F32)
            cvb = [A([128], F32) for _ in range(2)]
            accb = [A([128], F32) for _ in range(2)]
            qk32 = A([8, 128], F32)
            sq = A([8, 128], F32, at=off_pooltmp)
            rs = A([8, 128], F32)
            HS = [(A([8, 128], BF16), A([4, 128], BF16), A([4, 128], BF16), A([4, 128], BF16), A([4], F32), A([4], F32))
                  for _ in range(2)]
            ydn = A([4, 128], BF16)
            tsm = A([8], F32)
            gcc = A([8], F32)
            gam = A([4], F32)
            dlt = A([4], F32)
            bgm = A([4], F32)
            X1 = A([4, 128], F32)
            X2 = A([4, 128], F32)
            Eb = A([4, 128], F32)
            grow = A([4, 128], F32)
            P32 = A([4, 128], F32)
            Nb = A([4, 128], BF16)
            NTb = A([4, 128], BF16)
            Pb = A([4, 128], BF16)
            A2b = [A([4, 128], BF16) for _ in range(2)]
            A2Tb = [A([4, 128], BF16) for _ in range(2)]
            NTbd = A([4, 128], BF16)
            lvl_masks = [A([128], BF16) for _ in range(5)]
            aqkT = A([4, 128], BF16)
            qgT = A([4, 128], BF16)
            bgk = A([4, 128], BF16)
            kd = A([4, 128], BF16)
            bv = A([4, 128], BF16)
            negwT = A([4, 128], BF16)
            utT = A([4, 128], BF16)
            utk = A([4, 128], BF16)
            kdm = [A([4, 128], BF16) for _ in range(2)]
            S32p = A([4, 128], F32)
            Sbp = A([4, 128], BF16)
            S32s = [A([4, 128], F32) for _ in range(2)]
            Sbs = [A([4, 128], BF16) for _ in range(2)]
            msk = [[A([128], F32) for _ in range(4)] for _ in range(2)]
            onehot = A([NSS], F32)
            invcnt = A([4, 16], F32)
            for s_ in range(2):
                for m_ in range(4):
                    P.dma("sp", msk[s_][m_], masks_in[s_, m_])
            P.dma("sp", onehot, onehot_in)
            for m_ in range(5):
                P.dma("pool", lvl_masks[m_], lvlmask_in[m_])
            P.dma("sp", invcnt, invcnt_in)
            ONES = msk[0][1]
            for (c0, c1) in ((0, 1024), (1024, 2048), (2048, IN_AB)):
                for k0 in (0, 4):
                    P.dma("pool", w_in[:, k0:k0 + 4, c0:c1], w_in_ab[ia, :, k0:k0 + 4, c0:c1])
            P.dma("pool", poolw, pool_w_in[ia])
            P.dma("sp", wnb, norm_mix[l:l + 1, :].broadcast_to([128, D]))
            P.dma("sp", pscale, pool_scale_in[ia])
            P.dma("sp", dcw, dn_cw_in[ia])
            P.dma("sp", nrmw, dn_nw_in[ia])
            P.dma("sp", alog, a_log_in[ia:ia + 1, :].broadcast_to([128, 4]))
            P.dma("sp", dtb, dt_bias_in[ia:ia + 1, :].broadcast_to([128, 4]))
            P.act(negA, alog, AF.Exp)
            P.ts("dve", negA, negA, -1.0, ALU.mult)
            P.memset("dve", lnq, float(np.log(128.0 ** -0.5)))
            P.memset("pool", pfx_pool[:, :, 0, :], 0.0)
            P.memset("pool", pfx_q[:, :, 0, :], 0.0)
            P.memset("pool", S32p, 0.0)
            P.memset("pool", Sbp, 0.0)
            tiles = prompt_tiles(1) + [sample_tile]
            bc4 = lambda ap_: ap_.unsqueeze(1).broadcast_to([128, 4, 128])
            col4 = lambda ap_: ap_.unsqueeze(2).broadcast_to([128, 4, 128])
            r4 = lambda bank: bank[:, :].rearrange("p (a b) -> p a b", a=4)
            b4 = lambda bank, half: bank[:].bitcast(BF16)[:, half * 512:(half + 1) * 512].rearrange(
                "p (a b) -> p a b", a=4)
            T = 128

            def front_steps(tile, H):
                steps = []
                step = steps.append
                L, nseq = tile.L, tile.nseq
                v3 = lambda ap_: ap_.rearrange("p (s l) -> p s l", s=nseq)
                qkT, vT, sz, ypool, beta, gg = H

                def inproj(bank, col0):
                    for k in range(8):
                        P.mm(bank[:, :T], w_in[:, k, col0:col0 + 128], hT[:, k, :], start=(k == 0), stop=(k == 7))

                def s_norm():
                    if tile.sample:
                        P.dma("sp", o_pool_p[ia], pfx_pool[:, :, 0, :])
                        P.dma("sp", o_dnc_p[ia], pfx_q[:, :, 0, :])
                        P.dma("sp", pfx_pool, st_pool[ia])
                        P.dma("sp", pfx_q, st_dnc[ia])
                    norm_blocks([tile.b0], wnb, hT, 0, [ps[7]])
                step(s_norm)
                Wp = 15 + L

                def s_pool(g):
                    bank = ps[5 + g % 2]
                    inproj(bank, g * 128)
                    eb = ebp[g % 2][:, 0:nseq * Wp].rearrange("p (s w) -> p s w", s=nseq)
                    t1 = T1[:, 0:nseq * Wp].rearrange("p (s w) -> p s w", s=nseq)
                    t2 = T2[:, 0:nseq * Wp].rearrange("p (s w) -> p s w", s=nseq)
                    P.copy("pool", eb[:, :, 0:15], pfx_pool[:, g, 0:nseq, :])
                    P.copy("act", eb[:, :, 15:Wp], v3(bank[:, :T]))
                    P.tt("dve", t1[:, :, 1:Wp], eb[:, :, 1:Wp], eb[:, :, 0:Wp - 1], ALU.add)
                    res = t1
                    if g >= 1:
                        P.tt("dve", t2[:, :, 3:Wp], t1[:, :, 3:Wp], t1[:, :, 1:Wp - 2], ALU.add)
                        res = t2
                    if g >= 2:
                        P.tt("dve", t1[:, :, 7:Wp], t2[:, :, 7:Wp], t2[:, :, 3:Wp - 4], ALU.add)
                        res = t1
                    if g >= 3:
                        P.tt("dve", t2[:, :, 15:Wp], t1[:, :, 15:Wp], t1[:, :, 7:Wp - 8], ALU.add)
                        res = t2
                    y = yp[g % 2]
                    P.stt(v3(y), res[:, :, 15:Wp], 1.0 / POOL_WINDOWS[g], eb[:, :, 15:Wp], ALU.mult, ALU.subtract)
                    if tile.first and not tile.sample:
                        P.tt("dve", t16, res[:, 0, 15:31], invcnt[:, g, :], ALU.mult)
                        P.tt("dve", y[:, 0:16], t16, eb[:, 0, 15:31], ALU.subtract)
                    P.copy("pool", pfx_pool[:, g, 0:nseq, :], eb[:, :, L:L + 15])
                    P.mm(ps[7][:, :T], poolw[:, g, :], y)
                    P.act(ypool[:, g, :], ps[7][:, :T], AF.Identity, scale=pscale[:, g:g + 1])
                for g in range(4):
                    step(lambda g=g: s_pool(g))
                Wq = 3 + L

                def s_qkv_a(c):
                    bank = ps[5 + c % 2]
                    inproj(bank, 512 + c * 128)
                    eb = ebq[c % 2][:, 0:nseq * Wq].rearrange("p (s w) -> p s w", s=nseq)
                    P.copy("pool", eb[:, :, 0:3], pfx_q[:, c, 0:nseq, :])
                    P.copy("act", eb[:, :, 3:Wq], v3(bank[:, :T]))
                    conv_tap0(eb, dcw[:, c, :], L, v3(accb[c % 2]))

                def s_qkv_b(c):
                    eb = ebq[c % 2][:, 0:nseq * Wq].rearrange("p (s w) -> p s w", s=nseq)
                    cv = cvb[c % 2]
                    conv_rest(v3(cv), eb, dcw[:, c, :], 4, L, v3(accb[c % 2]))
                    P.copy("pool", pfx_q[:, c, 0:nseq, :], eb[:, :, L:L + 3])
                    if c < 8:
                        P.act(qk32[:, c, :], cv, AF.Silu)
                    else:
                        P.act(vT[:, c - 8, :], cv, AF.Silu)
                step(lambda: s_qkv_a(0))
                for c in range(12):
                    if c + 1 < 12:
                        step(lambda c=c: (s_qkv_a(c + 1), s_qkv_b(c)))
                    else:
                        step(lambda c=c: s_qkv_b(c))

                def s_z(c):
                    bank = ps[5 + c % 2]
                    inproj(bank, 2048 + c * 128)
                    P.act(sz[:, c, :], bank[:, :T], AF.Silu)
                for c in range(4):
                    step(lambda c=c: s_z(c))

                def s_ba():
                    bank = ps[7]
                    for k in range(8):
                        P.mm(bank[:, 0:8], hT[:, k, :], w_in[:, k, 2560:2568], start=(k == 0), stop=(k == 7))
                    P.copy("dve", tsm, bank[:, 0:8])
                    P.act(beta, tsm[:, 0:4], AF.Exp, scale=-1.0)
                    P.ts("dve", beta, beta, 1.0, ALU.add)
                    P.rec("dve", lambda e, o=beta: e.reciprocal(o, o), reads=[beta], writes=[beta])
                    P.tt("dve", gg, tsm[:, 4:8], dtb, ALU.add)
                    P.act(gg, gg, AF.Exp)
                    P.act(gg, gg, AF.Ln, bias=1.0, scale=1.0)
                    P.tt("dve", gg, gg, negA, ALU.mult)
                step(s_ba)

                def s_l2():
                    P.act(sq, qk32, AF.Square)
                    sqf = sq.rearrange("p a b -> p (a b)")
                    rsf = rs.rearrange("p a b -> p (a b)")
                    for hh in range(2):
                        P.mm(ps[5 + hh][:, :], ONES, sqf[:, hh * 512:(hh + 1) * 512])
                        P.act(rsf[:, hh * 512:(hh + 1) * 512], ps[5 + hh][:, :], AF.Ln, bias=eps_col[:], scale=1.0)
                    P.act(rsf[:, 0:512], rsf[:, 0:512], AF.Exp, scale=-0.5, bias=lnq)
                    P.act(rsf[:, 512:1024], rsf[:, 512:1024], AF.Exp, scale=-0.5)
                    P.tt("dve", qkT, qk32, rs, ALU.mult)
                step(s_l2)
                return steps

            def back_steps(tile, H):
                steps = []
                step = steps.append
                qkT, vT, sz, ypool, beta, gg = H
                ms = msk[1 if tile.sample else 0]
                UCUM, SSM, NEG, STRICT = ms
                gcrow = r4(ps[0])
                brow = r4(ps[2])
                KK = r4(ps[3])
                QK = r4(ps[4])

                def s_prep1():
                    for k in range(4):
                        P.dma("pool", wo_ring[k], w_out_ab[ia, :, k, :])
                    P.tt("dve", X1, bc4(UCUM), col4(gg), ALU.mult)
                    P.mm(ps[0][:, :], ONES, X1.rearrange("p a b -> p (a b)"))
                    P.mm(ps[1][:, 0:4], UCUM, gg)
                    P.mm(ps[1][:, 4:8], SSM, gg)
                    P.tt("dve", X2, bc4(ident_f[:]), col4(beta), ALU.mult)
                    P.mm(ps[2][:, :], ONES, X2.rearrange("p a b -> p (a b)"))
                    P.copy("dve", gcc, ps[1][:, 0:8])
                    P.act(gam, gcc[:, 0:4], AF.Exp)
                    P.tt("dve", dlt, gcc[:, 4:8], gcc[:, 0:4], ALU.subtract)
                    P.act(dlt, dlt, AF.Exp)
                    P.tt("dve", bgm, beta, gam, ALU.mult)
                step(s_prep1)

                def s_prep2():
                    for h in range(4):
                        P.ts("dve", X1[:, h, :], gcrow[:, h, :], gcc[:, h:h + 1], ALU.subtract, 0.0, ALU.min)
                    P.tt("dve", X1, X1, bc4(NEG), ALU.add)
                    P.act(Eb, X1, AF.Exp)
                    P.act(grow, gcrow, AF.Exp)
                    P.tt("dve", qgT, qkT[:, 0:4, :], grow, ALU.mult)
                    for h in range(4):
                        P.mm(KK[:, h, :], qkT[:, 4 + h, :], qkT[:, 4 + h, :])
                    for h in range(4):
                        P.mm(QK[:, h, :], qkT[:, 4 + h, :], qkT[:, h, :])
                step(s_prep2)

                def s_prep3():
                    P.tt("dve", aqkT, QK, Eb, ALU.mult)
                    P.tt("dve", X2, Eb, bc4(STRICT), ALU.mult)
                    P.tt("dve", X2, X2, brow, ALU.mult)
                    P.tt("dve", Nb, KK, X2, ALU.mult)
                    NTp = b4(ps[1], 0)
                    for h in range(4):
                        P.transpose(NTp[:, h, :], Nb[:, h, :], ident_b[:])
                    P.copy("act", NTb, NTp)
                    if tile.sample:
                        P.stt(P32, Nb, -1.0, bc4(ident_f[:]), ALU.mult, ALU.add)
                    else:
                        P.tt("dve", A2b[1], Nb, bc4(lvl_masks[0]), ALU.mult)
                        P.tt("dve", NTbd, NTb, bc4(lvl_masks[0]), ALU.mult)
                        P.stt(P32, A2b[1], -1.0, bc4(ident_f[:]), ALU.mult, ALU.add)
                    P.copy("act", Pb, P32)
                step(s_prep3)
                A2T_ps = r4(ps[3])
                A2_ps = r4(ps[4])
                PP_ps = r4(ps[1])

                def s_base(m_):
                    if tile.sample:
                        A0, AT0 = Nb, NTb
                    else:
                        A0, AT0 = A2b[1], NTbd
                    Acur, ATcur = (A0, AT0) if m_ == 0 else (A2b[0], A2Tb[0])
                    lastl = (m_ == 1)
                    a2t = A2Tb[m_]
                    for h in range(4):
                        P.mm(A2T_ps[:, h, :], Acur[:, h, :], ATcur[:, h, :])
                    if not lastl:
                        for h in range(4):
                            P.mm(A2_ps[:, h, :], ATcur[:, h, :], Acur[:, h, :])
                    P.copy("act", a2t, A2T_ps)
                    if not lastl:
                        P.copy("dve", A2b[0], A2_ps)
                    for h in range(4):
                        P.mm(PP_ps[:, h, :], a2t[:, h, :], Pb[:, h, :])
                    P.tt("dve", P32, P32, PP_ps, ALU.add)
                    P.copy("act", Pb, P32)
                step(lambda: s_base(0))
                step(lambda: s_base(1))

                def s_dbl(lv):
                    Zb, tmpb, UTb = A2b[0], A2Tb[0], A2Tb[1]
                    UT_ps = b4(ps[1], 0)
                    for h in range(4):
                        P.transpose(UT_ps[:, h, :], Pb[:, h, :], ident_b[:])
                    for h in range(4):
                        P.mm(A2T_ps[:, h, :], NTb[:, h, :], Pb[:, h, :])
                    P.copy("act", UTb, UT_ps)
                    P.copy("dve", Zb, A2T_ps)
                    for h in range(4):
                        P.mm(A2_ps[:, h, :], UTb[:, h, :], Zb[:, h, :])
                    P.tt("dve", tmpb, A2_ps, bc4(lvl_masks[1 + lv]), ALU.mult)
                    P.tt("dve", Pb, Pb, tmpb, ALU.add)
                if not tile.sample:
                    for lv in range(4):
                        step(lambda lv=lv: s_dbl(lv))
                wT_ps = r4(ps[2])

                def s_ktok():
                    ktp = b4(ps[2], 0)
                    vtp = b4(ps[2], 1)
                    for h in range(4):
                        P.transpose(ktp[:, h, :], qkT[:, 4 + h, :], ident_b[:])
                    for h in range(4):
                        P.transpose(vtp[:, h, :], vT[:, h, :], ident_b[:])
                    P.tt("dve", bgk, ktp, col4(bgm), ALU.mult)
                    P.tt("dve", kd, ktp, col4(dlt), ALU.mult)
                    P.tt("dve", bv, vtp, col4(beta), ALU.mult)
                step(s_ktok)

                def s_w():
                    for h in range(4):
                        P.mm(wT_ps[:, h, :], bgk[:, h, :], Pb[:, h, :])
                    P.rec("act", lambda e: e.mul(negwT, wT_ps, -1.0), reads=[wT_ps], writes=[negwT])
                step(s_w)
                if tile.sample:
                    seqs = [(s_ * LS, (s_ + 1) * LS, s_) for s_ in range(NSS)]
                else:
                    seqs = [(0, 128, None)]
                uT_ps = r4(ps[0])
                oT_ps = r4(ps[3])
                Sn_ps = r4(ps[4])

                def s_u():
                    for h in range(4):
                        P.mm(uT_ps[:, h, :], bv[:, h, :], Pb[:, h, :], start=(h == 0), stop=False, skip=True)
                    for si, (c0, c1, sidx) in enumerate(seqs):
                        if sidx is None:
                            Sb = Sbp
                        else:
                            Sb = Sbs[si % 2]
                            P.dma("pool", Sb, st_dn[ia, sidx])
                        for h in range(4):
                            P.mm(uT_ps[:, h, c0:c1], Sb[:, h, :], negwT[:, h, c0:c1], start=False, stop=True, skip=True)
                    P.copy("act", utT, uT_ps)
                    utk_ps = b4(ps[1], 0)
                    for h in range(4):
                        P.transpose(utk_ps[:, h, :], utT[:, h, :], ident_b[:])
                    P.copy("dve", utk, utk_ps)
                step(s_u)

                def s_o():
                    for h in range(4):
                        P.mm(oT_ps[:, h, :], utk[:, h, :], aqkT[:, h, :], start=(h == 0), stop=False, skip=True)
                    for si, (c0, c1, sidx) in enumerate(seqs):
                        if sidx is None:
                            Sb, S32, kdu = Sbp, S32p, kd
                        else:
                            Sb, S32, kdu = Sbs[si % 2], S32s[si % 2], kdm[si % 2]
                            P.dma("pool", Sb, st_dn[ia, sidx])
                            P.dma("sp", S32, st_dn[ia, sidx])
                            P.ts("dve", kdu, kd, onehot[:, sidx:sidx + 1], ALU.mult)
                        for h in range(4):
                            P.mm(oT_ps[:, h, c0:c1], Sb[:, h, :], qgT[:, h, c0:c1], start=False, stop=True, skip=True)
                        for h in range(4):
                            P.mm(Sn_ps[:, h, :], kdu[:, h, :], utk[:, h, :])
                        for h in range(4):
                            P.stt(S32[:, h, :], S32[:, h, :], grow[:, h, c1 - 1:c1], Sn_ps[:, h, :], ALU.mult, ALU.add)
                        if sidx is None:
                            P.copy("act", Sbp, S32p)
                            if tile.last:
                                P.dma("sp", o_dn_p[ia], S32p)
                        else:
                            P.dma("sp", o_dn_s[ia, sidx], S32)
                step(s_o)

                def s_out():
                    P.act(X1, oT_ps, AF.Square)
                    P.mm(ps[1][:, :], ONES, X1.rearrange("p a b -> p (a b)"))
                    X2f = X2.rearrange("p a b -> p (a b)")
                    P.act(X2f, ps[1][:, :], AF.Ln, bias=eps_col[:], scale=1.0 / 128.0)
                    P.act(X2f, X2f, AF.Exp, scale=-0.5)
                    P.tt("dve", X1, oT_ps, X2, ALU.mult)
                    P.stt(ydn, X1, nrmw[:, 0:1], sz, ALU.mult, ALU.mult)
                step(s_out)

                def s_proj():
                    for k in range(8):
                        src = ypool if k < 4 else ydn
                        for half in range(2):
                            P.mm(ps[half * 4][:, :], src[:, k % 4, :], wo_ring[k % 4][:, half * 512:(half + 1) * 512],
                                 start=(k == 0), stop=(k == 7))
                        if k + 4 < 8:
                            P.dma("pool", wo_ring[k % 4], w_out_ab[ia, :, k + 4, :])
                    for half in range(2):
                        xv = xs[:, tile.b0, half * 512:(half + 1) * 512]
                        P.tt("dve", xv, xv, ps[half * 4][:, :], ALU.add)
                step(s_proj)
                return steps

            def interleave(a, b):
                out, ia_, ib_ = [], 0, 0
                na, nb = len(a), len(b)
                while ia_ < na or ib_ < nb:
                    if ib_ >= nb or (ia_ < na and ia_ * nb <= ib_ * na):
                        out.append(a[ia_]); ia_ += 1
                    else:
                        out.append(b[ib_]); ib_ += 1
                return out

            pend = []
            for ti, tile in enumerate(tiles):
                fs = front_steps(tile, HS[ti % 2])
                for st_ in interleave(fs, pend):
                    st_()
                pend = back_steps(tile, HS[ti % 2])
            for st_ in pend:
                st_()
            P.dma("sp", o_pool_s[ia], pfx_pool)
            P.dma("sp", o_dnc_s[ia], pfx_q)
            AR.release()

        GS = 4

        def ffn(l):
            AR.mark()
            slots = []
            for s in range(2):
                slots.append((AR.alloc([8, 2, GS * 128], BF16), AR.alloc([GS, D], BF16)))
            hTall = AR.alloc([8, NTOK], BF16)
            cw = AR.alloc([NFC, 3], F32)
            a_sb = [AR.alloc([GS, 512], BF16) for _ in range(2)]
            ext_p = AR.alloc([GS, 1, 2 + 512], F32)
            ext_s = AR.alloc([GS, NSS, 2 + LS], F32)
            stage_s = [AR.alloc([GS, NSS, 2], F32) for _ in range(2)]
            stage_o = [AR.alloc([GS, NSS, 2], F32) for _ in range(2)]
            stage_p = [AR.alloc([GS, 2], F32) for _ in range(2)]
            acc = [AR.alloc([512], F32) for _ in range(3)]
            cv = [AR.alloc([512], F32) for _ in range(3)]
            P.dma("sp", cw, ffn_cw[l])
            wn = load_norm_w(norm_ffn[l:l + 1, :])
            norm_blocks(range(NB), wn, hTall, 0, [ps[6], ps[7]])
            groups = [(g0, min(GS, NFC - g0)) for g0 in range(0, NFC, GS)]

            def load_group(gi):
                g0, gs = groups[gi]
                wu, wd = slots[gi % 2]
                for j in range(2):
                    for k0 in range(0, 8, 4):
                        P.dma("pool", wu[:, k0:k0 + 4, j, 0:gs * 128],
                              w_up[l, :, k0:k0 + 4, j * DFF + g0 * 128: j * DFF + (g0 + gs) * 128])
                P.dma("pool", wd[:, 0:gs, :], w_down[l, :, g0:g0 + gs, :])

            load_group(0)
            tiles = prompt_tiles(4) + [sample_tile]
            ai = 0
            for gi, (g0, gs) in enumerate(groups):
                if gi + 1 < len(groups):
                    load_group(gi + 1)
                wu, wd = slots[gi % 2]
                sts = stage_s[gi % 2]
                P.dma("sp", sts[:, 0:gs], st_ffn[l, :, g0:g0 + gs])
                P.memset("pool", ext_p[:, :, 0, 0:2], 0.0)
                for c in range(gs):
                    P.copy("pool", ext_s[:, c, :, 0:2], sts[:, c])
                tasks = []
                for tile in tiles:
                    a = a_sb[ai % 2]
                    ai += 1
                    for c in range(gs):
                        tasks.append((tile, c, a))

                def FA(i, tasks=tasks, wu=wu, g0=g0):
                    tile, c, a = tasks[i]
                    T, L, nseq = tile.T, tile.L, tile.nseq
                    ext = ext_s if tile.sample else ext_p
                    tok0 = tile.b0 * 128
                    v3 = lambda ap_: ap_.rearrange("p (s l) -> p s l", s=nseq)
                    pg, pv = ps[(i % 3) * 2], ps[(i % 3) * 2 + 1]
                    for j, bank in enumerate((pg, pv)):
                        for k in range(8):
                            P.mm(bank[:, :T], wu[:, k, j, c * 128:(c + 1) * 128], hTall[:, k, tok0:tok0 + T],
                                 start=(k == 0), stop=(k == 7))
                    P.copy("act", ext[:, c, :, 2:2 + L], v3(pg[:, :T]))
                    conv_tap0(ext[:, c], cw[:, g0 + c, :], L, v3(acc[i % 3][:, :T]))

                def FB(i, tasks=tasks, wd=wd, g0=g0, gs=gs, gi=gi):
                    tile, c, a = tasks[i]
                    T, L, nseq = tile.T, tile.L, tile.nseq
                    ext = ext_s if tile.sample else ext_p
                    v3 = lambda ap_: ap_.rearrange("p (s l) -> p s l", s=nseq)
                    pv = ps[(i % 3) * 2 + 1]
                    conv_rest(v3(cv[i % 3][:, :T]), ext[:, c], cw[:, g0 + c, :], 3, L, v3(acc[i % 3][:, :T]))
                    P.copy("pool", ext[:, c, :, 0:2], ext[:, c, :, L:L + 2])
                    P.act(cv[i % 3][:, :T], cv[i % 3][:, :T], AF.Silu)
                    P.tt("dve", a[:, c, :T], cv[i % 3][:, :T], pv[:, :T], ALU.mult)
                    if c == gs - 1:
                        out_proj_accum(tile, lambda k, blk: a[:, k, blk * 128:(blk + 1) * 128],
                                       gs, lambda k, half: wd[:, k, half * 512:(half + 1) * 512],
                                       [ps[6], ps[7]])
                        if tile.last:
                            if tile.sample:
                                so = stage_o[gi % 2]
                                for c2 in range(gs):
                                    P.copy("pool", so[:, c2], ext_s[:, c2, :, 0:2])
                                P.dma("sp", o_ffn_s[l, :, g0:g0 + gs], so[:, 0:gs])
                            else:
                                sp_ = stage_p[gi % 2]
                                P.copy("pool", sp_[:, 0:gs], ext_p[:, 0:gs, 0, 0:2])
                                P.dma("sp", o_ffn_p[l, :, g0:g0 + gs, :], sp_[:, 0:gs])
                skewed(len(tasks), FA, FB)
            AR.release()

        ic = 0
        ia = 0
        for l, typ in enumerate(layers):
            if typ == "c":
                mixer_c(l, ic)
                ic += 1
            else:
                mixer_ab(l, ia)
                ia += 1
            ffn(l)

        AR.mark()
        wn = load_norm_w(norm_final[0:1, :])
        yo = [AR.alloc([D], F32) for _ in range(2)]
        for b in range(NB):
            y = yo[b % 2]
            P.act(y, xs[:, b, :], AF.Square, accum_out=ss[:, b:b + 1])
            P.act(rstd[:, b:b + 1], ss[:, b:b + 1], AF.Ln, bias=eps_col[:], scale=1.0 / D)
            P.act(rstd[:, b:b + 1], rstd[:, b:b + 1], AF.Exp, scale=-0.5)
            P.stt(y, xs[:, b, :], rstd[:, b:b + 1], wn[:], ALU.mult, ALU.mult)
            if b < NPB:
                P.dma("sp" if b % 2 == 0 else "act", y_p[b * 128:(b + 1) * 128, :], y)
            else:
                P.dma("sp", y_s, y)
        AR.release()
        P.emit()
    P.arena_bytes = ARB
    return nc, P


def _pkn(w):
    sh = w.shape
    k = sh[-2] // 128
    w = w.reshape(sh[:-2] + (k, 128, sh[-1]))
    return np.ascontiguousarray(np.swapaxes(w, -3, -2))


def _chan(v, nchunk):
    sh = v.shape
    v = v.reshape(sh[:-1] + (nchunk, 128))
    return np.ascontiguousarray(np.moveaxis(np.moveaxis(v, -1, -3), -1, -2))


_CACHE = {}


def run_model(inputs, SEQ, layers):
    key = (SEQ, tuple(layers))
    if key not in _CACHE:
        _CACHE[key] = build_program(SEQ, layers)
    nc, P = _CACHE[key]
    DEPTH = len(layers)
    f = lambda a: np.ascontiguousarray(np.asarray(a, dtype=np.float32))
    shared = {
        "norm_mix": f(inputs["norm_mix"]), "norm_ffn": f(inputs["norm_ffn"]),
        "norm_final": f(inputs["norm_final"]).reshape(1, D),
        "w_up": _pkn(f(inputs["w_up"])), "w_down": _pkn(f(inputs["w_down"])),
        "ffn_cw": _chan(f(inputs["ffn_conv_w"]), NFC),
        "ident": np.eye(128, dtype=np.float32),
    }
    N_C = sum(1 for t in layers if t == "c")
    N_AB = sum(1 for t in layers if t == "ab")
    if N_AB:
        shared["w_in_ab"] = _pkn(f(inputs["w_in_ab"]))
        shared["w_out_ab"] = _pkn(f(inputs["w_out_ab"]))
        shared["pool_w"] = np.ascontiguousarray(f(inputs["pool_w"]).transpose(0, 2, 1, 3))
        shared["pool_scale"] = np.ascontiguousarray(f(inputs["pool_scale"]).reshape(N_AB, 4, 128).transpose(0, 2, 1))
        shared["dn_cw"] = _chan(f(inputs["dn_conv_w"]), 12)
        shared["dn_nw"] = f(inputs["dn_norm_w"]).reshape(N_AB, 128, 1)
        shared["a_log"] = f(inputs["dn_a_log"])
        shared["dt_bias"] = f(inputs["dn_dt_bias"])
        idx = np.arange(128)
        k_ = idx[:, None]
        j_ = idx[None, :]
        msk = np.zeros((2, 4, 128, 128), np.float32)
        for si, same in enumerate((np.ones((128, 128), bool), (k_ // LS) == (j_ // LS))):
            msk[si, 0] = (same & (k_ <= j_))
            msk[si, 1] = same
            msk[si, 2] = np.where(same & (j_ >= k_), 0.0, -BIG)
            msk[si, 3] = (same & (j_ > k_))
        shared["masks"] = msk
        lm = np.zeros((5, 128, 128), np.float32)
        lm[0] = ((k_ // 8) == (j_ // 8))
        for li, sz_ in enumerate((8, 16, 32, 64)):
            lm[1 + li] = -1.0 * (((k_ // (2 * sz_)) == (j_ // (2 * sz_))) & ((k_ % (2 * sz_)) < sz_) & ((j_ % (2 * sz_)) >= sz_))
        shared["lvlmask"] = lm
        shared["onehot"] = ((idx[:, None] // LS) == np.arange(NSS)[None, :]).astype(np.float32)
        t_ = np.arange(16)
        ic_ = np.stack([1.0 / np.minimum(t_ + 1, w) for w in POOL_WINDOWS]).astype(np.float32)
        shared["invcnt"] = np.ascontiguousarray(np.broadcast_to(ic_[None], (128, 4, 16)))
    if N_C:
        shared["w_in_c"] = _pkn(f(inputs["w_in_c"]))
        shared["w_out_c"] = _pkn(f(inputs["w_out_c"]))
        shared["sc_cw"] = _chan(f(inputs["sc_conv_w"]), 8)
    xp = f(inputs["x_prompt"])
    xsm = f(inputs["x_sample"])
    st_ffn = f(inputs["state_ffn_conv"])
    st_sc = f(inputs["state_sconv"]) if N_C else None
    in_maps = []
    for c in range(NCORES):
        m = dict(shared)
        m["x_p"] = xp[c]
        m["x_s"] = xsm[c * NSS:(c + 1) * NSS].reshape(NSS * LS, D)
        s = st_ffn[:, c * NSS:(c + 1) * NSS]
        s = s.reshape(DEPTH, NSS, 2, NFC, 128)
        m["st_ffn"] = np.ascontiguousarray(s.transpose(0, 4, 3, 1, 2))
        if N_C:
            s = st_sc[:, c * NSS:(c + 1) * NSS].reshape(N_C, NSS, 2, 8, 128)
            m["st_sc"] = np.ascontiguousarray(s.transpose(0, 4, 3, 1, 2))
        if N_AB:
            s = f(inputs["state_pool"])[:, c * NSS:(c + 1) * NSS].reshape(N_AB, NSS, 15, 4, 128)
            m["st_pool"] = np.ascontiguousarray(s.transpose(0, 4, 3, 1, 2))
            s = f(inputs["state_dn_conv"])[:, c * NSS:(c + 1) * NSS].reshape(N_AB, NSS, 3, 12, 128)
            m["st_dnc"] = np.ascontiguousarray(s.transpose(0, 4, 3, 1, 2))
            s = f(inputs["state_dn"])[:, c * NSS:(c + 1) * NSS]
            m["st_dn"] = np.ascontiguousarray(s.transpose(0, 1, 3, 2, 4))
        in_maps.append(m)
    res = run_bass_kernel_spmd(nc, in_maps, core_ids=list(range(NCORES)))
    R = res.results
    out = {}
    out["y_prompt"] = np.stack([R[c]["y_p"] for c in range(NCORES)])
    out["y_sample"] = np.concatenate([R[c]["y_s"].reshape(NSS, LS, D) for c in range(NCORES)])
    out["new_ffnconv_p"] = np.stack([R[c]["o_ffn_p"].transpose(0, 3, 2, 1).reshape(DEPTH, 2, DFF)
                                     for c in range(NCORES)], axis=1)
    out["new_ffnconv_s"] = np.concatenate(
        [R[c]["o_ffn_s"].transpose(0, 3, 4, 2, 1).reshape(DEPTH, NSS, 2, DFF) for c in range(NCORES)], axis=1)
    if N_AB:
        out["new_pool_p"] = np.stack([R[c]["o_pool_p"].transpose(0, 3, 2, 1).reshape(N_AB, 15, DPOOL)
                                      for c in range(NCORES)], axis=1)
        out["new_pool_s"] = np.concatenate(
            [R[c]["o_pool_s"].transpose(0, 3, 4, 2, 1).reshape(N_AB, NSS, 15, DPOOL) for c in range(NCORES)], axis=1)
        out["new_dnconv_p"] = np.stack([R[c]["o_dnc_p"].transpose(0, 3, 2, 1).reshape(N_AB, 3, DQKV)
                                        for c in range(NCORES)], axis=1)
        out["new_dnconv_s"] = np.concatenate(
            [R[c]["o_dnc_s"].transpose(0, 3, 4, 2, 1).reshape(N_AB, NSS, 3, DQKV) for c in range(NCORES)], axis=1)
        out["new_dn_p"] = np.stack([R[c]["o_dn_p"].transpose(0, 2, 1, 3) for c in range(NCORES)], axis=1)
        out["new_dn_s"] = np.concatenate([R[c]["o_dn_s"].transpose(0, 1, 3, 2, 4) for c in range(NCORES)], axis=1)
    if N_C:
        out["new_sconv_p"] = np.stack([R[c]["o_sc_p"].transpose(0, 3, 2, 1).reshape(N_C, 2, D)
                                       for c in range(NCORES)], axis=1)
        out["new_sconv_s"] = np.concatenate(
            [R[c]["o_sc_s"].transpose(0, 3, 4, 2, 1).reshape(N_C, NSS, 2, D) for c in range(NCORES)], axis=1)
    return out


def kernel(**inputs):
    out = run_model(inputs, 2048, ["ab", "c", "ab", "c"])
    names = ["y_prompt", "y_sample", "new_pool_p", "new_pool_s", "new_dnconv_p", "new_dnconv_s", "new_dn_p",
             "new_dn_s", "new_sconv_p", "new_sconv_s", "new_ffnconv_p", "new_ffnconv_s"]
    return tuple(np.ascontiguousarray(out[n], dtype=np.float32) for n in names)
```

```python
import contextlib
import numpy as np
import concourse.bass as bass
import concourse.mybir as mybir
from concourse.bass_utils import run_bass_kernel_spmd

F32 = mybir.dt.float32
BF16 = mybir.dt.bfloat16
AF = mybir.ActivationFunctionType
ALU = mybir.AluOpType

ENGS = ("pe", "act", "dve", "pool", "sp")
BLOCKATTR = {"pe": "tensor", "act": "scalar", "dve": "vector", "pool": "gpsimd", "sp": "sync"}


class Op:
    __slots__ = ("eng", "fn", "deps", "odeps", "cost", "gi", "fin", "signal", "semval", "idx", "is_dma", "dsem",
                 "dval", "qprev")

    def __init__(self, eng, fn, is_dma=False):
        self.eng = eng
        self.fn = fn
        self.deps = set()
        self.odeps = set()
        self.cost = 300.0
        self.gi = 0
        self.fin = 0.0
        self.signal = False
        self.semval = None
        self.idx = None
        self.is_dma = is_dma
        self.dsem = None
        self.dval = None
        self.qprev = None


def _isz(dt):
    return 2 if dt == BF16 else 4


def _box(ap):
    t = ap.tensor
    item = _isz(ap.dtype)
    pstep = 1
    for s in list(t.shape)[1:]:
        pstep *= int(s)
    off = int(ap.offset)
    dims = [(int(a), int(b)) for a, b in ap.ap]
    p0 = off // pstep
    f0 = off % pstep
    if dims and dims[0][0] == pstep:
        pn = dims[0][1]
        rest = dims[1:]
    elif dims and dims[0][0] == 0 and len(dims) > 1:
        pn = 1
        rest = dims[1:]
    else:
        pn = 1
        rest = dims
    ext = 0
    lo = 0
    for st, cnt in rest:
        if st >= 0:
            ext += st * (cnt - 1)
        else:
            lo += st * (cnt - 1)
    if type(t).__name__.startswith("PSum"):
        return (t.name, 0, 128, 0, 1 << 30)
    return (t.name, p0, p0 + pn, (f0 + lo) * item, (f0 + ext + 1) * item)


def _overlap(a, b):
    return a[1] < b[2] and b[1] < a[2] and a[3] < b[4] and b[3] < a[4]


def _covers(a, b):
    return a[1] <= b[1] and a[2] >= b[2] and a[3] <= b[3] and a[4] >= b[4]


def _is_dram(ap):
    return type(ap.tensor).__name__.startswith("DRam")


class Prog:
    NDSEM = 8

    def __init__(self, nc):
        self.nc = nc
        self.q = {e: [] for e in ENGS}
        self.acc = {}
        self.dma_ops = {e: [] for e in ENGS}
        self.nwaits = 0
        self.ngi = 0

    def rec(self, eng, fn, reads=(), writes=(), is_dma=False, cost=None):
        op = Op(eng, fn, is_dma=is_dma)
        op.idx = len(self.q[eng])
        op.gi = self.ngi
        self.ngi += 1
        if cost is None:
            n = 0
            for ap in list(writes)[:1]:
                n = 1
                for d_ in list(ap.shape)[1:]:
                    n *= int(d_)
            if is_dma:
                nb = n * 128 * 4
                cost = 2000.0 + nb / 80.0
            elif eng == "act":
                cost = 224.0 + 0.85 * n
            elif eng == "dve":
                cost = 200.0 + 0.95 * n
            elif eng == "pool":
                cost = 150.0 + 2.0 * n
            else:
                cost = 100.0
        op.cost = cost
        for ap in reads:
            if ap is None or _is_dram(ap):
                continue
            bx = _box(ap)
            lst = self.acc.setdefault(bx[0], [])
            for (b2, o2, w2) in lst:
                if w2 and o2 is not op and _overlap(bx, b2):
                    op.deps.add(o2)
            if not is_dma:
                for e in lst:
                    if (not e[2]) and e[0] == bx and e[1].eng == eng and not e[1].is_dma and e[1] is not op:
                        op.odeps.add(e[1])
                lst[:] = [e for e in lst if not ((not e[2]) and e[0] == bx and e[1].eng == eng and not e[1].is_dma)]
            lst.append((bx, op, False))
        psum_reads = [ap for ap in reads if ap is not None and type(ap.tensor).__name__.startswith("PSum")]
        for ap in list(writes) + psum_reads:
            if ap is None or _is_dram(ap):
                continue
            bx = _box(ap)
            lst = self.acc.setdefault(bx[0], [])
            keep = []
            for ent in lst:
                b2, o2, w2 = ent
                if o2 is op:
                    keep.append(ent)
                    continue
                if _overlap(bx, b2):
                    same_compute = (o2.eng == eng == "pe") and (not o2.is_dma) and (not is_dma)
                    if not same_compute:
                        op.deps.add(o2)
                    else:
                        op.odeps.add(o2)
                    if _covers(bx, b2):
                        continue
                keep.append(ent)
            keep.append((bx, op, True))
            self.acc[bx[0]] = keep
        self.q[eng].append(op)
        return op

    def schedule(self):
        import heapq
        allops = []
        for e in ENGS:
            allops.extend(self.q[e])
        allops.sort(key=lambda o: o.gi)
        succ = {}
        indeg = {}
        for o in allops:
            ds = o.deps | o.odeps
            indeg[o] = len(ds)
            for d in ds:
                succ.setdefault(d, []).append(o)
        import os as _os2
        HOP = float(_os2.environ.get("KHOP", "700"))
        CSC = float(_os2.environ.get("KCSC", "1.3"))
        MODE = _os2.environ.get("KMODE", "cp")
        tail = {}
        for o in reversed(allops):
            m_ = 0.0
            for c in succ.get(o, ()):
                lat = HOP if (c.eng != o.eng or o.is_dma) else 60.0
                v_ = lat + tail[c]
                if v_ > m_:
                    m_ = v_
            tail[o] = o.cost + m_
        est = {}
        free = {e: 0.0 for e in ENGS}
        newq = {e: [] for e in ENGS}
        pend = {e: [] for e in ENGS}
        avail = {e: [] for e in ENGS}
        for o in allops:
            if indeg[o] == 0:
                est[o] = 0.0
                heapq.heappush(pend[o.eng], (0.0, o.gi, o))
        INF = float("inf")
        nleft = len(allops)
        while nleft:
            best_e, best_t = None, INF
            for e in ENGS:
                pe_, av_ = pend[e], avail[e]
                while pe_ and pe_[0][0] <= free[e]:
                    t_, g_, o_ = heapq.heappop(pe_)
                    heapq.heappush(av_, ((-tail[o_]) if MODE == "cp" else t_, g_, o_))
                if av_:
                    t = free[e]
                elif pe_:
                    t = pe_[0][0]
                else:
                    continue
                if t < best_t:
                    best_e, best_t = e, t
            e = best_e
            if not avail[e]:
                t_, g_, o_ = heapq.heappop(pend[e])
                heapq.heappush(avail[e], ((-tail[o_]) if MODE == "cp" else t_, g_, o_))
            _, _, o = heapq.heappop(avail[e])
            start = max(est[o], free[e])
            if o.is_dma:
                free[e] = start + 60.0
                o.fin = start + o.cost
            else:
                o.fin = start + o.cost * (CSC if o.eng != "pe" else 1.0)
                free[e] = o.fin
            newq[e].append(o)
            nleft -= 1
            for c in succ.get(o, ()):
                indeg[c] -= 1
                lat = HOP if (c.eng != o.eng or o.is_dma) else 60.0
                if o in c.odeps and o not in c.deps:
                    tt_ = max(est.get(c, 0.0), start)
                else:
                    tt_ = max(est.get(c, 0.0), o.fin + lat)
                est[c] = tt_
                if indeg[c] == 0:
                    heapq.heappush(pend[c.eng], (tt_, c.gi, c))
        assert sum(len(v) for v in newq.values()) == len(allops)
        import os as _os
        keep = _os.environ.get("KEEP_ORDER", "").split(",")
        for e in ENGS:
            if e in keep or "all" in keep:
                newq[e] = sorted(newq[e], key=lambda o: o.gi)
        self.q = newq
        for e in ENGS:
            self.dma_ops[e] = []
            for i, o in enumerate(self.q[e]):
                o.idx = i
            for o in self.q[e]:
                if o.is_dma:
                    i = len(self.dma_ops[e])
                    o.dsem = i % self.NDSEM
                    o.dval = 16 * (i // self.NDSEM + 1)
                    o.qprev = self.dma_ops[e][i - self.NDSEM] if i >= self.NDSEM else None
                    self.dma_ops[e].append(o)

    def mm(self, out, lhsT, rhs, start=True, stop=True, skip=False):
        n = 1
        for d_ in list(rhs.shape)[1:]:
            n *= int(d_)
        cost = 64.0 + 0.52 * n
        if rhs.dtype == F32:
            cost *= 4
        if skip:
            return self.rec("pe", lambda e: e.matmul(out, lhsT, rhs, start=start, stop=stop, skip_group_check=True),
                            reads=[lhsT, rhs], writes=[out], cost=cost)
        return self.rec("pe", lambda e: e.matmul(out, lhsT, rhs, start=start, stop=stop),
                        reads=[lhsT, rhs], writes=[out], cost=cost)

    def transpose(self, out, in_, ident):
        return self.rec("pe", lambda e: e.transpose(out, in_, ident), reads=[in_, ident], writes=[out], cost=110.0)

    def act(self, out, in_, func, bias=None, scale=None, accum_out=None):
        kw = {}
        rd = [in_]
        wr = [out]
        if bias is not None:
            kw["bias"] = bias
            if not isinstance(bias, (int, float)):
                rd.append(bias)
        if scale is not None:
            kw["scale"] = scale
            if not isinstance(scale, (int, float)):
                rd.append(scale)
        if accum_out is not None:
            kw["accum_out"] = accum_out
            wr.append(accum_out)
        return self.rec("act", lambda e: e.activation(out, in_, func, **kw), reads=rd, writes=wr)

    def tt(self, eng, out, in0, in1, op):
        return self.rec(eng, lambda e: e.tensor_tensor(out, in0, in1, op), reads=[in0, in1], writes=[out])

    def ts(self, eng, out, in0, s1, op0, s2=None, op1=None):
        rd = [in0]
        for s in (s1, s2):
            if s is not None and not isinstance(s, (int, float)):
                rd.append(s)
        if op1 is None:
            return self.rec(eng, lambda e: e.tensor_scalar(out, in0, s1, None, op0), reads=rd, writes=[out])
        return self.rec(eng, lambda e: e.tensor_scalar(out, in0, s1, s2, op0, op1), reads=rd, writes=[out])

    def stt(self, out, in0, scalar, in1, op0, op1):
        rd = [in0, in1]
        if not isinstance(scalar, (int, float)):
            rd.append(scalar)
        return self.rec("dve", lambda e: e.scalar_tensor_tensor(out, in0, scalar, in1, op0, op1),
                        reads=rd, writes=[out])

    def copy(self, eng, out, in_):
        if eng == "act":
            return self.rec("act", lambda e: e.copy(out, in_), reads=[in_], writes=[out])
        return self.rec(eng, lambda e: e.tensor_copy(out, in_), reads=[in_], writes=[out])

    def memset(self, eng, ap, val):
        return self.rec(eng, lambda e: e.memset(ap, val), reads=[], writes=[ap])

    def dma(self, eng, out, in_, **kw):
        if out.dtype != in_.dtype:
            kw.setdefault("max_dma_last_dim", 4096)
        return self.rec(eng, lambda e: e.dma_start(out=out, in_=in_, **kw), reads=[in_], writes=[out], is_dma=True)

    def emit(self):
        nc = self.nc
        self.schedule()
        for e in ENGS:
            for op in self.q[e]:
                for d in op.deps:
                    if not d.is_dma:
                        d.signal = True
        for e in ENGS:
            c = 0
            for op in self.q[e]:
                if op.signal:
                    c += 1
                    op.semval = c
        with contextlib.ExitStack() as st:
            csem = {e: st.enter_context(nc.semaphore("s_" + e)) for e in ENGS}
            dsem = {e: [st.enter_context(nc.semaphore("d_%s%d" % (e, i))) for i in range(self.NDSEM)]
                    for e in ENGS if self.dma_ops[e]}
            block = st.enter_context(nc.Block())

            def make(ename):
                def body(eh):
                    waited = {}
                    for op in self.q[ename]:
                        need = {}
                        for d in op.deps:
                            if d.is_dma:
                                key = ("d", d.eng, d.dsem)
                                val = d.dval
                            else:
                                key = ("c", d.eng)
                                val = d.semval
                            if val > need.get(key, 0):
                                need[key] = val
                        if op.is_dma and op.qprev is not None:
                            key = ("d", op.eng, op.dsem)
                            val = op.qprev.dval
                            if val > need.get(key, 0):
                                need[key] = val
                        for key, val in need.items():
                            if waited.get(key, 0) >= val:
                                continue
                            waited[key] = val
                            sem = csem[key[1]] if key[0] == "c" else dsem[key[1]][key[2]]
                            eh.wait_ge(sem, val)
                            self.nwaits += 1
                        inst = op.fn(eh)
                        if op.is_dma:
                            inst.then_inc(dsem[op.eng][op.dsem], 16)
                        elif op.signal:
                            inst.then_inc(csem[op.eng], 1)
                    if ename == "sp":
                        for e2 in ENGS:
                            lst = self.dma_ops[e2]
                            if not lst:
                                continue
                            last = {}
                            for o in lst:
                                last[o.dsem] = o.dval
                            for s, v in last.items():
                                if waited.get(("d", e2, s), 0) < v:
                                    eh.wait_ge(dsem[e2][s], v)
                return body

            for ename in ENGS:
                getattr(block, BLOCKATTR[ename])(make(ename))


D = 1024
DFF = 2816
NFC = DFF // 128
IN_AB = 2568
DPOOL = 512
DQKV = 1536
NH = 4
NSS = 16
LS = 8
EPS = 1e-6
POOL_WINDOWS = (2, 4, 8, 16)
NCORES = 8
BIG = 30000.0


class Tile_:
    def __init__(self, b0, nblk, nseq, L, sample, first, last):
        self.b0, self.nblk, self.nseq, self.L = b0, nblk, nseq, L
        self.T = nblk * 128
        self.sample, self.first, self.last = sample, first, last


class Arena:
    def __init__(self, ar, nbytes):
        self.ar = ar
        self.nbytes = nbytes
        self.top = 0
        self.marks = []

    def alloc(self, shape, dt, at=None):
        n = 1
        for s in shape:
            n *= s
        nb = n * _isz(dt)
        nb_al = (nb + 63) // 64 * 64
        if at is None:
            off = self.top
            self.top += nb_al
        else:
            off = at
            assert off + nb_al <= self.top
        assert self.top <= self.nbytes, ("arena overflow", self.top, self.nbytes)
        v = self.ar[:, off // 2: off // 2 + nb // 2]
        if dt == F32:
            v = v.bitcast(F32)
        if len(shape) > 1:
            names = " ".join("d%d" % i for i in range(len(shape)))
            kw = {"d%d" % i: shape[i] for i in range(len(shape) - 1)}
            v = v.rearrange("p (%s) -> p %s" % (names, names), **kw)
        return v

    def mark(self):
        self.marks.append(self.top)

    def release(self):
        self.top = self.marks.pop()


def build_program(SEQ, layers):
    DEPTH = len(layers)
    N_AB = sum(1 for t in layers if t == "ab")
    N_C = sum(1 for t in layers if t == "c")
    NPB = SEQ // 128
    NB = NPB + 1
    NTOK = NB * 128
    nc = bass.Bass("TRN2", target_bir_lowering=False)
    P = Prog(nc)

    def din(name, shape):
        return nc.dram_tensor(name, list(shape), F32, kind="ExternalInput").ap()

    def dout(name, shape):
        return nc.dram_tensor(name, list(shape), F32, kind="ExternalOutput").ap()

    x_p = din("x_p", [SEQ, D])
    x_s = din("x_s", [128, D])
    norm_mix = din("norm_mix", [DEPTH, D])
    norm_ffn = din("norm_ffn", [DEPTH, D])
    norm_final = din("norm_final", [1, D])
    w_up = din("w_up", [DEPTH, 128, 8, 2 * DFF])
    w_down = din("w_down", [DEPTH, 128, NFC, D])
    ffn_cw = din("ffn_cw", [DEPTH, 128, NFC, 3])
    st_ffn = din("st_ffn", [DEPTH, 128, NFC, NSS, 2])
    ident_in = din("ident", [128, 128])
    y_p = dout("y_p", [SEQ, D])
    y_s = dout("y_s", [128, D])
    o_ffn_p = dout("o_ffn_p", [DEPTH, 128, NFC, 2])
    o_ffn_s = dout("o_ffn_s", [DEPTH, 128, NFC, NSS, 2])
    if N_C:
        w_in_c = din("w_in_c", [N_C, 128, 8, 3 * D])
        w_out_c = din("w_out_c", [N_C, 128, 8, D])
        sc_cw = din("sc_cw", [N_C, 128, 8, 3])
        st_sc = din("st_sc", [N_C, 128, 8, NSS, 2])
        o_sc_p = dout("o_sc_p", [N_C, 128, 8, 2])
        o_sc_s = dout("o_sc_s", [N_C, 128, 8, NSS, 2])

    if N_AB:
        w_in_ab = din("w_in_ab", [N_AB, 128, 8, IN_AB])
        w_out_ab = din("w_out_ab", [N_AB, 128, 8, D])
        pool_w_in = din("pool_w", [N_AB, 128, 4, 128])
        pool_scale_in = din("pool_scale", [N_AB, 128, 4])
        dn_cw_in = din("dn_cw", [N_AB, 128, 12, 4])
        dn_nw_in = din("dn_nw", [N_AB, 128, 1])
        a_log_in = din("a_log", [N_AB, 4])
        dt_bias_in = din("dt_bias", [N_AB, 4])
        st_pool = din("st_pool", [N_AB, 128, 4, NSS, 15])
        st_dnc = din("st_dnc", [N_AB, 128, 12, NSS, 3])
        st_dn = din("st_dn", [N_AB, NSS, 128, 4, 128])
        masks_in = din("masks", [2, 4, 128, 128])
        onehot_in = din("onehot", [128, NSS])
        invcnt_in = din("invcnt", [128, 4, 16])
        lvlmask_in = din("lvlmask", [5, 128, 128])
        o_pool_p = dout("o_pool_p", [N_AB, 128, 4, 15])
        o_pool_s = dout("o_pool_s", [N_AB, 128, 4, NSS, 15])
        o_dnc_p = dout("o_dnc_p", [N_AB, 128, 12, 3])
        o_dnc_s = dout("o_dnc_s", [N_AB, 128, 12, NSS, 3])
        o_dn_p = dout("o_dn_p", [N_AB, 128, 4, 128])
        o_dn_s = dout("o_dn_s", [N_AB, NSS, 128, 4, 128])

    with contextlib.ExitStack() as st:
        def sb(name, shape, dt):
            return st.enter_context(nc.sbuf_tensor(name, list(shape), dt))

        xs = sb("xs", [128, NB, D], F32)
        ident_f = sb("ident_f", [128, 128], F32)
        ident_b = sb("ident_b", [128, 128], BF16)
        eps_col = sb("eps_col", [128, 1], F32)
        hn = [sb("hn%d" % i, [128, D], BF16) for i in range(2)]
        ss = sb("ss", [128, NB], F32)
        rstd = sb("rstd", [128, NB], F32)
        ARB = (int(nc.sbuf_bytes_remaining) - 256) // 64 * 64
        ar_t = sb("arena", [128, ARB // 2], BF16)
        AR = Arena(ar_t, ARB)
        ps = [st.enter_context(nc.psum_tensor("ps%d" % i, [128, 512], F32)) for i in range(8)]

        P.dma("sp", ident_f[:], ident_in)
        P.copy("dve", ident_b[:], ident_f[:])
        P.memset("dve", eps_col[:], EPS)
        for b in range(NPB):
            P.dma("sp" if b % 2 == 0 else "act", xs[:, b, :], x_p[b * 128:(b + 1) * 128, :])
        P.dma("sp", xs[:, NPB, :], x_s)

        prompt_tiles = lambda tb: [Tile_(i * tb, tb, 1, tb * 128, False, i == 0, i == NPB // tb - 1)
                                   for i in range(NPB // tb)]
        sample_tile = Tile_(NPB, 1, NSS, LS, True, True, True)

        def load_norm_w(src_row):
            wnb_ = AR.alloc([D], F32)
            P.dma("sp", wnb_, src_row.broadcast_to([128, D]))
            return wnb_

        hn_ctr = [0]
        tp_ctr = [0]

        def norm_blocks(blocks, wn, hT, col0, tp_banks):
            for i, b in enumerate(blocks):
                h = hn[hn_ctr[0] % 2]
                hn_ctr[0] += 1
                P.act(h[:], xs[:, b, :], AF.Square, accum_out=ss[:, b:b + 1])
                P.act(rstd[:, b:b + 1], ss[:, b:b + 1], AF.Ln, bias=eps_col[:], scale=1.0 / D)
                P.act(rstd[:, b:b + 1], rstd[:, b:b + 1], AF.Exp, scale=-0.5)
                P.stt(h[:], xs[:, b, :], rstd[:, b:b + 1], wn[:], ALU.mult, ALU.mult)
                bank = tp_banks[tp_ctr[0] % len(tp_banks)]
                tp_ctr[0] += 1
                pv = bank[:].bitcast(BF16).rearrange("p (k t) -> p k t", k=8)
                for k in range(8):
                    P.transpose(pv[:, k, :], h[:, k * 128:(k + 1) * 128], ident_b[:])
                c0 = col0 + i * 128
                P.copy("act" if i % 2 == 0 else "dve", hT[:, :, c0:c0 + 128], pv)

        def conv_tap0(ext, cw, L, acc):
            P.act(acc, ext[:, :, 0:L], AF.Identity, scale=cw[:, 0:1])

        def conv_rest(out, ext, cw, ntaps, L, acc):
            for j in range(1, ntaps):
                dst = out if j == ntaps - 1 else acc
                P.stt(dst, ext[:, :, j:j + L], cw[:, j:j + 1], acc, ALU.mult, ALU.add)

        def conv_taps(out, ext, cw, ntaps, L, acc):
            conv_tap0(ext, cw, L, acc)
            conv_rest(out, ext, cw, ntaps, L, acc)

        def skewed(n, A_, B_, skew=1):
            for i in range(min(skew, n)):
                A_(i)
            for i in range(n):
                if i + skew < n:
                    A_(i + skew)
                B_(i)

        def out_proj_accum(tile, lhs_fn, nk, rhs_fn, banks):
            i = 0
            for blk in range(tile.nblk):
                for half in range(2):
                    bank = banks[i % len(banks)]
                    i += 1
                    for k in range(nk):
                        P.mm(bank[:, :], lhs_fn(k, blk), rhs_fn(k, half), start=(k == 0), stop=(k == nk - 1))
                    xv = xs[:, tile.b0 + blk, half * 512:(half + 1) * 512]
                    P.tt("dve", xv, xv, bank[:, :], ALU.add)

        def mixer_c(l, ic):
            AR.mark()
            w_in = AR.alloc([8, 3 * D], BF16)
            w_out = AR.alloc([8, D], BF16)
            hT = AR.alloc([8, 512], BF16)
            m = AR.alloc([8, 512], BF16)
            cw = AR.alloc([8, 3], F32)
            ext_p = AR.alloc([8, 1, 2 + 512], F32)
            ext_s = AR.alloc([8, NSS, 2 + LS], F32)
            stage = AR.alloc([8, NSS, 2], F32)
            tmpc = [AR.alloc([512], F32) for _ in range(2)]
            acc = [AR.alloc([512], F32) for _ in range(2)]
            cv = [AR.alloc([512], F32) for _ in range(2)]
            for j in range(3):
                for k0 in range(0, 8, 4):
                    P.dma("pool", w_in[:, k0:k0 + 4, j * D:(j + 1) * D], w_in_c[ic, :, k0:k0 + 4, j * D:(j + 1) * D])
            for k0 in range(0, 8, 4):
                P.dma("pool", w_out[:, k0:k0 + 4, :], w_out_c[ic, :, k0:k0 + 4, :])
            P.dma("sp", cw, sc_cw[ic])
            P.dma("sp", stage, st_sc[ic])
            P.memset("pool", ext_p[:, :, 0, 0:2], 0.0)
            for c in range(8):
                P.copy("pool", ext_s[:, c, :, 0:2], stage[:, c])
            wn = load_norm_w(norm_mix[l:l + 1, :])
            tasks = [(tile, c) for tile in prompt_tiles(4) + [sample_tile] for c in range(8)]

            def CA(i):
                tile, c = tasks[i]
                T, L, nseq = tile.T, tile.L, tile.nseq
                ext = ext_s if tile.sample else ext_p
                if c == 0:
                    norm_blocks(range(tile.b0, tile.b0 + tile.nblk), wn, hT, 0, [ps[6], ps[7]])
                pb, pc, ph = ps[(i % 2) * 3], ps[(i % 2) * 3 + 1], ps[(i % 2) * 3 + 2]
                for j, bank in enumerate((pb, pc, ph)):
                    for k in range(8):
                        P.mm(bank[:, :T], w_in[:, k, j * D + c * 128: j * D + (c + 1) * 128], hT[:, k, :T],
                             start=(k == 0), stop=(k == 7))
                v3 = lambda a: a.rearrange("p (s l) -> p s l", s=nseq)
                t = tmpc[i % 2]
                P.copy("act", t[:, :T], pc[:, :T])
                P.tt("dve", ext[:, c, :, 2:2 + L], v3(t[:, :T]), v3(ph[:, :T]), ALU.mult)
                conv_tap0(ext[:, c], cw[:, c, :], L, v3(acc[i % 2][:, :T]))

            def CB(i):
                tile, c = tasks[i]
                T, L, nseq = tile.T, tile.L, tile.nseq
                ext = ext_s if tile.sample else ext_p
                pb = ps[(i % 2) * 3]
                v3 = lambda a: a.rearrange("p (s l) -> p s l", s=nseq)
                conv_rest(v3(cv[i % 2][:, :T]), ext[:, c], cw[:, c, :], 3, L, v3(acc[i % 2][:, :T]))
                P.copy("pool", ext[:, c, :, 0:2], ext[:, c, :, L:L + 2])
                P.tt("dve", m[:, c, :T], cv[i % 2][:, :T], pb[:, :T], ALU.mult)
                if c == 7:
                    out_proj_accum(tile, lambda k, blk: m[:, k, blk * 128:(blk + 1) * 128],
                                   8, lambda k, half: w_out[:, k, half * 512:(half + 1) * 512], [ps[6], ps[7]])
                    if tile.last:
                        if tile.sample:
                            for c2 in range(8):
                                P.copy("pool", stage[:, c2], ext_s[:, c2, :, 0:2])
                            P.dma("sp", o_sc_s[ic], stage)
                        else:
                            stp = AR.alloc([8, 2], F32)
                            P.copy("pool", stp, ext_p[:, :, 0, 0:2])
                            P.dma("sp", o_sc_p[ic], stp)
            skewed(len(tasks), CA, CB)
            AR.release()

        def mixer_ab(l, ia):
            AR.mark()
            A = AR.alloc
            w_in = A([8, IN_AB], BF16)
            wo_ring = [A([D], BF16) for _ in range(4)]
            poolw = A([4, 128], BF16)
            hT = A([8, 128], BF16)
            wnb = A([D], F32)
            pscale = A([4], F32)
            dcw = A([12, 4], F32)
            nrmw = A([1], F32)
            alog = A([4], F32)
            dtb = A([4], F32)
            negA = A([4], F32)
            lnq = A([1], F32)
            WPS = NSS * (15 + LS)
            off_pooltmp = AR.top
            ebp = [A([WPS], F32) for _ in range(2)]
            T1 = A([WPS], F32)
            T2 = A([WPS], F32)
            pfx_pool = A([4, NSS, 15], F32)
            yp = [A([128], BF16) for _ in range(2)]
            t16 = A([16], F32)
            WQS = NSS * (3 + LS)
            ebq = [A([WQS], F32) for _ in range(2)]
            pfx_q = A([12, NSS, 3], F32)
            cvb = [A([128], F32) for _ in range(2)]
            accb = [A([128], F32) for _ in range(2)]
            qk32 = A([8, 128], F32)
            sq = A([8, 128], F32, at=off_pooltmp)
            rs = A([8, 128], F32)
            HS = [(A([8, 128], BF16), A([4, 128], BF16), A([4, 128], BF16), A([4, 128], BF16), A([4], F32), A([4], F32))
                  for _ in range(2)]
            ydn = A([4, 128], BF16)
            tsm = A([8], F32)
            gcc = A([8], F32)
            gam = A([4], F32)
            dlt = A([4], F32)
            bgm = A([4], F32)
            X1 = A([4, 128], F32)
            X2 = A([4, 128], F32)
            Eb = A([4, 128], F32)
            grow = A([4, 128], F32)
            P32 = A([4, 128], F32)
            Nb = A([4, 128], BF16)
            NTb = A([4, 128], BF16)
            Pb = A([4, 128], BF16)
            A2b = [A([4, 128], BF16) for _ in range(2)]
            A2Tb = [A([4, 128], BF16) for _ in range(2)]
            NTbd = A([4, 128], BF16)
            lvl_masks = [A([128], BF16) for _ in range(5)]
            aqkT = A([4, 128], BF16)
            qgT = A([4, 128], BF16)
            bgk = A([4, 128], BF16)
            kd = A([4, 128], BF16)
            bv = A([4, 128], BF16)
            negwT = A([4, 128], BF16)
            utT = A([4, 128], BF16)
            utk = A([4, 128], BF16)
            kdm = [A([4, 128], BF16) for _ in range(2)]
            S32p = A([4, 128], F32)
            Sbp = A([4, 128], BF16)
            S32s = [A([4, 128], F32) for _ in range(2)]
            Sbs = [A([4, 128], BF16) for _ in range(2)]
            msk = [[A([128], F32) for _ in range(4)] for _ in range(2)]
            onehot = A([NSS], F32)
            invcnt = A([4, 16], F32)
            for s_ in range(2):
                for m_ in range(4):
                    P.dma("sp", msk[s_][m_], masks_in[s_, m_])
            P.dma("sp", onehot, onehot_in)
            for m_ in range(5):
                P.dma("pool", lvl_masks[m_], lvlmask_in[m_])
            P.dma("sp", invcnt, invcnt_in)
            ONES = msk[0][1]
            for (c0, c1) in ((0, 1024), (1024, 2048), (2048, IN_AB)):
                for k0 in (0, 4):
                    P.dma("pool", w_in[:, k0:k0 + 4, c0:c1], w_in_ab[ia, :, k0:k0 + 4, c0:c1])
            P.dma("pool", poolw, pool_w_in[ia])
            P.dma("sp", wnb, norm_mix[l:l + 1, :].broadcast_to([128, D]))
            P.dma("sp", pscale, pool_scale_in[ia])
            P.dma("sp", dcw, dn_cw_in[ia])
            P.dma("sp", nrmw, dn_nw_in[ia])
            P.dma("sp", alog, a_log_in[ia:ia + 1, :].broadcast_to([128, 4]))
            P.dma("sp", dtb, dt_bias_in[ia:ia + 1, :].broadcast_to([128, 4]))
            P.act(negA, alog, AF.Exp)
            P.ts("dve", negA, negA, -1.0, ALU.mult)
            P.memset("dve", lnq, float(np.log(128.0 ** -0.5)))
            P.memset("pool", S32p, 0.0)
            P.memset("pool", Sbp, 0.0)
            tiles = [sample_tile] + prompt_tiles(1)
            bc4 = lambda ap_: ap_.unsqueeze(1).broadcast_to([128, 4, 128])
            col4 = lambda ap_: ap_.unsqueeze(2).broadcast_to([128, 4, 128])
            r4 = lambda bank: bank[:, :].rearrange("p (a b) -> p a b", a=4)
            b4 = lambda bank, half: bank[:].bitcast(BF16)[:, half * 512:(half + 1) * 512].rearrange(
                "p (a b) -> p a b", a=4)
            T = 128

            def front_steps(tile, H):
                steps = []
                step = steps.append
                L, nseq = tile.L, tile.nseq
                v3 = lambda ap_: ap_.rearrange("p (s l) -> p s l", s=nseq)
                qkT, vT, sz, ypool, beta, gg = H

                def inproj(bank, col0):
                    for k in range(8):
                        P.mm(bank[:, :T], w_in[:, k, col0:col0 + 128], hT[:, k, :], start=(k == 0), stop=(k == 7))

                def s_norm():
                    if tile.sample:
                        P.dma("sp", pfx_pool, st_pool[ia])
                        P.dma("sp", pfx_q, st_dnc[ia])
                    elif tile.first:
                        P.dma("sp", o_pool_s[ia], pfx_pool)
                        P.dma("sp", o_dnc_s[ia], pfx_q)
                        P.memset("pool", pfx_pool[:, :, 0, :], 0.0)
                        P.memset("pool", pfx_q[:, :, 0, :], 0.0)
                    norm_blocks([tile.b0], wnb, hT, 0, [ps[7]])
                step(s_norm)
                Wp = 15 + L

                def s_pool(g):
                    bank = ps[5 + g % 2]
                    inproj(bank, g * 128)
                    eb = ebp[g % 2][:, 0:nseq * Wp].rearrange("p (s w) -> p s w", s=nseq)
                    t1 = T1[:, 0:nseq * Wp].rearrange("p (s w) -> p s w", s=nseq)
                    t2 = T2[:, 0:nseq * Wp].rearrange("p (s w) -> p s w", s=nseq)
                    P.copy("pool", eb[:, :, 0:15], pfx_pool[:, g, 0:nseq, :])
                    P.copy("act", eb[:, :, 15:Wp], v3(bank[:, :T]))
                    P.tt("dve", t1[:, :, 1:Wp], eb[:, :, 1:Wp], eb[:, :, 0:Wp - 1], ALU.add)
                    res = t1
                    if g >= 1:
                        P.tt("dve", t2[:, :, 3:Wp], t1[:, :, 3:Wp], t1[:, :, 1:Wp - 2], ALU.add)
                        res = t2
                    if g >= 2:
                        P.tt("dve", t1[:, :, 7:Wp], t2[:, :, 7:Wp], t2[:, :, 3:Wp - 4], ALU.add)
                        res = t1
                    if g >= 3:
                        P.tt("dve", t2[:, :, 15:Wp], t1[:, :, 15:Wp], t1[:, :, 7:Wp - 8], ALU.add)
                        res = t2
                    y = yp[g % 2]
                    P.stt(v3(y), res[:, :, 15:Wp], 1.0 / POOL_WINDOWS[g], eb[:, :, 15:Wp], ALU.mult, ALU.subtract)
                    if tile.first and not tile.sample:
                        P.tt("dve", t16, res[:, 0, 15:31], invcnt[:, g, :], ALU.mult)
                        P.tt("dve", y[:, 0:16], t16, eb[:, 0, 15:31], ALU.subtract)
                    P.copy("pool", pfx_pool[:, g, 0:nseq, :], eb[:, :, L:L + 15])
                    P.mm(ps[7][:, :T], poolw[:, g, :], y)
                    P.act(ypool[:, g, :], ps[7][:, :T], AF.Identity, scale=pscale[:, g:g + 1])
                for g in range(4):
                    step(lambda g=g: s_pool(g))
                Wq = 3 + L

                def s_qkv_a(c):
                    bank = ps[5 + c % 2]
                    inproj(bank, 512 + c * 128)
                    eb = ebq[c % 2][:, 0:nseq * Wq].rearrange("p (s w) -> p s w", s=nseq)
                    P.copy("pool", eb[:, :, 0:3], pfx_q[:, c, 0:nseq, :])
                    P.copy("act", eb[:, :, 3:Wq], v3(bank[:, :T]))
                    conv_tap0(eb, dcw[:, c, :], L, v3(accb[c % 2]))

                def s_qkv_b(c):
                    eb = ebq[c % 2][:, 0:nseq * Wq].rearrange("p (s w) -> p s w", s=nseq)
                    cv = cvb[c % 2]
                    conv_rest(v3(cv), eb, dcw[:, c, :], 4, L, v3(accb[c % 2]))
                    P.copy("pool", pfx_q[:, c, 0:nseq, :], eb[:, :, L:L + 3])
                    if c < 8:
                        P.act(qk32[:, c, :], cv, AF.Silu)
                    else:
                        P.act(vT[:, c - 8, :], cv, AF.Silu)
                step(lambda: s_qkv_a(0))
                for c in range(12):
                    if c + 1 < 12:
                        step(lambda c=c: (s_qkv_a(c + 1), s_qkv_b(c)))
                    else:
                        step(lambda c=c: s_qkv_b(c))

                def s_z(c):
                    bank = ps[5 + c % 2]
                    inproj(bank, 2048 + c * 128)
                    P.act(sz[:, c, :], bank[:, :T], AF.Silu)
                for c in range(4):
                    step(lambda c=c: s_z(c))

                def s_ba():
                    bank = ps[7]
                    for k in range(8):
                        P.mm(bank[:, 0:8], hT[:, k, :], w_in[:, k, 2560:2568], start=(k == 0), stop=(k == 7))
                    P.copy("dve", tsm, bank[:, 0:8])
                    P.act(beta, tsm[:, 0:4], AF.Exp, scale=-1.0)
                    P.ts("dve", beta, beta, 1.0, ALU.add)
                    P.rec("dve", lambda e, o=beta: e.reciprocal(o, o), reads=[beta], writes=[beta])
                    P.tt("dve", gg, tsm[:, 4:8], dtb, ALU.add)
                    P.act(gg, gg, AF.Exp)
                    P.act(gg, gg, AF.Ln, bias=1.0, scale=1.0)
                    P.tt("dve", gg, gg, negA, ALU.mult)
                step(s_ba)

                def s_l2():
                    P.act(sq, qk32, AF.Square)
                    sqf = sq.rearrange("p a b -> p (a b)")
                    rsf = rs.rearrange("p a b -> p (a b)")
                    for hh in range(2):
                        P.mm(ps[5 + hh][:, :], ONES, sqf[:, hh * 512:(hh + 1) * 512])
                        P.act(rsf[:, hh * 512:(hh + 1) * 512], ps[5 + hh][:, :], AF.Ln, bias=eps_col[:], scale=1.0)
                    P.act(rsf[:, 0:512], rsf[:, 0:512], AF.Exp, scale=-0.5, bias=lnq)
                    P.act(rsf[:, 512:1024], rsf[:, 512:1024], AF.Exp, scale=-0.5)
                    P.tt("dve", qkT, qk32, rs, ALU.mult)
                step(s_l2)
                return steps

            def back_steps(tile, H):
                steps = []
                step = steps.append
                qkT, vT, sz, ypool, beta, gg = H
                ms = msk[1 if tile.sample else 0]
                UCUM, SSM, NEG, STRICT = ms
                gcrow = r4(ps[0])
                brow = r4(ps[2])
                KK = r4(ps[3])
                QK = r4(ps[4])

                def s_prep1():
                    for k in range(4):
                        P.dma("pool", wo_ring[k], w_out_ab[ia, :, k, :])
                    P.tt("dve", X1, bc4(UCUM), col4(gg), ALU.mult)
                    P.mm(ps[0][:, :], ONES, X1.rearrange("p a b -> p (a b)"))
                    P.mm(ps[1][:, 0:4], UCUM, gg)
                    P.mm(ps[1][:, 4:8], SSM, gg)
                    P.tt("dve", X2, bc4(ident_f[:]), col4(beta), ALU.mult)
                    P.mm(ps[2][:, :], ONES, X2.rearrange("p a b -> p (a b)"))
                    P.copy("dve", gcc, ps[1][:, 0:8])
                    P.act(gam, gcc[:, 0:4], AF.Exp)
                    P.tt("dve", dlt, gcc[:, 4:8], gcc[:, 0:4], ALU.subtract)
                    P.act(dlt, dlt, AF.Exp)
                    P.tt("dve", bgm, beta, gam, ALU.mult)
                step(s_prep1)

                def s_prep2():
                    for h in range(4):
                        P.ts("dve", X1[:, h, :], gcrow[:, h, :], gcc[:, h:h + 1], ALU.subtract, 0.0, ALU.min)
                    P.tt("dve", X1, X1, bc4(NEG), ALU.add)
                    P.act(Eb, X1, AF.Exp)
                    P.act(grow, gcrow, AF.Exp)
                    P.tt("dve", qgT, qkT[:, 0:4, :], grow, ALU.mult)
                    for h in range(4):
                        P.mm(KK[:, h, :], qkT[:, 4 + h, :], qkT[:, 4 + h, :])
                    for h in range(4):
                        P.mm(QK[:, h, :], qkT[:, 4 + h, :], qkT[:, h, :])
                step(s_prep2)

                def s_prep3():
                    P.tt("dve", aqkT, QK, Eb, ALU.mult)
                    P.tt("dve", X2, Eb, bc4(STRICT), ALU.mult)
                    P.tt("dve", X2, X2, brow, ALU.mult)
                    P.tt("dve", Nb, KK, X2, ALU.mult)
                    NTp = b4(ps[1], 0)
                    for h in range(4):
                        P.transpose(NTp[:, h, :], Nb[:, h, :], ident_b[:])
                    P.copy("act", NTb, NTp)
                    if tile.sample:
                        P.stt(P32, Nb, -1.0, bc4(ident_f[:]), ALU.mult, ALU.add)
                    else:
                        P.tt("dve", A2b[1], Nb, bc4(lvl_masks[0]), ALU.mult)
                        P.tt("dve", NTbd, NTb, bc4(lvl_masks[0]), ALU.mult)
                        P.stt(P32, A2b[1], -1.0, bc4(ident_f[:]), ALU.mult, ALU.add)
                    P.copy("act", Pb, P32)
                step(s_prep3)
                A2T_ps = r4(ps[3])
                A2_ps = r4(ps[4])
                PP_ps = r4(ps[1])

                def s_base(m_):
                    if tile.sample:
                        A0, AT0 = Nb, NTb
                    else:
                        A0, AT0 = A2b[1], NTbd
                    Acur, ATcur = (A0, AT0) if m_ == 0 else (A2b[0], A2Tb[0])
                    lastl = (m_ == 1)
                    a2t = A2Tb[m_]
                    for h in range(4):
                        P.mm(A2T_ps[:, h, :], Acur[:, h, :], ATcur[:, h, :])
                    if not lastl:
                        for h in range(4):
                            P.mm(A2_ps[:, h, :], ATcur[:, h, :], Acur[:, h, :])
                    P.copy("act", a2t, A2T_ps)
                    if not lastl:
                        P.copy("dve", A2b[0], A2_ps)
                    for h in range(4):
                        P.mm(PP_ps[:, h, :], a2t[:, h, :], Pb[:, h, :])
                    P.tt("dve", P32, P32, PP_ps, ALU.add)
                    P.copy("act", Pb, P32)
                step(lambda: s_base(0))
                step(lambda: s_base(1))

                def s_dbl(lv):
                    Zb, tmpb, UTb = A2b[0], A2Tb[0], A2Tb[1]
                    UT_ps = b4(ps[1], 0)
                    for h in range(4):
                        P.transpose(UT_ps[:, h, :], Pb[:, h, :], ident_b[:])
                    for h in range(4):
                        P.mm(A2T_ps[:, h, :], NTb[:, h, :], Pb[:, h, :])
                    P.copy("act", UTb, UT_ps)
                    P.copy("dve", Zb, A2T_ps)
                    for h in range(4):
                        P.mm(A2_ps[:, h, :], UTb[:, h, :], Zb[:, h, :])
                    P.tt("dve", tmpb, A2_ps, bc4(lvl_masks[1 + lv]), ALU.mult)
                    P.tt("dve", Pb, Pb, tmpb, ALU.add)
                if not tile.sample:
                    for lv in range(4):
                        step(lambda lv=lv: s_dbl(lv))
                wT_ps = r4(ps[2])

                def s_ktok():
                    ktp = b4(ps[2], 0)
                    vtp = b4(ps[2], 1)
                    for h in range(4):
                        P.transpose(ktp[:, h, :], qkT[:, 4 + h, :], ident_b[:])
                    for h in range(4):
                        P.transpose(vtp[:, h, :], vT[:, h, :], ident_b[:])
                    P.tt("dve", bgk, ktp, col4(bgm), ALU.mult)
                    P.tt("dve", kd, ktp, col4(dlt), ALU.mult)
                    P.tt("dve", bv, vtp, col4(beta), ALU.mult)
                step(s_ktok)

                def s_w():
                    for h in range(4):
                        P.mm(wT_ps[:, h, :], bgk[:, h, :], Pb[:, h, :])
                    P.rec("act", lambda e: e.mul(negwT, wT_ps, -1.0), reads=[wT_ps], writes=[negwT])
                step(s_w)
                if tile.sample:
                    seqs = [(s_ * LS, (s_ + 1) * LS, s_) for s_ in range(NSS)]
                else:
                    seqs = [(0, 128, None)]
                uT_ps = r4(ps[0])
                oT_ps = r4(ps[3])
                Sn_ps = r4(ps[4])

                def s_u():
                    for h in range(4):
                        P.mm(uT_ps[:, h, :], bv[:, h, :], Pb[:, h, :], start=(h == 0), stop=False, skip=True)
                    for si, (c0, c1, sidx) in enumerate(seqs):
                        if sidx is None:
                            Sb = Sbp
                        else:
                            Sb = Sbs[si % 2]
                            P.dma("pool", Sb, st_dn[ia, sidx])
                        for h in range(4):
                            P.mm(uT_ps[:, h, c0:c1], Sb[:, h, :], negwT[:, h, c0:c1], start=False, stop=True, skip=True)
                    P.copy("act", utT, uT_ps)
                    utk_ps = b4(ps[1], 0)
                    for h in range(4):
                        P.transpose(utk_ps[:, h, :], utT[:, h, :], ident_b[:])
                    P.copy("dve", utk, utk_ps)
                step(s_u)

                def s_o():
                    for h in range(4):
                        P.mm(oT_ps[:, h, :], utk[:, h, :], aqkT[:, h, :], start=(h == 0), stop=False, skip=True)
                    for si, (c0, c1, sidx) in enumerate(seqs):
                        if sidx is None:
                            Sb, S32, kdu = Sbp, S32p, kd
                        else:
                            Sb, S32, kdu = Sbs[si % 2], S32s[si % 2], kdm[si % 2]
                            P.dma("pool", Sb, st_dn[ia, sidx])
                            P.dma("sp", S32, st_dn[ia, sidx])
                            P.ts("dve", kdu, kd, onehot[:, sidx:sidx + 1], ALU.mult)
                        for h in range(4):
                            P.mm(oT_ps[:, h, c0:c1], Sb[:, h, :], qgT[:, h, c0:c1], start=False, stop=True, skip=True)
                        for h in range(4):
                            P.mm(Sn_ps[:, h, :], kdu[:, h, :], utk[:, h, :])
                        for h in range(4):
                            P.stt(S32[:, h, :], S32[:, h, :], grow[:, h, c1 - 1:c1], Sn_ps[:, h, :], ALU.mult, ALU.add)
                        if sidx is None:
                            P.copy("act", Sbp, S32p)
                            if tile.last:
                                P.dma("sp", o_dn_p[ia], S32p)
                        else:
                            P.dma("sp", o_dn_s[ia, sidx], S32)
                step(s_o)

                def s_out():
                    P.act(X1, oT_ps, AF.Square)
                    P.mm(ps[1][:, :], ONES, X1.rearrange("p a b -> p (a b)"))
                    X2f = X2.rearrange("p a b -> p (a b)")
                    P.act(X2f, ps[1][:, :], AF.Ln, bias=eps_col[:], scale=1.0 / 128.0)
                    P.act(X2f, X2f, AF.Exp, scale=-0.5)
                    P.tt("dve", X1, oT_ps, X2, ALU.mult)
                    P.stt(ydn, X1, nrmw[:, 0:1], sz, ALU.mult, ALU.mult)
                step(s_out)

                def s_proj():
                    for k in range(8):
                        src = ypool if k < 4 else ydn
                        for half in range(2):
                            P.mm(ps[half * 4][:, :], src[:, k % 4, :], wo_ring[k % 4][:, half * 512:(half + 1) * 512],
                                 start=(k == 0), stop=(k == 7))
                        if k + 4 < 8:
                            P.dma("pool", wo_ring[k % 4], w_out_ab[ia, :, k + 4, :])
                    for half in range(2):
                        xv = xs[:, tile.b0, half * 512:(half + 1) * 512]
                        P.tt("dve", xv, xv, ps[half * 4][:, :], ALU.add)
                step(s_proj)
                return steps

            def interleave(a, b):
                out, ia_, ib_ = [], 0, 0
                na, nb = len(a), len(b)
                while ia_ < na or ib_ < nb:
                    if ib_ >= nb or (ia_ < na and ia_ * nb <= ib_ * na):
                        out.append(a[ia_]); ia_ += 1
                    else:
                        out.append(b[ib_]); ib_ += 1
                return out

            pend = []
            for ti, tile in enumerate(tiles):
                fs = front_steps(tile, HS[ti % 2])
                for st_ in interleave(fs, pend):
                    st_()
                pend = back_steps(tile, HS[ti % 2])
            for st_ in pend:
                st_()
            P.dma("sp", o_pool_p[ia], pfx_pool[:, :, 0, :])
            P.dma("sp", o_dnc_p[ia], pfx_q[:, :, 0, :])
            AR.release()

        GS = 4

        def ffn(l):
            AR.mark()
            slots = []
            for s in range(2):
                slots.append((AR.alloc([8, 2, GS * 128], BF16), AR.alloc([GS, D], BF16)))
            hTall = AR.alloc([8, NTOK], BF16)
            cw = AR.alloc([NFC, 3], F32)
            a_sb = [AR.alloc([GS, 512], BF16) for _ in range(2)]
            ext_p = AR.alloc([GS, 1, 2 + 512], F32)
            ext_s = AR.alloc([GS, NSS, 2 + LS], F32)
            stage_s = [AR.alloc([GS, NSS, 2], F32) for _ in range(2)]
            stage_o = [AR.alloc([GS, NSS, 2], F32) for _ in range(2)]
            stage_p = [AR.alloc([GS, 2], F32) for _ in range(2)]
            acc = [AR.alloc([512], F32) for _ in range(3)]
            cv = [AR.alloc([512], F32) for _ in range(3)]
            P.dma("sp", cw, ffn_cw[l])
            wn = load_norm_w(norm_ffn[l:l + 1, :])
            norm_blocks(range(NB), wn, hTall, 0, [ps[6], ps[7]])
            groups = [(g0, min(GS, NFC - g0)) for g0 in range(0, NFC, GS)]

            def load_group(gi):
                g0, gs = groups[gi]
                wu, wd = slots[gi % 2]
                for j in range(2):
                    for k0 in range(0, 8, 4):
                        P.dma("pool", wu[:, k0:k0 + 4, j, 0:gs * 128],
                              w_up[l, :, k0:k0 + 4, j * DFF + g0 * 128: j * DFF + (g0 + gs) * 128])
                P.dma("pool", wd[:, 0:gs, :], w_down[l, :, g0:g0 + gs, :])

            load_group(0)
            tiles = prompt_tiles(4) + [sample_tile]
            ai = 0
            for gi, (g0, gs) in enumerate(groups):
                if gi + 1 < len(groups):
                    load_group(gi + 1)
                wu, wd = slots[gi % 2]
                sts = stage_s[gi % 2]
                P.dma("sp", sts[:, 0:gs], st_ffn[l, :, g0:g0 + gs])
                P.memset("pool", ext_p[:, :, 0, 0:2], 0.0)
                for c in range(gs):
                    P.copy("pool", ext_s[:, c, :, 0:2], sts[:, c])
                tasks = []
                for tile in tiles:
                    a = a_sb[ai % 2]
                    ai += 1
                    for c in range(gs):
                        tasks.append((tile, c, a))

                def FA(i, tasks=tasks, wu=wu, g0=g0):
                    tile, c, a = tasks[i]
                    T, L, nseq = tile.T, tile.L, tile.nseq
                    ext = ext_s if tile.sample else ext_p
                    tok0 = tile.b0 * 128
                    v3 = lambda ap_: ap_.rearrange("p (s l) -> p s l", s=nseq)
                    pg, pv = ps[(i % 3) * 2], ps[(i % 3) * 2 + 1]
                    for j, bank in enumerate((pg, pv)):
                        for k in range(8):
                            P.mm(bank[:, :T], wu[:, k, j, c * 128:(c + 1) * 128], hTall[:, k, tok0:tok0 + T],
                                 start=(k == 0), stop=(k == 7))
                    P.copy("act", ext[:, c, :, 2:2 + L], v3(pg[:, :T]))
                    conv_tap0(ext[:, c], cw[:, g0 + c, :], L, v3(acc[i % 3][:, :T]))

                def FB(i, tasks=tasks, wd=wd, g0=g0, gs=gs, gi=gi):
                    tile, c, a = tasks[i]
                    T, L, nseq = tile.T, tile.L, tile.nseq
                    ext = ext_s if tile.sample else ext_p
                    v3 = lambda ap_: ap_.rearrange("p (s l) -> p s l", s=nseq)
                    pv = ps[(i % 3) * 2 + 1]
                    conv_rest(v3(cv[i % 3][:, :T]), ext[:, c], cw[:, g0 + c, :], 3, L, v3(acc[i % 3][:, :T]))
                    P.copy("pool", ext[:, c, :, 0:2], ext[:, c, :, L:L + 2])
                    P.act(cv[i % 3][:, :T], cv[i % 3][:, :T], AF.Silu)
                    P.tt("dve", a[:, c, :T], cv[i % 3][:, :T], pv[:, :T], ALU.mult)
                    if c == gs - 1:
                        out_proj_accum(tile, lambda k, blk: a[:, k, blk * 128:(blk + 1) * 128],
                                       gs, lambda k, half: wd[:, k, half * 512:(half + 1) * 512],
                                       [ps[6], ps[7]])
                        if tile.last:
                            if tile.sample:
                                so = stage_o[gi % 2]
                                for c2 in range(gs):
                                    P.copy("pool", so[:, c2], ext_s[:, c2, :, 0:2])
                                P.dma("sp", o_ffn_s[l, :, g0:g0 + gs], so[:, 0:gs])
                            else:
                                sp_ = stage_p[gi % 2]
                                P.copy("pool", sp_[:, 0:gs], ext_p[:, 0:gs, 0, 0:2])
                                P.dma("sp", o_ffn_p[l, :, g0:g0 + gs, :], sp_[:, 0:gs])
                skewed(len(tasks), FA, FB)
            AR.release()

        ic = 0
        ia = 0
        for l, typ in enumerate(layers):
            if typ == "c":
                mixer_c(l, ic)
                ic += 1
            else:
                mixer_ab(l, ia)
                ia += 1
            ffn(l)

        AR.mark()
        wn = load_norm_w(norm_final[0:1, :])
        yo = [AR.alloc([D], F32) for _ in range(2)]
        for b in range(NB):
            y = yo[b % 2]
            P.act(y, xs[:, b, :], AF.Square, accum_out=ss[:, b:b + 1])
            P.act(rstd[:, b:b + 1], ss[:, b:b + 1], AF.Ln, bias=eps_col[:], scale=1.0 / D)
            P.act(rstd[:, b:b + 1], rstd[:, b:b + 1], AF.Exp, scale=-0.5)
            P.stt(y, xs[:, b, :], rstd[:, b:b + 1], wn[:], ALU.mult, ALU.mult)
            if b < NPB:
                P.dma("sp" if b % 2 == 0 else "act", y_p[b * 128:(b + 1) * 128, :], y)
            else:
                P.dma("sp", y_s, y)
        AR.release()
        P.emit()
    P.arena_bytes = ARB
    return nc, P


def _pkn(w):
    sh = w.shape
    k = sh[-2] // 128
    w = w.reshape(sh[:-2] + (k, 128, sh[-1]))
    return np.ascontiguousarray(np.swapaxes(w, -3, -2))


def _chan(v, nchunk):
    sh = v.shape
    v = v.reshape(sh[:-1] + (nchunk, 128))
    return np.ascontiguousarray(np.moveaxis(np.moveaxis(v, -1, -3), -1, -2))


_CACHE = {}


def run_model(inputs, SEQ, layers):
    key = (SEQ, tuple(layers))
    if key not in _CACHE:
        _CACHE[key] = build_program(SEQ, layers)
    nc, P = _CACHE[key]
    DEPTH = len(layers)
    f = lambda a: np.ascontiguousarray(np.asarray(a, dtype=np.float32))
    shared = {
        "norm_mix": f(inputs["norm_mix"]), "norm_ffn": f(inputs["norm_ffn"]),
        "norm_final": f(inputs["norm_final"]).reshape(1, D),
        "w_up": _pkn(f(inputs["w_up"])), "w_down": _pkn(f(inputs["w_down"])),
        "ffn_cw": _chan(f(inputs["ffn_conv_w"]), NFC),
        "ident": np.eye(128, dtype=np.float32),
    }
    N_C = sum(1 for t in layers if t == "c")
    N_AB = sum(1 for t in layers if t == "ab")
    if N_AB:
        shared["w_in_ab"] = _pkn(f(inputs["w_in_ab"]))
        shared["w_out_ab"] = _pkn(f(inputs["w_out_ab"]))
        shared["pool_w"] = np.ascontiguousarray(f(inputs["pool_w"]).transpose(0, 2, 1, 3))
        shared["pool_scale"] = np.ascontiguousarray(f(inputs["pool_scale"]).reshape(N_AB, 4, 128).transpose(0, 2, 1))
        shared["dn_cw"] = _chan(f(inputs["dn_conv_w"]), 12)
        shared["dn_nw"] = f(inputs["dn_norm_w"]).reshape(N_AB, 128, 1)
        shared["a_log"] = f(inputs["dn_a_log"])
        shared["dt_bias"] = f(inputs["dn_dt_bias"])
        idx = np.arange(128)
        k_ = idx[:, None]
        j_ = idx[None, :]
        msk = np.zeros((2, 4, 128, 128), np.float32)
        for si, same in enumerate((np.ones((128, 128), bool), (k_ // LS) == (j_ // LS))):
            msk[si, 0] = (same & (k_ <= j_))
            msk[si, 1] = same
            msk[si, 2] = np.where(same & (j_ >= k_), 0.0, -BIG)
            msk[si, 3] = (same & (j_ > k_))
        shared["masks"] = msk
        lm = np.zeros((5, 128, 128), np.float32)
        lm[0] = ((k_ // 8) == (j_ // 8))
        for li, sz_ in enumerate((8, 16, 32, 64)):
            lm[1 + li] = -1.0 * (((k_ // (2 * sz_)) == (j_ // (2 * sz_))) & ((k_ % (2 * sz_)) < sz_) & ((j_ % (2 * sz_)) >= sz_))
        shared["lvlmask"] = lm
        shared["onehot"] = ((idx[:, None] // LS) == np.arange(NSS)[None, :]).astype(np.float32)
        t_ = np.arange(16)
        ic_ = np.stack([1.0 / np.minimum(t_ + 1, w) for w in POOL_WINDOWS]).astype(np.float32)
        shared["invcnt"] = np.ascontiguousarray(np.broadcast_to(ic_[None], (128, 4, 16)))
    if N_C:
        shared["w_in_c"] = _pkn(f(inputs["w_in_c"]))
        shared["w_out_c"] = _pkn(f(inputs["w_out_c"]))
        shared["sc_cw"] = _chan(f(inputs["sc_conv_w"]), 8)
    xp = f(inputs["x_prompt"])
    xsm = f(inputs["x_sample"])
    st_ffn = f(inputs["state_ffn_conv"])
    st_sc = f(inputs["state_sconv"]) if N_C else None
    in_maps = []
    for c in range(NCORES):
        m = dict(shared)
        m["x_p"] = xp[c]
        m["x_s"] = xsm[c * NSS:(c + 1) * NSS].reshape(NSS * LS, D)
        s = st_ffn[:, c * NSS:(c + 1) * NSS]
        s = s.reshape(DEPTH, NSS, 2, NFC, 128)
        m["st_ffn"] = np.ascontiguousarray(s.transpose(0, 4, 3, 1, 2))
        if N_C:
            s = st_sc[:, c * NSS:(c + 1) * NSS].reshape(N_C, NSS, 2, 8, 128)
            m["st_sc"] = np.ascontiguousarray(s.transpose(0, 4, 3, 1, 2))
        if N_AB:
            s = f(inputs["state_pool"])[:, c * NSS:(c + 1) * NSS].reshape(N_AB, NSS, 15, 4, 128)
            m["st_pool"] = np.ascontiguousarray(s.transpose(0, 4, 3, 1, 2))
            s = f(inputs["state_dn_conv"])[:, c * NSS:(c + 1) * NSS].reshape(N_AB, NSS, 3, 12, 128)
            m["st_dnc"] = np.ascontiguousarray(s.transpose(0, 4, 3, 1, 2))
            s = f(inputs["state_dn"])[:, c * NSS:(c + 1) * NSS]
            m["st_dn"] = np.ascontiguousarray(s.transpose(0, 1, 3, 2, 4))
        in_maps.append(m)
    res = run_bass_kernel_spmd(nc, in_maps, core_ids=list(range(NCORES)))
    R = res.results
    out = {}
    out["y_prompt"] = np.stack([R[c]["y_p"] for c in range(NCORES)])
    out["y_sample"] = np.concatenate([R[c]["y_s"].reshape(NSS, LS, D) for c in range(NCORES)])
    out["new_ffnconv_p"] = np.stack([R[c]["o_ffn_p"].transpose(0, 3, 2, 1).reshape(DEPTH, 2, DFF)
                                     for c in range(NCORES)], axis=1)
    out["new_ffnconv_s"] = np.concatenate(
        [R[c]["o_ffn_s"].transpose(0, 3, 4, 2, 1).reshape(DEPTH, NSS, 2, DFF) for c in range(NCORES)], axis=1)
    if N_AB:
        out["new_pool_p"] = np.stack([R[c]["o_pool_p"].transpose(0, 3, 2, 1).reshape(N_AB, 15, DPOOL)
                                      for c in range(NCORES)], axis=1)
        out["new_pool_s"] = np.concatenate(
            [R[c]["o_pool_s"].transpose(0, 3, 4, 2, 1).reshape(N_AB, NSS, 15, DPOOL) for c in range(NCORES)], axis=1)
        out["new_dnconv_p"] = np.stack([R[c]["o_dnc_p"].transpose(0, 3, 2, 1).reshape(N_AB, 3, DQKV)
                                        for c in range(NCORES)], axis=1)
        out["new_dnconv_s"] = np.concatenate(
            [R[c]["o_dnc_s"].transpose(0, 3, 4, 2, 1).reshape(N_AB, NSS, 3, DQKV) for c in range(NCORES)], axis=1)
        out["new_dn_p"] = np.stack([R[c]["o_dn_p"].transpose(0, 2, 1, 3) for c in range(NCORES)], axis=1)
        out["new_dn_s"] = np.concatenate([R[c]["o_dn_s"].transpose(0, 1, 3, 2, 4) for c in range(NCORES)], axis=1)
    if N_C:
        out["new_sconv_p"] = np.stack([R[c]["o_sc_p"].transpose(0, 3, 2, 1).reshape(N_C, 2, D)
                                       for c in range(NCORES)], axis=1)
        out["new_sconv_s"] = np.concatenate(
            [R[c]["o_sc_s"].transpose(0, 3, 4, 2, 1).reshape(N_C, NSS, 2, D) for c in range(NCORES)], axis=1)
    return out


def kernel(**inputs):
    out = run_model(inputs, 2048, ["ab", "c", "ab", "c"])
    names = ["y_prompt", "y_sample", "new_pool_p", "new_pool_s", "new_dnconv_p", "new_dnconv_s", "new_dn_p",
             "new_dn_s", "new_sconv_p", "new_sconv_s", "new_ffnconv_p", "new_ffnconv_s"]
    return tuple(np.ascontiguousarray(out[n], dtype=np.float32) for n in names)
```

```python
import contextlib
import numpy as np
import concourse.bass as bass
import concourse.mybir as mybir
from concourse.bass_utils import run_bass_kernel_spmd

F32 = mybir.dt.float32
BF16 = mybir.dt.bfloat16
AF = mybir.ActivationFunctionType
ALU = mybir.AluOpType

ENGS = ("pe", "act", "dve", "pool", "sp")
BLOCKATTR = {"pe": "tensor", "act": "scalar", "dve": "vector", "pool": "gpsimd", "sp": "sync"}


class Op:
    __slots__ = ("eng", "fn", "deps", "odeps", "cost", "gi", "fin", "signal", "semval", "idx", "is_dma", "dsem",
                 "dval", "qprev")

    def __init__(self, eng, fn, is_dma=False):
        self.eng = eng
        self.fn = fn
        self.deps = set()
        self.odeps = set()
        self.cost = 300.0
        self.gi = 0
        self.fin = 0.0
        self.signal = False
        self.semval = None
        self.idx = None
        self.is_dma = is_dma
        self.dsem = None
        self.dval = None
        self.qprev = None


def _isz(dt):
    return 2 if dt == BF16 else 4


def _box(ap):
    t = ap.tensor
    item = _isz(ap.dtype)
    pstep = 1
    for s in list(t.shape)[1:]:
        pstep *= int(s)
    off = int(ap.offset)
    dims = [(int(a), int(b)) for a, b in ap.ap]
    p0 = off // pstep
    f0 = off % pstep
    if dims and dims[0][0] == pstep:
        pn = dims[0][1]
        rest = dims[1:]
    elif dims and dims[0][0] == 0 and len(dims) > 1:
        pn = 1
        rest = dims[1:]
    else:
        pn = 1
        rest = dims
    ext = 0
    lo = 0
    for st, cnt in rest:
        if st >= 0:
            ext += st * (cnt - 1)
        else:
            lo += st * (cnt - 1)
    if type(t).__name__.startswith("PSum"):
        return (t.name, 0, 128, 0, 1 << 30)
    return (t.name, p0, p0 + pn, (f0 + lo) * item, (f0 + ext + 1) * item)


def _overlap(a, b):
    return a[1] < b[2] and b[1] < a[2] and a[3] < b[4] and b[3] < a[4]


def _covers(a, b):
    return a[1] <= b[1] and a[2] >= b[2] and a[3] <= b[3] and a[4] >= b[4]


def _is_dram(ap):
    return type(ap.tensor).__name__.startswith("DRam")


class Prog:
    NDSEM = 8

    def __init__(self, nc):
        self.nc = nc
        self.q = {e: [] for e in ENGS}
        self.acc = {}
        self.dma_ops = {e: [] for e in ENGS}
        self.nwaits = 0
        self.ngi = 0

    def rec(self, eng, fn, reads=(), writes=(), is_dma=False, cost=None):
        op = Op(eng, fn, is_dma=is_dma)
        op.idx = len(self.q[eng])
        op.gi = self.ngi
        self.ngi += 1
        if cost is None:
            n = 0
            for ap in list(writes)[:1]:
                n = 1
                for d_ in list(ap.shape)[1:]:
                    n *= int(d_)
            if is_dma:
                nb = n * 128 * 4
                cost = 2000.0 + nb / 80.0
            elif eng == "act":
                cost = 224.0 + 0.85 * n
            elif eng == "dve":
                cost = 200.0 + 0.95 * n
            elif eng == "pool":
                cost = 150.0 + 2.0 * n
            else:
                cost = 100.0
        op.cost = cost
        for ap in reads:
            if ap is None or _is_dram(ap):
                continue
            bx = _box(ap)
            lst = self.acc.setdefault(bx[0], [])
            for (b2, o2, w2) in lst:
                if w2 and o2 is not op and _overlap(bx, b2):
                    op.deps.add(o2)
            if not is_dma:
                for e in lst:
                    if (not e[2]) and e[0] == bx and e[1].eng == eng and not e[1].is_dma and e[1] is not op:
                        op.odeps.add(e[1])
                lst[:] = [e for e in lst if not ((not e[2]) and e[0] == bx and e[1].eng == eng and not e[1].is_dma)]
            lst.append((bx, op, False))
        psum_reads = [ap for ap in reads if ap is not None and type(ap.tensor).__name__.startswith("PSum")]
        for ap in list(writes) + psum_reads:
            if ap is None or _is_dram(ap):
                continue
            bx = _box(ap)
            lst = self.acc.setdefault(bx[0], [])
            keep = []
            for ent in lst:
                b2, o2, w2 = ent
                if o2 is op:
                    keep.append(ent)
                    continue
                if _overlap(bx, b2):
                    same_compute = (o2.eng == eng == "pe") and (not o2.is_dma) and (not is_dma)
                    if not same_compute:
                        op.deps.add(o2)
                    else:
                        op.odeps.add(o2)
                    if _covers(bx, b2):
                        continue
                keep.append(ent)
            keep.append((bx, op, True))
            self.acc[bx[0]] = keep
        self.q[eng].append(op)
        return op

    def schedule(self):
        import heapq
        allops = []
        for e in ENGS:
            allops.extend(self.q[e])
        allops.sort(key=lambda o: o.gi)
        succ = {}
        indeg = {}
        for o in allops:
            ds = o.deps | o.odeps
            indeg[o] = len(ds)
            for d in ds:
                succ.setdefault(d, []).append(o)
        import os as _os2
        HOP = float(_os2.environ.get("KHOP", "500"))
        CSC = float(_os2.environ.get("KCSC", "1.5"))
        MODE = _os2.environ.get("KMODE", "cp")
        tail = {}
        for o in reversed(allops):
            m_ = 0.0
            for c in succ.get(o, ()):
                lat = HOP if (c.eng != o.eng or o.is_dma) else 60.0
                v_ = lat + tail[c]
                if v_ > m_:
                    m_ = v_
            tail[o] = o.cost + m_
        est = {}
        free = {e: 0.0 for e in ENGS}
        newq = {e: [] for e in ENGS}
        pend = {e: [] for e in ENGS}
        avail = {e: [] for e in ENGS}
        for o in allops:
            if indeg[o] == 0:
                est[o] = 0.0
                heapq.heappush(pend[o.eng], (0.0, o.gi, o))
        INF = float("inf")
        nleft = len(allops)
        while nleft:
            best_e, best_t = None, INF
            for e in ENGS:
                pe_, av_ = pend[e], avail[e]
                while pe_ and pe_[0][0] <= free[e]:
                    t_, g_, o_ = heapq.heappop(pe_)
                    heapq.heappush(av_, ((-tail[o_]) if MODE == "cp" else t_, g_, o_))
                if av_:
                    t = free[e]
                elif pe_:
                    t = pe_[0][0]
                else:
                    continue
                if t < best_t:
                    best_e, best_t = e, t
            e = best_e
            if not avail[e]:
                t_, g_, o_ = heapq.heappop(pend[e])
                heapq.heappush(avail[e], ((-tail[o_]) if MODE == "cp" else t_, g_, o_))
            _, _, o = heapq.heappop(avail[e])
            start = max(est[o], free[e])
            if o.is_dma:
                free[e] = start + 60.0
                o.fin = start + o.cost
            else:
                o.fin = start + o.cost * (CSC if o.eng != "pe" else 1.0)
                free[e] = o.fin
            newq[e].append(o)
            nleft -= 1
            for c in succ.get(o, ()):
                indeg[c] -= 1
                lat = HOP if (c.eng != o.eng or o.is_dma) else 60.0
                if o in c.odeps and o not in c.deps:
                    tt_ = max(est.get(c, 0.0), start)
                else:
                    tt_ = max(est.get(c, 0.0), o.fin + lat)
                est[c] = tt_
                if indeg[c] == 0:
                    heapq.heappush(pend[c.eng], (tt_, c.gi, c))
        assert sum(len(v) for v in newq.values()) == len(allops)
        import os as _os
        keep = _os.environ.get("KEEP_ORDER", "").split(",")
        for e in ENGS:
            if e in keep or "all" in keep:
                newq[e] = sorted(newq[e], key=lambda o: o.gi)
        self.q = newq
        for e in ENGS:
            self.dma_ops[e] = []
            for i, o in enumerate(self.q[e]):
                o.idx = i
            for o in self.q[e]:
                if o.is_dma:
                    i = len(self.dma_ops[e])
                    o.dsem = i % self.NDSEM
                    o.dval = 16 * (i // self.NDSEM + 1)
                    o.qprev = self.dma_ops[e][i - self.NDSEM] if i >= self.NDSEM else None
                    self.dma_ops[e].append(o)

    def mm(self, out, lhsT, rhs, start=True, stop=True, skip=False):
        n = 1
        for d_ in list(rhs.shape)[1:]:
            n *= int(d_)
        cost = 64.0 + 0.52 * n
        if rhs.dtype == F32:
            cost *= 4
        if skip:
            return self.rec("pe", lambda e: e.matmul(out, lhsT, rhs, start=start, stop=stop, skip_group_check=True),
                            reads=[lhsT, rhs], writes=[out], cost=cost)
        return self.rec("pe", lambda e: e.matmul(out, lhsT, rhs, start=start, stop=stop),
                        reads=[lhsT, rhs], writes=[out], cost=cost)

    def transpose(self, out, in_, ident):
        return self.rec("pe", lambda e: e.transpose(out, in_, ident), reads=[in_, ident], writes=[out], cost=110.0)

    def act(self, out, in_, func, bias=None, scale=None, accum_out=None):
        kw = {}
        rd = [in_]
        wr = [out]
        if bias is not None:
            kw["bias"] = bias
            if not isinstance(bias, (int, float)):
                rd.append(bias)
        if scale is not None:
            kw["scale"] = scale
            if not isinstance(scale, (int, float)):
                rd.append(scale)
        if accum_out is not None:
            kw["accum_out"] = accum_out
            wr.append(accum_out)
        return self.rec("act", lambda e: e.activation(out, in_, func, **kw), reads=rd, writes=wr)

    def tt(self, eng, out, in0, in1, op):
        return self.rec(eng, lambda e: e.tensor_tensor(out, in0, in1, op), reads=[in0, in1], writes=[out])

    def ts(self, eng, out, in0, s1, op0, s2=None, op1=None):
        rd = [in0]
        for s in (s1, s2):
            if s is not None and not isinstance(s, (int, float)):
                rd.append(s)
        if op1 is None:
            return self.rec(eng, lambda e: e.tensor_scalar(out, in0, s1, None, op0), reads=rd, writes=[out])
        return self.rec(eng, lambda e: e.tensor_scalar(out, in0, s1, s2, op0, op1), reads=rd, writes=[out])

    def stt(self, out, in0, scalar, in1, op0, op1):
        rd = [in0, in1]
        if not isinstance(scalar, (int, float)):
            rd.append(scalar)
        return self.rec("dve", lambda e: e.scalar_tensor_tensor(out, in0, scalar, in1, op0, op1),
                        reads=rd, writes=[out])

    def copy(self, eng, out, in_):
        if eng == "act":
            return self.rec("act", lambda e: e.copy(out, in_), reads=[in_], writes=[out])
        return self.rec(eng, lambda e: e.tensor_copy(out, in_), reads=[in_], writes=[out])

    def memset(self, eng, ap, val):
        return self.rec(eng, lambda e: e.memset(ap, val), reads=[], writes=[ap])

    def dma(self, eng, out, in_, **kw):
        if out.dtype != in_.dtype:
            kw.setdefault("max_dma_last_dim", 4096)
        return self.rec(eng, lambda e: e.dma_start(out=out, in_=in_, **kw), reads=[in_], writes=[out], is_dma=True)

    def emit(self):
        nc = self.nc
        self.schedule()
        for e in ENGS:
            for op in self.q[e]:
                for d in op.deps:
                    if not d.is_dma:
                        d.signal = True
        for e in ENGS:
            c = 0
            for op in self.q[e]:
                if op.signal:
                    c += 1
                    op.semval = c
        with contextlib.ExitStack() as st:
            csem = {e: st.enter_context(nc.semaphore("s_" + e)) for e in ENGS}
            dsem = {e: [st.enter_context(nc.semaphore("d_%s%d" % (e, i))) for i in range(self.NDSEM)]
                    for e in ENGS if self.dma_ops[e]}
            block = st.enter_context(nc.Block())

            def make(ename):
                def body(eh):
                    waited = {}
                    for op in self.q[ename]:
                        need = {}
                        for d in op.deps:
                            if d.is_dma:
                                key = ("d", d.eng, d.dsem)
                                val = d.dval
                            else:
                                key = ("c", d.eng)
                                val = d.semval
                            if val > need.get(key, 0):
                                need[key] = val
                        if op.is_dma and op.qprev is not None:
                            key = ("d", op.eng, op.dsem)
                            val = op.qprev.dval
                            if val > need.get(key, 0):
                                need[key] = val
                        for key, val in need.items():
                            if waited.get(key, 0) >= val:
                                continue
                            waited[key] = val
                            sem = csem[key[1]] if key[0] == "c" else dsem[key[1]][key[2]]
                            eh.wait_ge(sem, val)
                            self.nwaits += 1
                        inst = op.fn(eh)
                        if op.is_dma:
                            inst.then_inc(dsem[op.eng][op.dsem], 16)
                        elif op.signal:
                            inst.then_inc(csem[op.eng], 1)
                    if ename == "sp":
                        for e2 in ENGS:
                            lst = self.dma_ops[e2]
                            if not lst:
                                continue
                            last = {}
                            for o in lst:
                                last[o.dsem] = o.dval
                            for s, v in last.items():
                                if waited.get(("d", e2, s), 0) < v:
                                    eh.wait_ge(dsem[e2][s], v)
                return body

            for ename in ENGS:
                getattr(block, BLOCKATTR[ename])(make(ename))


D = 1024
DFF = 2816
NFC = DFF // 128
IN_AB = 2568
DPOOL = 512
DQKV = 1536
NH = 4
NSS = 16
LS = 8
EPS = 1e-6
POOL_WINDOWS = (2, 4, 8, 16)
NCORES = 8
BIG = 30000.0


class Tile_:
    def __init__(self, b0, nblk, nseq, L, sample, first, last):
        self.b0, self.nblk, self.nseq, self.L = b0, nblk, nseq, L
        self.T = nblk * 128
        self.sample, self.first, self.last = sample, first, last


class Arena:
    def __init__(self, ar, nbytes):
        self.ar = ar
        self.nbytes = nbytes
        self.top = 0
        self.marks = []

    def alloc(self, shape, dt, at=None):
        n = 1
        for s in shape:
            n *= s
        nb = n * _isz(dt)
        nb_al = (nb + 63) // 64 * 64
        if at is None:
            off = self.top
            self.top += nb_al
        else:
            off = at
            assert off + nb_al <= self.top
        assert self.top <= self.nbytes, ("arena overflow", self.top, self.nbytes)
        v = self.ar[:, off // 2: off // 2 + nb // 2]
        if dt == F32:
            v = v.bitcast(F32)
        if len(shape) > 1:
            names = " ".join("d%d" % i for i in range(len(shape)))
            kw = {"d%d" % i: shape[i] for i in range(len(shape) - 1)}
            v = v.rearrange("p (%s) -> p %s" % (names, names), **kw)
        return v

    def mark(self):
        self.marks.append(self.top)

    def release(self):
        self.top = self.marks.pop()


def build_program(SEQ, layers):
    DEPTH = len(layers)
    N_AB = sum(1 for t in layers if t == "ab")
    N_C = sum(1 for t in layers if t == "c")
    NPB = SEQ // 128
    NB = NPB + 1
    NTOK = NB * 128
    nc = bass.Bass("TRN2", target_bir_lowering=False)
    P = Prog(nc)

    def din(name, shape):
        return nc.dram_tensor(name, list(shape), F32, kind="ExternalInput").ap()

    def dout(name, shape):
        return nc.dram_tensor(name, list(shape), F32, kind="ExternalOutput").ap()

    x_p = din("x_p", [SEQ, D])
    x_s = din("x_s", [128, D])
    norm_mix = din("norm_mix", [DEPTH, D])
    norm_ffn = din("norm_ffn", [DEPTH, D])
    norm_final = din("norm_final", [1, D])
    w_up = din("w_up", [DEPTH, 128, 8, 2 * DFF])
    w_down = din("w_down", [DEPTH, 128, NFC, D])
    ffn_cw = din("ffn_cw", [DEPTH, 128, NFC, 3])
    st_ffn = din("st_ffn", [DEPTH, 128, NFC, NSS, 2])
    ident_in = din("ident", [128, 128])
    y_p = dout("y_p", [SEQ, D])
    y_s = dout("y_s", [128, D])
    o_ffn_p = dout("o_ffn_p", [DEPTH, 128, NFC, 2])
    o_ffn_s = dout("o_ffn_s", [DEPTH, 128, NFC, NSS, 2])
    if N_C:
        w_in_c = din("w_in_c", [N_C, 128, 8, 3 * D])
        w_out_c = din("w_out_c", [N_C, 128, 8, D])
        sc_cw = din("sc_cw", [N_C, 128, 8, 3])
        st_sc = din("st_sc", [N_C, 128, 8, NSS, 2])
        o_sc_p = dout("o_sc_p", [N_C, 128, 8, 2])
        o_sc_s = dout("o_sc_s", [N_C, 128, 8, NSS, 2])

    if N_AB:
        w_in_ab = din("w_in_ab", [N_AB, 128, 8, IN_AB])
        w_out_ab = din("w_out_ab", [N_AB, 128, 8, D])
        pool_w_in = din("pool_w", [N_AB, 128, 4, 128])
        pool_scale_in = din("pool_scale", [N_AB, 128, 4])
        dn_cw_in = din("dn_cw", [N_AB, 128, 12, 4])
        dn_nw_in = din("dn_nw", [N_AB, 128, 1])
        a_log_in = din("a_log", [N_AB, 4])
        dt_bias_in = din("dt_bias", [N_AB, 4])
        st_pool = din("st_pool", [N_AB, 128, 4, NSS, 15])
        st_dnc = din("st_dnc", [N_AB, 128, 12, NSS, 3])
        st_dn = din("st_dn", [N_AB, NSS, 128, 4, 128])
        masks_in = din("masks", [2, 4, 128, 128])
        onehot_in = din("onehot", [128, NSS])
        invcnt_in = din("invcnt", [128, 4, 16])
        lvlmask_in = din("lvlmask", [5, 128, 128])
        o_pool_p = dout("o_pool_p", [N_AB, 128, 4, 15])
        o_pool_s = dout("o_pool_s", [N_AB, 128, 4, NSS, 15])
        o_dnc_p = dout("o_dnc_p", [N_AB, 128, 12, 3])
        o_dnc_s = dout("o_dnc_s", [N_AB, 128, 12, NSS, 3])
        o_dn_p = dout("o_dn_p", [N_AB, 128, 4, 128])
        o_dn_s = dout("o_dn_s", [N_AB, NSS, 128, 4, 128])

    with contextlib.ExitStack() as st:
        def sb(name, shape, dt):
            return st.enter_context(nc.sbuf_tensor(name, list(shape), dt))

        xs = sb("xs", [128, NB, D], F32)
        ident_f = sb("ident_f", [128, 128], F32)
        ident_b = sb("ident_b", [128, 128], BF16)
        eps_col = sb("eps_col", [128, 1], F32)
        hn = [sb("hn%d" % i, [128, D], BF16) for i in range(2)]
        ss = sb("ss", [128, NB], F32)
        rstd = sb("rstd", [128, NB], F32)
        ARB = (int(nc.sbuf_bytes_remaining) - 256) // 64 * 64
        ar_t = sb("arena", [128, ARB // 2], BF16)
        AR = Arena(ar_t, ARB)
        ps = [st.enter_context(nc.psum_tensor("ps%d" % i, [128, 512], F32)) for i in range(8)]

        P.dma("sp", ident_f[:], ident_in)
        P.copy("dve", ident_b[:], ident_f[:])
        P.memset("dve", eps_col[:], EPS)
        for b in range(NPB):
            P.dma("sp" if b % 2 == 0 else "act", xs[:, b, :], x_p[b * 128:(b + 1) * 128, :])
        P.dma("sp", xs[:, NPB, :], x_s)

        prompt_tiles = lambda tb: [Tile_(i * tb, tb, 1, tb * 128, False, i == 0, i == NPB // tb - 1)
                                   for i in range(NPB // tb)]
        sample_tile = Tile_(NPB, 1, NSS, LS, True, True, True)

        def load_norm_w(src_row):
            wnb_ = AR.alloc([D], F32)
            P.dma("sp", wnb_, src_row.broadcast_to([128, D]))
            return wnb_

        hn_ctr = [0]
        tp_ctr = [0]

        def norm_blocks(blocks, wn, hT, col0, tp_banks):
            for i, b in enumerate(blocks):
                h = hn[hn_ctr[0] % 2]
                hn_ctr[0] += 1
                P.act(h[:], xs[:, b, :], AF.Square, accum_out=ss[:, b:b + 1])
                P.act(rstd[:, b:b + 1], ss[:, b:b + 1], AF.Ln, bias=eps_col[:], scale=1.0 / D)
                P.act(rstd[:, b:b + 1], rstd[:, b:b + 1], AF.Exp, scale=-0.5)
                P.stt(h[:], xs[:, b, :], rstd[:, b:b + 1], wn[:], ALU.mult, ALU.mult)
                bank = tp_banks[tp_ctr[0] % len(tp_banks)]
                tp_ctr[0] += 1
                pv = bank[:].bitcast(BF16).rearrange("p (k t) -> p k t", k=8)
                for k in range(8):
                    P.transpose(pv[:, k, :], h[:, k * 128:(k + 1) * 128], ident_b[:])
                c0 = col0 + i * 128
                P.copy("act" if i % 2 == 0 else "dve", hT[:, :, c0:c0 + 128], pv)

        def conv_tap0(ext, cw, L, acc):
            P.act(acc, ext[:, :, 0:L], AF.Identity, scale=cw[:, 0:1])

        def conv_rest(out, ext, cw, ntaps, L, acc):
            for j in range(1, ntaps):
                dst = out if j == ntaps - 1 else acc
                P.stt(dst, ext[:, :, j:j + L], cw[:, j:j + 1], acc, ALU.mult, ALU.add)

        def conv_taps(out, ext, cw, ntaps, L, acc):
            conv_tap0(ext, cw, L, acc)
            conv_rest(out, ext, cw, ntaps, L, acc)

        def skewed(n, A_, B_, skew=1):
            for i in range(min(skew, n)):
                A_(i)
            for i in range(n):
                if i + skew < n:
                    A_(i + skew)
                B_(i)

        def out_proj_accum(tile, lhs_fn, nk, rhs_fn, banks):
            i = 0
            for blk in range(tile.nblk):
                for half in range(2):
                    bank = banks[i % len(banks)]
                    i += 1
                    for k in range(nk):
                        P.mm(bank[:, :], lhs_fn(k, blk), rhs_fn(k, half), start=(k == 0), stop=(k == nk - 1))
                    xv = xs[:, tile.b0 + blk, half * 512:(half + 1) * 512]
                    P.tt("dve", xv, xv, bank[:, :], ALU.add)

        def mixer_c(l, ic):
            AR.mark()
            w_in = AR.alloc([8, 3 * D], BF16)
            w_out = AR.alloc([8, D], BF16)
            hT = AR.alloc([8, 512], BF16)
            m = AR.alloc([8, 512], BF16)
            cw = AR.alloc([8, 3], F32)
            ext_p = AR.alloc([8, 1, 2 + 512], F32)
            ext_s = AR.alloc([8, NSS, 2 + LS], F32)
            stage = AR.alloc([8, NSS, 2], F32)
            tmpc = [AR.alloc([512], F32) for _ in range(2)]
            acc = [AR.alloc([512], F32) for _ in range(2)]
            cv = [AR.alloc([512], F32) for _ in range(2)]
            for j in range(3):
                for k0 in range(0, 8, 4):
                    P.dma("pool", w_in[:, k0:k0 + 4, j * D:(j + 1) * D], w_in_c[ic, :, k0:k0 + 4, j * D:(j + 1) * D])
            for k0 in range(0, 8, 4):
                P.dma("pool", w_out[:, k0:k0 + 4, :], w_out_c[ic, :, k0:k0 + 4, :])
            P.dma("sp", cw, sc_cw[ic])
            P.dma("sp", stage, st_sc[ic])
            P.memset("pool", ext_p[:, :, 0, 0:2], 0.0)
            for c in range(8):
                P.copy("pool", ext_s[:, c, :, 0:2], stage[:, c])
            wn = load_norm_w(norm_mix[l:l + 1, :])
            tasks = [(tile, c) for tile in prompt_tiles(4) + [sample_tile] for c in range(8)]

            def CA(i):
                tile, c = tasks[i]
                T, L, nseq = tile.T, tile.L, tile.nseq
                ext = ext_s if tile.sample else ext_p
                if c == 0:
                    norm_blocks(range(tile.b0, tile.b0 + tile.nblk), wn, hT, 0, [ps[6], ps[7]])
                pb, pc, ph = ps[(i % 2) * 3], ps[(i % 2) * 3 + 1], ps[(i % 2) * 3 + 2]
                for j, bank in enumerate((pb, pc, ph)):
                    for k in range(8):
                        P.mm(bank[:, :T], w_in[:, k, j * D + c * 128: j * D + (c + 1) * 128], hT[:, k, :T],
                             start=(k == 0), stop=(k == 7))
                v3 = lambda a: a.rearrange("p (s l) -> p s l", s=nseq)
                t = tmpc[i % 2]
                P.copy("act", t[:, :T], pc[:, :T])
                P.tt("dve", ext[:, c, :, 2:2 + L], v3(t[:, :T]), v3(ph[:, :T]), ALU.mult)
                conv_tap0(ext[:, c], cw[:, c, :], L, v3(acc[i % 2][:, :T]))

            def CB(i):
                tile, c = tasks[i]
                T, L, nseq = tile.T, tile.L, tile.nseq
                ext = ext_s if tile.sample else ext_p
                pb = ps[(i % 2) * 3]
                v3 = lambda a: a.rearrange("p (s l) -> p s l", s=nseq)
                conv_rest(v3(cv[i % 2][:, :T]), ext[:, c], cw[:, c, :], 3, L, v3(acc[i % 2][:, :T]))
                P.copy("pool", ext[:, c, :, 0:2], ext[:, c, :, L:L + 2])
                P.tt("dve", m[:, c, :T], cv[i % 2][:, :T], pb[:, :T], ALU.mult)
                if c == 7:
                    out_proj_accum(tile, lambda k, blk: m[:, k, blk * 128:(blk + 1) * 128],
                                   8, lambda k, half: w_out[:, k, half * 512:(half + 1) * 512], [ps[6], ps[7]])
                    if tile.last:
                        if tile.sample:
                            for c2 in range(8):
                                P.copy("pool", stage[:, c2], ext_s[:, c2, :, 0:2])
                            P.dma("sp", o_sc_s[ic], stage)
                        else:
                            stp = AR.alloc([8, 2], F32)
                            P.copy("pool", stp, ext_p[:, :, 0, 0:2])
                            P.dma("sp", o_sc_p[ic], stp)
            skewed(len(tasks), CA, CB)
            AR.release()

        def mixer_ab(l, ia):
            AR.mark()
            A = AR.alloc
            w_in = A([8, IN_AB], BF16)
            wo_ring = [A([D], BF16) for _ in range(4)]
            poolw = A([4, 128], BF16)
            hT = A([8, 128], BF16)
            wnb = A([D], F32)
            pscale = A([4], F32)
            dcw = A([12, 4], F32)
            nrmw = A([1], F32)
            alog = A([4], F32)
            dtb = A([4], F32)
            negA = A([4], F32)
            lnq = A([1], F32)
            WPS = NSS * (15 + LS)
            off_pooltmp = AR.top
            ebp = [A([WPS], F32) for _ in range(2)]
            T1 = A([WPS], F32)
            T2 = A([WPS], F32)
            pfx_pool = A([4, NSS, 15], F32)
            yp = [A([128], BF16) for _ in range(2)]
            t16 = A([16], F32)
            WQS = NSS * (3 + LS)
            ebq = [A([WQS], F32) for _ in range(2)]
            pfx_q = A([12, NSS, 3], F32)
            cvb = [A([128], F32) for _ in range(2)]
            accb = [A([128], F32) for _ in range(2)]
            qk32 = A([8, 128], F32)
            sq = A([8, 128], F32, at=off_pooltmp)
            rs = A([8, 128], F32)
            HS = [(A([8, 128], BF16), A([4, 128], BF16), A([4, 128], BF16), A([4, 128], BF16), A([4], F32), A([4], F32))
                  for _ in range(2)]
            ydn = A([4, 128], BF16)
            tsm = A([8], F32)
            gcc = A([8], F32)
            gam = A([4], F32)
            dlt = A([4], F32)
            bgm = A([4], F32)
            X1 = A([4, 128], F32)
            X2 = A([4, 128], F32)
            Eb = A([4, 128], F32)
            grow = A([4, 128], F32)
            P32 = A([4, 128], F32)
            Nb = A([4, 128], BF16)
            NTb = A([4, 128], BF16)
            Pb = A([4, 128], BF16)
            A2b = [A([4, 128], BF16) for _ in range(2)]
            A2Tb = [A([4, 128], BF16) for _ in range(2)]
            NTbd = A([4, 128], BF16)
            lvl_masks = [A([128], BF16) for _ in range(5)]
            aqkT = A([4, 128], BF16)
            qgT = A([4, 128], BF16)
            bgk = A([4, 128], BF16)
            kd = A([4, 128], BF16)
            bv = A([4, 128], BF16)
            negwT = A([4, 128], BF16)
            utT = A([4, 128], BF16)
            utk = A([4, 128], BF16)
            kdm = [A([4, 128], BF16) for _ in range(2)]
            S32p = A([4, 128], F32)
            Sbp = A([4, 128], BF16)
            S32s = [A([4, 128], F32) for _ in range(2)]
            Sbs = [A([4, 128], BF16) for _ in range(2)]
            msk = [[A([128], F32) for _ in range(4)] for _ in range(2)]
            onehot = A([NSS], F32)
            invcnt = A([4, 16], F32)
            for s_ in range(2):
                for m_ in range(4):
                    P.dma("sp", msk[s_][m_], masks_in[s_, m_])
            P.dma("sp", onehot, onehot_in)
            for m_ in range(5):
                P.dma("pool", lvl_masks[m_], lvlmask_in[m_])
            P.dma("sp", invcnt, invcnt_in)
            ONES = msk[0][1]
            for (c0, c1) in ((0, 1024), (1024, 2048), (2048, IN_AB)):
                for k0 in (0, 4):
                    P.dma("pool", w_in[:, k0:k0 + 4, c0:c1], w_in_ab[ia, :, k0:k0 + 4, c0:c1])
            P.dma("pool", poolw, pool_w_in[ia])
            P.dma("sp", wnb, norm_mix[l:l + 1, :].broadcast_to([128, D]))
            P.dma("sp", pscale, pool_scale_in[ia])
            P.dma("sp", dcw, dn_cw_in[ia])
            P.dma("sp", nrmw, dn_nw_in[ia])
            P.dma("sp", alog, a_log_in[ia:ia + 1, :].broadcast_to([128, 4]))
            P.dma("sp", dtb, dt_bias_in[ia:ia + 1, :].broadcast_to([128, 4]))
            P.act(negA, alog, AF.Exp)
            P.ts("dve", negA, negA, -1.0, ALU.mult)
            P.memset("dve", lnq, float(np.log(128.0 ** -0.5)))
            P.memset("pool", pfx_pool[:, :, 0, :], 0.0)
            P.memset("pool", pfx_q[:, :, 0, :], 0.0)
            P.memset("pool", S32p, 0.0)
            P.memset("pool", Sbp, 0.0)
            tiles = prompt_tiles(1) + [sample_tile]
            bc4 = lambda ap_: ap_.unsqueeze(1).broadcast_to([128, 4, 128])
            col4 = lambda ap_: ap_.unsqueeze(2).broadcast_to([128, 4, 128])
            r4 = lambda bank: bank[:, :].rearrange("p (a b) -> p a b", a=4)
            b4 = lambda bank, half: bank[:].bitcast(BF16)[:, half * 512:(half + 1) * 512].rearrange(
                "p (a b) -> p a b", a=4)
            T = 128

            def front_steps(tile, H):
                steps = []
                step = steps.append
                L, nseq = tile.L, tile.nseq
                v3 = lambda ap_: ap_.rearrange("p (s l) -> p s l", s=nseq)
                qkT, vT, sz, ypool, beta, gg = H

                def inproj(bank, col0):
                    for k in range(8):
                        P.mm(bank[:, :T], w_in[:, k, col0:col0 + 128], hT[:, k, :], start=(k == 0), stop=(k == 7))

                def s_norm():
                    if tile.sample:
                        P.dma("sp", o_pool_p[ia], pfx_pool[:, :, 0, :])
                        P.dma("sp", o_dnc_p[ia], pfx_q[:, :, 0, :])
                        P.dma("sp", pfx_pool, st_pool[ia])
                        P.dma("sp", pfx_q, st_dnc[ia])
                    norm_blocks([tile.b0], wnb, hT, 0, [ps[7]])
                step(s_norm)
                Wp = 15 + L

                def s_pool(g):
                    bank = ps[5 + g % 2]
                    inproj(bank, g * 128)
                    eb = ebp[g % 2][:, 0:nseq * Wp].rearrange("p (s w) -> p s w", s=nseq)
                    t1 = T1[:, 0:nseq * Wp].rearrange("p (s w) -> p s w", s=nseq)
                    t2 = T2[:, 0:nseq * Wp].rearrange("p (s w) -> p s w", s=nseq)
                    P.copy("pool", eb[:, :, 0:15], pfx_pool[:, g, 0:nseq, :])
                    P.copy("act", eb[:, :, 15:Wp], v3(bank[:, :T]))
                    P.tt("dve", t1[:, :, 1:Wp], eb[:, :, 1:Wp], eb[:, :, 0:Wp - 1], ALU.add)
                    res = t1
                    if g >= 1:
                        P.tt("dve", t2[:, :, 3:Wp], t1[:, :, 3:Wp], t1[:, :, 1:Wp - 2], ALU.add)
                        res = t2
                    if g >= 2:
                        P.tt("dve", t1[:, :, 7:Wp], t2[:, :, 7:Wp], t2[:, :, 3:Wp - 4], ALU.add)
                        res = t1
                    if g >= 3:
                        P.tt("dve", t2[:, :, 15:Wp], t1[:, :, 15:Wp], t1[:, :, 7:Wp - 8], ALU.add)
                        res = t2
                    y = yp[g % 2]
                    P.stt(v3(y), res[:, :, 15:Wp], 1.0 / POOL_WINDOWS[g], eb[:, :, 15:Wp], ALU.mult, ALU.subtract)
                    if tile.first and not tile.sample:
                        P.tt("dve", t16, res[:, 0, 15:31], invcnt[:, g, :], ALU.mult)
                        P.tt("dve", y[:, 0:16], t16, eb[:, 0, 15:31], ALU.subtract)
                    P.copy("pool", pfx_pool[:, g, 0:nseq, :], eb[:, :, L:L + 15])
                    P.mm(ps[7][:, :T], poolw[:, g, :], y)
                    P.act(ypool[:, g, :], ps[7][:, :T], AF.Identity, scale=pscale[:, g:g + 1])
                for g in range(4):
                    step(lambda g=g: s_pool(g))
                Wq = 3 + L

                def s_qkv_a(c):
                    bank = ps[5 + c % 2]
                    inproj(bank, 512 + c * 128)
                    eb = ebq[c % 2][:, 0:nseq * Wq].rearrange("p (s w) -> p s w", s=nseq)
                    P.copy("pool", eb[:, :, 0:3], pfx_q[:, c, 0:nseq, :])
                    P.copy("act", eb[:, :, 3:Wq], v3(bank[:, :T]))
                    conv_tap0(eb, dcw[:, c, :], L, v3(accb[c % 2]))

                def s_qkv_b(c):
                    eb = ebq[c % 2][:, 0:nseq * Wq].rearrange("p (s w) -> p s w", s=nseq)
                    cv = cvb[c % 2]
                    conv_rest(v3(cv), eb, dcw[:, c, :], 4, L, v3(accb[c % 2]))
                    P.copy("pool", pfx_q[:, c, 0:nseq, :], eb[:, :, L:L + 3])
                    if c < 8:
                        P.act(qk32[:, c, :], cv, AF.Silu)
                    else:
                        P.act(vT[:, c - 8, :], cv, AF.Silu)
                step(lambda: s_qkv_a(0))
                for c in range(12):
                    if c + 1 < 12:
                        step(lambda c=c: (s_qkv_a(c + 1), s_qkv_b(c)))
                    else:
                        step(lambda c=c: s_qkv_b(c))

                def s_z(c):
                    bank = ps[5 + c % 2]
                    inproj(bank, 2048 + c * 128)
                    P.act(sz[:, c, :], bank[:, :T], AF.Silu)
                for c in range(4):
                    step(lambda c=c: s_z(c))

                def s_ba():
                    bank = ps[7]
                    for k in range(8):
                        P.mm(bank[:, 0:8], hT[:, k, :], w_in[:, k, 2560:2568], start=(k == 0), stop=(k == 7))
                    P.copy("dve", tsm, bank[:, 0:8])
                    P.act(beta, tsm[:, 0:4], AF.Exp, scale=-1.0)
                    P.ts("dve", beta, beta, 1.0, ALU.add)
                    P.rec("dve", lambda e, o=beta: e.reciprocal(o, o), reads=[beta], writes=[beta])
                    P.tt("dve", gg, tsm[:, 4:8], dtb, ALU.add)
                    P.act(gg, gg, AF.Exp)
                    P.act(gg, gg, AF.Ln, bias=1.0, scale=1.0)
                    P.tt("dve", gg, gg, negA, ALU.mult)
                step(s_ba)

                def s_l2():
                    P.act(sq, qk32, AF.Square)
                    sqf = sq.rearrange("p a b -> p (a b)")
                    rsf = rs.rearrange("p a b -> p (a b)")
                    for hh in range(2):
                        P.mm(ps[5 + hh][:, :], ONES, sqf[:, hh * 512:(hh + 1) * 512])
                        P.act(rsf[:, hh * 512:(hh + 1) * 512], ps[5 + hh][:, :], AF.Ln, bias=eps_col[:], scale=1.0)
                    P.act(rsf[:, 0:512], rsf[:, 0:512], AF.Exp, scale=-0.5, bias=lnq)
                    P.act(rsf[:, 512:1024], rsf[:, 512:1024], AF.Exp, scale=-0.5)
                    P.tt("dve", qkT, qk32, rs, ALU.mult)
                step(s_l2)
                return steps

            def back_steps(tile, H):
                steps = []
                step = steps.append
                qkT, vT, sz, ypool, beta, gg = H
                ms = msk[1 if tile.sample else 0]
                UCUM, SSM, NEG, STRICT = ms
                gcrow = r4(ps[0])
                brow = r4(ps[2])
                KK = r4(ps[3])
                QK = r4(ps[4])

                def s_prep1():
                    for k in range(4):
                        P.dma("pool", wo_ring[k], w_out_ab[ia, :, k, :])
                    P.tt("dve", X1, bc4(UCUM), col4(gg), ALU.mult)
                    P.mm(ps[0][:, :], ONES, X1.rearrange("p a b -> p (a b)"))
                    P.mm(ps[1][:, 0:4], UCUM, gg)
                    P.mm(ps[1][:, 4:8], SSM, gg)
                    P.tt("dve", X2, bc4(ident_f[:]), col4(beta), ALU.mult)
                    P.mm(ps[2][:, :], ONES, X2.rearrange("p a b -> p (a b)"))
                    P.copy("dve", gcc, ps[1][:, 0:8])
                    P.act(gam, gcc[:, 0:4], AF.Exp)
                    P.tt("dve", dlt, gcc[:, 4:8], gcc[:, 0:4], ALU.subtract)
                    P.act(dlt, dlt, AF.Exp)
                    P.tt("dve", bgm, beta, gam, ALU.mult)
                step(s_prep1)

                def s_prep2():
                    for h in range(4):
                        P.ts("dve", X1[:, h, :], gcrow[:, h, :], gcc[:, h:h + 1], ALU.subtract, 0.0, ALU.min)
                    P.tt("dve", X1, X1, bc4(NEG), ALU.add)
                    P.act(Eb, X1, AF.Exp)
                    P.act(grow, gcrow, AF.Exp)
                    P.tt("dve", qgT, qkT[:, 0:4, :], grow, ALU.mult)
                    for h in range(4):
                        P.mm(KK[:, h, :], qkT[:, 4 + h, :], qkT[:, 4 + h, :])
                    for h in range(4):
                        P.mm(QK[:, h, :], qkT[:, 4 + h, :], qkT[:, h, :])
                step(s_prep2)

                def s_prep3():
                    P.tt("dve", aqkT, QK, Eb, ALU.mult)
                    P.tt("dve", X2, Eb, bc4(STRICT), ALU.mult)
                    P.tt("dve", X2, X2, brow, ALU.mult)
                    P.tt("dve", Nb, KK, X2, ALU.mult)
                    NTp = b4(ps[1], 0)
                    for h in range(4):
                        P.transpose(NTp[:, h, :], Nb[:, h, :], ident_b[:])
                    P.copy("act", NTb, NTp)
                    if tile.sample:
                        P.stt(P32, Nb, -1.0, bc4(ident_f[:]), ALU.mult, ALU.add)
                    else:
                        P.tt("dve", A2b[1], Nb, bc4(lvl_masks[0]), ALU.mult)
                        P.tt("dve", NTbd, NTb, bc4(lvl_masks[0]), ALU.mult)
                        P.stt(P32, A2b[1], -1.0, bc4(ident_f[:]), ALU.mult, ALU.add)
                    P.copy("act", Pb, P32)
                step(s_prep3)
                A2T_ps = r4(ps[3])
                A2_ps = r4(ps[4])
                PP_ps = r4(ps[1])

                def s_base(m_):
                    if tile.sample:
                        A0, AT0 = Nb, NTb
                    else:
                        A0, AT0 = A2b[1], NTbd
                    Acur, ATcur = (A0, AT0) if m_ == 0 else (A2b[0], A2Tb[0])
                    lastl = (m_ == 1)
                    a2t = A2Tb[m_]
                    for h in range(4):
                        P.mm(A2T_ps[:, h, :], Acur[:, h, :], ATcur[:, h, :])
                    if not lastl:
                        for h in range(4):
                            P.mm(A2_ps[:, h, :], ATcur[:, h, :], Acur[:, h, :])
                    P.copy("act", a2t, A2T_ps)
                    if not lastl:
                        P.copy("dve", A2b[0], A2_ps)
                    for h in range(4):
                        P.mm(PP_ps[:, h, :], a2t[:, h, :], Pb[:, h, :])
                    P.tt("dve", P32, P32, PP_ps, ALU.add)
                    P.copy("act", Pb, P32)
                step(lambda: s_base(0))
                step(lambda: s_base(1))

                def s_dbl(lv):
                    Zb, tmpb, UTb = A2b[0], A2Tb[0], A2Tb[1]
                    UT_ps = b4(ps[1], 0)
                    for h in range(4):
                        P.transpose(UT_ps[:, h, :], Pb[:, h, :], ident_b[:])
                    for h in range(4):
                        P.mm(A2T_ps[:, h, :], NTb[:, h, :], Pb[:, h, :])
                    P.copy("act", UTb, UT_ps)
                    P.copy("dve", Zb, A2T_ps)
                    for h in range(4):
                        P.mm(A2_ps[:, h, :], UTb[:, h, :], Zb[:, h, :])
                    P.tt("dve", tmpb, A2_ps, bc4(lvl_masks[1 + lv]), ALU.mult)
                    P.tt("dve", Pb, Pb, tmpb, ALU.add)
                if not tile.sample:
                    for lv in range(4):
                        step(lambda lv=lv: s_dbl(lv))
                wT_ps = r4(ps[2])

                def s_ktok():
                    ktp = b4(ps[2], 0)
                    vtp = b4(ps[2], 1)
                    for h in range(4):
                        P.transpose(ktp[:, h, :], qkT[:, 4 + h, :], ident_b[:])
                    for h in range(4):
                        P.transpose(vtp[:, h, :], vT[:, h, :], ident_b[:])
                    P.tt("dve", bgk, ktp, col4(bgm), ALU.mult)
                    P.tt("dve", kd, ktp, col4(dlt), ALU.mult)
                    P.tt("dve", bv, vtp, col4(beta), ALU.mult)
                step(s_ktok)

                def s_w():
                    for h in range(4):
                        P.mm(wT_ps[:, h, :], bgk[:, h, :], Pb[:, h, :])
                    P.rec("act", lambda e: e.mul(negwT, wT_ps, -1.0), reads=[wT_ps], writes=[negwT])
                step(s_w)
                if tile.sample:
                    seqs = [(s_ * LS, (s_ + 1) * LS, s_) for s_ in range(NSS)]
                else:
                    seqs = [(0, 128, None)]
                uT_ps = r4(ps[0])
                oT_ps = r4(ps[3])
                Sn_ps = r4(ps[4])

                def s_u():
                    for h in range(4):
                        P.mm(uT_ps[:, h, :], bv[:, h, :], Pb[:, h, :], start=(h == 0), stop=False, skip=True)
                    for si, (c0, c1, sidx) in enumerate(seqs):
                        if sidx is None:
                            Sb = Sbp
                        else:
                            Sb = Sbs[si % 2]
                            P.dma("pool", Sb, st_dn[ia, sidx])
                        for h in range(4):
                            P.mm(uT_ps[:, h, c0:c1], Sb[:, h, :], negwT[:, h, c0:c1], start=False, stop=True, skip=True)
                    P.copy("act", utT, uT_ps)
                    utk_ps = b4(ps[1], 0)
                    for h in range(4):
                        P.transpose(utk_ps[:, h, :], utT[:, h, :], ident_b[:])
                    P.copy("dve", utk, utk_ps)
                step(s_u)

                def s_o():
                    for h in range(4):
                        P.mm(oT_ps[:, h, :], utk[:, h, :], aqkT[:, h, :], start=(h == 0), stop=False, skip=True)
                    for si, (c0, c1, sidx) in enumerate(seqs):
                        if sidx is None:
                            Sb, S32, kdu = Sbp, S32p, kd
                        else:
                            Sb, S32, kdu = Sbs[si % 2], S32s[si % 2], kdm[si % 2]
                            P.dma("pool", Sb, st_dn[ia, sidx])
                            P.dma("sp", S32, st_dn[ia, sidx])
                            P.ts("dve", kdu, kd, onehot[:, sidx:sidx + 1], ALU.mult)
                        for h in range(4):
                            P.mm(oT_ps[:, h, c0:c1], Sb[:, h, :], qgT[:, h, c0:c1], start=False, stop=True, skip=True)
                        for h in range(4):
                            P.mm(Sn_ps[:, h, :], kdu[:, h, :], utk[:, h, :])
                        for h in range(4):
                            P.stt(S32[:, h, :], S32[:, h, :], grow[:, h, c1 - 1:c1], Sn_ps[:, h, :], ALU.mult, ALU.add)
                        if sidx is None:
                            P.copy("act", Sbp, S32p)
                            if tile.last:
                                P.dma("sp", o_dn_p[ia], S32p)
                        else:
                            P.dma("sp", o_dn_s[ia, sidx], S32)
                step(s_o)

                def s_out():
                    P.act(X1, oT_ps, AF.Square)
                    P.mm(ps[1][:, :], ONES, X1.rearrange("p a b -> p (a b)"))
                    X2f = X2.rearrange("p a b -> p (a b)")
                    P.act(X2f, ps[1][:, :], AF.Ln, bias=eps_col[:], scale=1.0 / 128.0)
                    P.act(X2f, X2f, AF.Exp, scale=-0.5)
                    P.tt("dve", X1, oT_ps, X2, ALU.mult)
                    P.stt(ydn, X1, nrmw[:, 0:1], sz, ALU.mult, ALU.mult)
                step(s_out)

                def s_proj():
                    for k in range(8):
                        src = ypool if k < 4 else ydn
                        for half in range(2):
                            P.mm(ps[half * 4][:, :], src[:, k % 4, :], wo_ring[k % 4][:, half * 512:(half + 1) * 512],
                                 start=(k == 0), stop=(k == 7))
                        if k + 4 < 8:
                            P.dma("pool", wo_ring[k % 4], w_out_ab[ia, :, k + 4, :])
                    for half in range(2):
                        xv = xs[:, tile.b0, half * 512:(half + 1) * 512]
                        P.tt("dve", xv, xv, ps[half * 4][:, :], ALU.add)
                step(s_proj)
                return steps

            def interleave(a, b):
                out, ia_, ib_ = [], 0, 0
                na, nb = len(a), len(b)
                while ia_ < na or ib_ < nb:
                    if ib_ >= nb or (ia_ < na and ia_ * nb <= ib_ * na):
                        out.append(a[ia_]); ia_ += 1
                    else:
                        out.append(b[ib_]); ib_ += 1
                return out

            pend = []
            for ti, tile in enumerate(tiles):
                fs = front_steps(tile, HS[ti % 2])
                for st_ in interleave(fs, pend):
                    st_()
                pend = back_steps(tile, HS[ti % 2])
            for st_ in pend:
                st_()
            P.dma("sp", o_pool_s[ia], pfx_pool)
            P.dma("sp", o_dnc_s[ia], pfx_q)
            AR.release()

        GS = 4

        def ffn(l):
            AR.mark()
            slots = []
            for s in range(2):
                slots.append((AR.alloc([8, 2, GS * 128], BF16), AR.alloc([GS, D], BF16)))
            hTall = AR.alloc([8, NTOK], BF16)
            cw = AR.alloc([NFC, 3], F32)
            a_sb = [AR.alloc([GS, 512], BF16) for _ in range(2)]
            ext_p = AR.alloc([GS, 1, 2 + 512], F32)
            ext_s = AR.alloc([GS, NSS, 2 + LS], F32)
            stage_s = [AR.alloc([GS, NSS, 2], F32) for _ in range(2)]
            stage_o = [AR.alloc([GS, NSS, 2], F32) for _ in range(2)]
            stage_p = [AR.alloc([GS, 2], F32) for _ in range(2)]
            acc = [AR.alloc([512], F32) for _ in range(3)]
            cv = [AR.alloc([512], F32) for _ in range(3)]
            P.dma("sp", cw, ffn_cw[l])
            wn = load_norm_w(norm_ffn[l:l + 1, :])
            norm_blocks(range(NB), wn, hTall, 0, [ps[6], ps[7]])
            groups = [(g0, min(GS, NFC - g0)) for g0 in range(0, NFC, GS)]

            def load_group(gi):
                g0, gs = groups[gi]
                wu, wd = slots[gi % 2]
                for j in range(2):
                    for k0 in range(0, 8, 4):
                        P.dma("pool", wu[:, k0:k0 + 4, j, 0:gs * 128],
                              w_up[l, :, k0:k0 + 4, j * DFF + g0 * 128: j * DFF + (g0 + gs) * 128])
                P.dma("pool", wd[:, 0:gs, :], w_down[l, :, g0:g0 + gs, :])

            load_group(0)
            tiles = prompt_tiles(4) + [sample_tile]
            ai = 0
            for gi, (g0, gs) in enumerate(groups):
                if gi + 1 < len(groups):
                    load_group(gi + 1)
                wu, wd = slots[gi % 2]
                sts = stage_s[gi % 2]
                P.dma("sp", sts[:, 0:gs], st_ffn[l, :, g0:g0 + gs])
                P.memset("pool", ext_p[:, :, 0, 0:2], 0.0)
                for c in range(gs):
                    P.copy("pool", ext_s[:, c, :, 0:2], sts[:, c])
                tasks = []
                for tile in tiles:
                    a = a_sb[ai % 2]
                    ai += 1
                    for c in range(gs):
                        tasks.append((tile, c, a))

                def FA(i, tasks=tasks, wu=wu, g0=g0):
                    tile, c, a = tasks[i]
                    T, L, nseq = tile.T, tile.L, tile.nseq
                    ext = ext_s if tile.sample else ext_p
                    tok0 = tile.b0 * 128
                    v3 = lambda ap_: ap_.rearrange("p (s l) -> p s l", s=nseq)
                    pg, pv = ps[(i % 3) * 2], ps[(i % 3) * 2 + 1]
                    for j, bank in enumerate((pg, pv)):
                        for k in range(8):
                            P.mm(bank[:, :T], wu[:, k, j, c * 128:(c + 1) * 128], hTall[:, k, tok0:tok0 + T],
                                 start=(k == 0), stop=(k == 7))
                    P.copy("act", ext[:, c, :, 2:2 + L], v3(pg[:, :T]))
                    conv_tap0(ext[:, c], cw[:, g0 + c, :], L, v3(acc[i % 3][:, :T]))

                def FB(i, tasks=tasks, wd=wd, g0=g0, gs=gs, gi=gi):
                    tile, c, a = tasks[i]
                    T, L, nseq = tile.T, tile.L, tile.nseq
                    ext = ext_s if tile.sample else ext_p
                    v3 = lambda ap_: ap_.rearrange("p (s l) -> p s l", s=nseq)
                    pv = ps[(i % 3) * 2 + 1]
                    conv_rest(v3(cv[i % 3][:, :T]), ext[:, c], cw[:, g0 + c, :], 3, L, v3(acc[i % 3][:, :T]))
                    P.copy("pool", ext[:, c, :, 0:2], ext[:, c, :, L:L + 2])
                    P.act(cv[i % 3][:, :T], cv[i % 3][:, :T], AF.Silu)
                    P.tt("dve", a[:, c, :T], cv[i % 3][:, :T], pv[:, :T], ALU.mult)
                    if c == gs - 1:
                        out_proj_accum(tile, lambda k, blk: a[:, k, blk * 128:(blk + 1) * 128],
                                       gs, lambda k, half: wd[:, k, half * 512:(half + 1) * 512],
                                       [ps[6], ps[7]])
                        if tile.last:
                            if tile.sample:
                                so = stage_o[gi % 2]
                                for c2 in range(gs):
                                    P.copy("pool", so[:, c2], ext_s[:, c2, :, 0:2])
                                P.dma("sp", o_ffn_s[l, :, g0:g0 + gs], so[:, 0:gs])
                            else:
                                sp_ = stage_p[gi % 2]
                                P.copy("pool", sp_[:, 0:gs], ext_p[:, 0:gs, 0, 0:2])
                                P.dma("sp", o_ffn_p[l, :, g0:g0 + gs, :], sp_[:, 0:gs])
                skewed(len(tasks), FA, FB)
            AR.release()

        ic = 0
        ia = 0
        for l, typ in enumerate(layers):
            if typ == "c":
                mixer_c(l, ic)
                ic += 1
            else:
                mixer_ab(l, ia)
                ia += 1
            ffn(l)

        AR.mark()
        wn = load_norm_w(norm_final[0:1, :])
        yo = [AR.alloc([D], F32) for _ in range(2)]
        for b in range(NB):
            y = yo[b % 2]
            P.act(y, xs[:, b, :], AF.Square, accum_out=ss[:, b:b + 1])
            P.act(rstd[:, b:b + 1], ss[:, b:b + 1], AF.Ln, bias=eps_col[:], scale=1.0 / D)
            P.act(rstd[:, b:b + 1], rstd[:, b:b + 1], AF.Exp, scale=-0.5)
            P.stt(y, xs[:, b, :], rstd[:, b:b + 1], wn[:], ALU.mult, ALU.mult)
            if b < NPB:
                P.dma("sp" if b % 2 == 0 else "act", y_p[b * 128:(b + 1) * 128, :], y)
            else:
                P.dma("sp", y_s, y)
        AR.release()
        P.emit()
    P.arena_bytes = ARB
    return nc, P


def _pkn(w):
    sh = w.shape
    k = sh[-2] // 128
    w = w.reshape(sh[:-2] + (k, 128, sh[-1]))
    return np.ascontiguousarray(np.swapaxes(w, -3, -2))


def _chan(v, nchunk):
    sh = v.shape
    v = v.reshape(sh[:-1] + (nchunk, 128))
    return np.ascontiguousarray(np.moveaxis(np.moveaxis(v, -1, -3), -1, -2))


_CACHE = {}


def run_model(inputs, SEQ, layers):
    key = (SEQ, tuple(layers))
    if key not in _CACHE:
        _CACHE[key] = build_program(SEQ, layers)
    nc, P = _CACHE[key]
    DEPTH = len(layers)
    f = lambda a: np.ascontiguousarray(np.asarray(a, dtype=np.float32))
    shared = {
        "norm_mix": f(inputs["norm_mix"]), "norm_ffn": f(inputs["norm_ffn"]),
        "norm_final": f(inputs["norm_final"]).reshape(1, D),
        "w_up": _pkn(f(inputs["w_up"])), "w_down": _pkn(f(inputs["w_down"])),
        "ffn_cw": _chan(f(inputs["ffn_conv_w"]), NFC),
        "ident": np.eye(128, dtype=np.float32),
    }
    N_C = sum(1 for t in layers if t == "c")
    N_AB = sum(1 for t in layers if t == "ab")
    if N_AB:
        shared["w_in_ab"] = _pkn(f(inputs["w_in_ab"]))
        shared["w_out_ab"] = _pkn(f(inputs["w_out_ab"]))
        shared["pool_w"] = np.ascontiguousarray(f(inputs["pool_w"]).transpose(0, 2, 1, 3))
        shared["pool_scale"] = np.ascontiguousarray(f(inputs["pool_scale"]).reshape(N_AB, 4, 128).transpose(0, 2, 1))
        shared["dn_cw"] = _chan(f(inputs["dn_conv_w"]), 12)
        shared["dn_nw"] = f(inputs["dn_norm_w"]).reshape(N_AB, 128, 1)
        shared["a_log"] = f(inputs["dn_a_log"])
        shared["dt_bias"] = f(inputs["dn_dt_bias"])
        idx = np.arange(128)
        k_ = idx[:, None]
        j_ = idx[None, :]
        msk = np.zeros((2, 4, 128, 128), np.float32)
        for si, same in enumerate((np.ones((128, 128), bool), (k_ // LS) == (j_ // LS))):
            msk[si, 0] = (same & (k_ <= j_))
            msk[si, 1] = same
            msk[si, 2] = np.where(same & (j_ >= k_), 0.0, -BIG)
            msk[si, 3] = (same & (j_ > k_))
        shared["masks"] = msk
        lm = np.zeros((5, 128, 128), np.float32)
        lm[0] = ((k_ // 8) == (j_ // 8))
        for li, sz_ in enumerate((8, 16, 32, 64)):
            lm[1 + li] = -1.0 * (((k_ // (2 * sz_)) == (j_ // (2 * sz_))) & ((k_ % (2 * sz_)) < sz_) & ((j_ % (2 * sz_)) >= sz_))
        shared["lvlmask"] = lm
        shared["onehot"] = ((idx[:, None] // LS) == np.arange(NSS)[None, :]).astype(np.float32)
        t_ = np.arange(16)
        ic_ = np.stack([1.0 / np.minimum(t_ + 1, w) for w in POOL_WINDOWS]).astype(np.float32)
        shared["invcnt"] = np.ascontiguousarray(np.broadcast_to(ic_[None], (128, 4, 16)))
    if N_C:
        shared["w_in_c"] = _pkn(f(inputs["w_in_c"]))
        shared["w_out_c"] = _pkn(f(inputs["w_out_c"]))
        shared["sc_cw"] = _chan(f(inputs["sc_conv_w"]), 8)
    xp = f(inputs["x_prompt"])
    xsm = f(inputs["x_sample"])
    st_ffn = f(inputs["state_ffn_conv"])
    st_sc = f(inputs["state_sconv"]) if N_C else None
    in_maps = []
    for c in range(NCORES):
        m = dict(shared)
        m["x_p"] = xp[c]
        m["x_s"] = xsm[c * NSS:(c + 1) * NSS].reshape(NSS * LS, D)
        s = st_ffn[:, c * NSS:(c + 1) * NSS]
        s = s.reshape(DEPTH, NSS, 2, NFC, 128)
        m["st_ffn"] = np.ascontiguousarray(s.transpose(0, 4, 3, 1, 2))
        if N_C:
            s = st_sc[:, c * NSS:(c + 1) * NSS].reshape(N_C, NSS, 2, 8, 128)
            m["st_sc"] = np.ascontiguousarray(s.transpose(0, 4, 3, 1, 2))
        if N_AB:
            s = f(inputs["state_pool"])[:, c * NSS:(c + 1) * NSS].reshape(N_AB, NSS, 15, 4, 128)
            m["st_pool"] = np.ascontiguousarray(s.transpose(0, 4, 3, 1, 2))
            s = f(inputs["state_dn_conv"])[:, c * NSS:(c + 1) * NSS].reshape(N_AB, NSS, 3, 12, 128)
            m["st_dnc"] = np.ascontiguousarray(s.transpose(0, 4, 3, 1, 2))
            s = f(inputs["state_dn"])[:, c * NSS:(c + 1) * NSS]
            m["st_dn"] = np.ascontiguousarray(s.transpose(0, 1, 3, 2, 4))
        in_maps.append(m)
    res = run_bass_kernel_spmd(nc, in_maps, core_ids=list(range(NCORES)))
    R = res.results
    out = {}
    out["y_prompt"] = np.stack([R[c]["y_p"] for c in range(NCORES)])
    out["y_sample"] = np.concatenate([R[c]["y_s"].reshape(NSS, LS, D) for c in range(NCORES)])
    out["new_ffnconv_p"] = np.stack([R[c]["o_ffn_p"].transpose(0, 3, 2, 1).reshape(DEPTH, 2, DFF)
                                     for c in range(NCORES)], axis=1)
    out["new_ffnconv_s"] = np.concatenate(
        [R[c]["o_ffn_s"].transpose(0, 3, 4, 2, 1).reshape(DEPTH, NSS, 2, DFF) for c in range(NCORES)], axis=1)
    if N_AB:
        out["new_pool_p"] = np.stack([R[c]["o_pool_p"].transpose(0, 3, 2, 1).reshape(N_AB, 15, DPOOL)
                                      for c in range(NCORES)], axis=1)
        out["new_pool_s"] = np.concatenate(
            [R[c]["o_pool_s"].transpose(0, 3, 4, 2, 1).reshape(N_AB, NSS, 15, DPOOL) for c in range(NCORES)], axis=1)
        out["new_dnconv_p"] = np.stack([R[c]["o_dnc_p"].transpose(0, 3, 2, 1).reshape(N_AB, 3, DQKV)
                                        for c in range(NCORES)], axis=1)
        out["new_dnconv_s"] = np.concatenate(
            [R[c]["o_dnc_s"].transpose(0, 3, 4, 2, 1).reshape(N_AB, NSS, 3, DQKV) for c in range(NCORES)], axis=1)
        out["new_dn_p"] = np.stack([R[c]["o_dn_p"].transpose(0, 2, 1, 3) for c in range(NCORES)], axis=1)
        out["new_dn_s"] = np.concatenate([R[c]["o_dn_s"].transpose(0, 1, 3, 2, 4) for c in range(NCORES)], axis=1)
    if N_C:
        out["new_sconv_p"] = np.stack([R[c]["o_sc_p"].transpose(0, 3, 2, 1).reshape(N_C, 2, D)
                                       for c in range(NCORES)], axis=1)
        out["new_sconv_s"] = np.concatenate(
            [R[c]["o_sc_s"].transpose(0, 3, 4, 2, 1).reshape(N_C, NSS, 2, D) for c in range(NCORES)], axis=1)
    return out


def kernel(**inputs):
    out = run_model(inputs, 2048, ["ab", "c", "ab", "c"])
    names = ["y_prompt", "y_sample", "new_pool_p", "new_pool_s", "new_dnconv_p", "new_dnconv_s", "new_dn_p",
             "new_dn_s", "new_sconv_p", "new_sconv_s", "new_ffnconv_p", "new_ffnconv_s"]
    return tuple(np.ascontiguousarray(out[n], dtype=np.float32) for n in names)
```
